# Optimizing a Trainium2 kernel written in Bass

```python
import math
import jax, jax.numpy as jnp
from jax import lax
import numpy as np

D_MODEL = 1024
BATCH = 4
SEQ = 8192
DEPTH = 1

CHUNK = 64
Q_BLOCK = 128
EPS = 1e-6
DA_HEADS = 8
DA_HEAD_DIM = D_MODEL // DA_HEADS // 2
DA_V_DIM = 2 * DA_HEAD_DIM
ROPE_THETA = 500000.0
ROPE_DIM = DA_HEAD_DIM // 4
SG_WIDTH = D_MODEL
SG_GROUPS = 8
SG_GROUP_DIM = SG_WIDTH // SG_GROUPS
SG_WINDOW = 128
MEM_LEN = 256
XA_HEADS = 4
XA_HEAD_DIM = D_MODEL // XA_HEADS
D_FF = 4 * D_MODEL
N_BRANCH = 2
Q_COLS = DA_HEADS * 2 * DA_HEAD_DIM
K_COLS = DA_HEADS * 2 * DA_HEAD_DIM
V_COLS = DA_HEADS * DA_V_DIM
SG_COLS = 2 * SG_WIDTH
GATE_COLS = N_BRANCH * D_MODEL
D_IN = Q_COLS + K_COLS + V_COLS + SG_COLS + GATE_COLS
SPLITS = (Q_COLS, Q_COLS + K_COLS, Q_COLS + K_COLS + V_COLS, Q_COLS + K_COLS + V_COLS + SG_COLS)

kernel_name = "hybrid_diffattn_gmlp_gated_block"


def rms_norm(x, g):
    xf = x.astype(jnp.float32)
    y = xf * lax.rsqrt(jnp.mean(xf * xf, axis=-1, keepdims=True) + EPS)
    return (y * g.astype(jnp.float32)).astype(x.dtype)


def layer_norm(x, g, b):
    xf = x.astype(jnp.float32)
    mu = jnp.mean(xf, axis=-1, keepdims=True)
    var = jnp.mean(jnp.square(xf - mu), axis=-1, keepdims=True)
    y = (xf - mu) * lax.rsqrt(var + 1e-5)
    return (y * g.astype(jnp.float32) + b.astype(jnp.float32)).astype(x.dtype)


def rope_tables(positions, dtype):
    idx = jnp.arange(0, ROPE_DIM, 2, dtype=jnp.float32)
    inv_freq = jnp.power(jnp.float32(ROPE_THETA), -idx / ROPE_DIM)
    ang = positions.astype(jnp.float32)[..., None] * inv_freq
    return (jnp.cos(ang)[:, :, None, None, :].astype(dtype),
            jnp.sin(ang)[:, :, None, None, :].astype(dtype))


def rope_partial(x, cos, sin):
    half = ROPE_DIM // 2
    x1 = x[..., :half]
    x2 = x[..., half:ROPE_DIM]
    rest = x[..., ROPE_DIM:]
    return jnp.concatenate([x1 * cos - x2 * sin, x2 * cos + x1 * sin, rest], axis=-1)


def diff_attention(q, k, v, lam):
    B, S = q.shape[0], q.shape[1]
    nb = S // Q_BLOCK
    scale = DA_HEAD_DIM ** -0.5
    k_chunk = jnp.arange(S) // CHUNK
    qb = q.reshape(B, nb, Q_BLOCK, DA_HEADS, 2, DA_HEAD_DIM).transpose(1, 0, 2, 3, 4, 5)

    def one_block(args):
        q_blk, blk = args
        q_chunk = (blk * Q_BLOCK + jnp.arange(Q_BLOCK)) // CHUNK
        mask = k_chunk[None, :] <= q_chunk[:, None]
        s = jnp.einsum('bqhmd,bkhmd->bhmqk', q_blk, k).astype(jnp.float32) * scale
        p = jax.nn.softmax(jnp.where(mask, s, -jnp.inf), axis=-1)
        a = p[:, :, 0] - lam * p[:, :, 1]
        return jnp.einsum('bhqk,bkhe->bqhe', a.astype(v.dtype), v)

    o = lax.map(one_block, (qb, jnp.arange(nb)))
    return o.transpose(1, 0, 2, 3, 4).reshape(B, S, DA_HEADS, DA_V_DIM)


def spatial_gate(z, ln_g, ln_b, w_s, b_s):
    B, S = z.shape[0], z.shape[1]
    u, v = jnp.split(z, 2, axis=-1)
    v = layer_norm(v, ln_g, ln_b)
    nw = S // SG_WINDOW
    v = v.reshape(B, nw, SG_WINDOW, SG_GROUPS, SG_GROUP_DIM)
    pos_chunk = jnp.arange(SG_WINDOW) // CHUNK
    mask = (pos_chunk[None, :] <= pos_chunk[:, None]).astype(w_s.dtype)
    v = jnp.einsum('gij,bwjgc->bwigc', w_s * mask[None], v) + b_s.T[None, None, :, :, None]
    return u * v.reshape(B, S, SG_WIDTH)


def cross_attention(h, m, w_q, w_kv, w_o):
    B, S = h.shape[0], h.shape[1]
    L = m.shape[1]
    q = (h @ w_q).reshape(B, S, XA_HEADS, XA_HEAD_DIM)
    k, v = jnp.split(m @ w_kv, 2, axis=-1)
    k = k.reshape(B, L, XA_HEADS, XA_HEAD_DIM)
    v = v.reshape(B, L, XA_HEADS, XA_HEAD_DIM)
    s = jnp.einsum('bqhd,bkhd->bhqk', q, k).astype(jnp.float32) * (XA_HEAD_DIM ** -0.5)
    p = jax.nn.softmax(s, axis=-1)
    o = jnp.einsum('bhqk,bkhd->bqhd', p.astype(v.dtype), v).reshape(B, S, D_MODEL)
    return o @ w_o


def setup_inputs(seed: int = 0) -> dict:
    key = jax.random.key(seed)
    ks = jax.random.split(key, 32)
    f32 = jnp.float32

    def nrm(k, shape, scale):
        return jax.random.normal(k, shape, f32) * scale

    def gain(k, shape):
        return 1.0 + 0.02 * jax.random.normal(k, shape, f32)

    x = jax.random.normal(ks[0], (BATCH, SEQ, D_MODEL), f32)
    mem = jax.random.normal(ks[1], (BATCH, MEM_LEN, D_MODEL), f32)
    start = jax.random.randint(ks[2], (BATCH, 1), 0, 64, dtype=jnp.int32) * CHUNK
    positions = (start + jnp.arange(SEQ, dtype=jnp.int32)[None, :]).astype(jnp.int32)
    L = DEPTH
    return {
        "x": x,
        "mem": mem,
        "positions": positions,
        "g_mix": gain(ks[3], (L, D_MODEL)),
        "w_in": nrm(ks[4], (L, D_MODEL, D_IN), D_MODEL ** -0.5),
        "lam_q1": nrm(ks[5], (L, DA_HEAD_DIM), 0.1),
        "lam_k1": nrm(ks[6], (L, DA_HEAD_DIM), 0.1),
        "lam_q2": nrm(ks[7], (L, DA_HEAD_DIM), 0.1),
        "lam_k2": nrm(ks[8], (L, DA_HEAD_DIM), 0.1),
        "g_subln": gain(ks[9], (L, DA_V_DIM)),
        "sg_ln_g": gain(ks[10], (L, SG_WIDTH)),
        "sg_ln_b": nrm(ks[11], (L, SG_WIDTH), 0.02),
        "sg_w": nrm(ks[12], (L, SG_GROUPS, SG_WINDOW, SG_WINDOW), SG_WINDOW ** -0.5),
        "sg_b": 1.0 + nrm(ks[13], (L, SG_GROUPS, SG_WINDOW), 0.01),
        "w_branch_attn": nrm(ks[14], (L, V_COLS, D_MODEL), V_COLS ** -0.5),
        "w_branch_sg": nrm(ks[15], (L, SG_WIDTH, D_MODEL), SG_WIDTH ** -0.5),
        "w_out": nrm(ks[16], (L, D_MODEL, D_MODEL), D_MODEL ** -0.5),
        "g_xa": gain(ks[17], (L, D_MODEL)),
        "g_mem": gain(ks[18], (L, D_MODEL)),
        "w_xq": nrm(ks[19], (L, D_MODEL, D_MODEL), D_MODEL ** -0.5),
        "w_xkv": nrm(ks[20], (L, D_MODEL, 2 * D_MODEL), D_MODEL ** -0.5),
        "w_xo": nrm(ks[21], (L, D_MODEL, D_MODEL), D_MODEL ** -0.5),
        "g_ffn": gain(ks[22], (L, D_MODEL)),
        "w_ff1": nrm(ks[23], (L, D_MODEL, D_FF), D_MODEL ** -0.5),
        "w_ff2": nrm(ks[24], (L, D_FF, D_MODEL), D_FF ** -0.5),
        "g_final": gain(ks[25], (D_MODEL,)),
    }


def reference(x, mem, positions, g_mix, w_in, lam_q1, lam_k1, lam_q2, lam_k2, g_subln,
              sg_ln_g, sg_ln_b, sg_w, sg_b, w_branch_attn, w_branch_sg, w_out,
              g_xa, g_mem, w_xq, w_xkv, w_xo, g_ffn, w_ff1, w_ff2, g_final):
    B, S = x.shape[0], x.shape[1]
    cos, sin = rope_tables(positions, x.dtype)
    h = x
    for l in range(DEPTH):
        n = rms_norm(h, g_mix[l])
        proj = n @ w_in[l]
        q, k, v, z, g = jnp.split(proj, SPLITS, axis=-1)
        q = rope_partial(q.reshape(B, S, DA_HEADS, 2, DA_HEAD_DIM), cos, sin)
        k = rope_partial(k.reshape(B, S, DA_HEADS, 2, DA_HEAD_DIM), cos, sin)
        v = v.reshape(B, S, DA_HEADS, DA_V_DIM)
        lam_init = 0.8 - 0.6 * math.exp(-0.3 * l)
        lam = (jnp.exp(jnp.sum(lam_q1[l].astype(jnp.float32) * lam_k1[l].astype(jnp.float32)))
               - jnp.exp(jnp.sum(lam_q2[l].astype(jnp.float32) * lam_k2[l].astype(jnp.float32)))
               + lam_init)
        a = diff_attention(q, k, v, lam)
        a = (rms_norm(a, g_subln[l]) * (1.0 - lam_init)).reshape(B, S, V_COLS)
        sgo = spatial_gate(jax.nn.gelu(z), sg_ln_g[l], sg_ln_b[l], sg_w[l], sg_b[l])
        g_a, g_s = jnp.split(jax.nn.sigmoid(g), N_BRANCH, axis=-1)
        merged = g_a * (a @ w_branch_attn[l]) + g_s * (sgo @ w_branch_sg[l])
        h = h + merged @ w_out[l]
        h = h + cross_attention(rms_norm(h, g_xa[l]), rms_norm(mem, g_mem[l]),
                                w_xq[l], w_xkv[l], w_xo[l])
        f = rms_norm(h, g_ffn[l]) @ w_ff1[l]
        h = h + jnp.square(jax.nn.relu(f)) @ w_ff2[l]
    return rms_norm(h, g_final)
```

```python
import math
from contextlib import ExitStack

import numpy as np
import ml_dtypes

import concourse.bass as bass
import concourse.mybir as mybir
from concourse.bass_utils import run_bass_kernel_spmd

F32 = mybir.dt.float32
BF = mybir.dt.bfloat16
I32 = mybir.dt.int32
AF = mybir.ActivationFunctionType
ALU = mybir.AluOpType
AX = mybir.AxisListType

D = 1024
S = 8192
NB = 64
NOWN = 32
NG = 8
NSG = 16
EPS = 1e-6
DIN = 7168
MAGIC = 12582912.0
TWO_PI_HI = 6.28125
TWO_PI_LO = 2.0 * math.pi - 6.28125
PI_SAFE = 3.1415925

ENGS = ("pe", "act", "dve", "pool", "sp")


class Buf:
    __slots__ = ("name", "writer", "readers", "excl")

    def __init__(self, name, excl=False):
        self.name = name
        self.writer = None
        self.readers = []
        self.excl = excl


class Op:
    __slots__ = ("eng", "dma", "reads", "writes", "emit", "deps", "signal",
                 "tokval", "sem", "idx", "waits", "n_dma")

    def __init__(self, eng, dma, reads, writes, emit, sem=None, n_dma=1):
        self.eng = eng
        self.dma = dma
        self.reads = reads
        self.writes = writes
        self.emit = emit
        self.deps = []
        self.signal = dma
        self.tokval = None
        self.sem = sem
        self.waits = None
        self.n_dma = n_dma


def _is_raw(p, o):
    for b in p.writes:
        for r in o.reads:
            if r is b:
                return True
    return False


class Prog:
    def __init__(self):
        self.ops = []
        self.out_keys = []
        self.marks = {}

    def mark(self, name):
        if name not in self.marks:
            self.marks[name] = len(self.ops)

    def op(self, eng, emit, reads=(), writes=(), dma=False, sem=None, n_dma=1):
        reads = list(reads)
        writes = list(writes)
        for b in list(reads):
            if b.excl:
                reads.remove(b)
                if b not in writes:
                    writes.append(b)
        o = Op(eng, dma, reads, writes, emit, sem, n_dma)
        o.idx = len(self.ops)
        self.ops.append(o)
        return o

    def analyze(self):
        for o in self.ops:
            deps = {}
            for b in o.reads:
                p = b.writer
                if p is not None:
                    deps[p.idx] = p
            for b in o.writes:
                p = b.writer
                if p is not None:
                    deps[p.idx] = p
                for r in b.readers:
                    deps[r.idx] = r
            deps.pop(o.idx, None)
            for b in o.reads:
                b.readers.append(o)
            for b in o.writes:
                b.writer = o
                b.readers = []
            o.deps = list(deps.values())
        last_wait = {e: {} for e in ENGS}
        need = {}
        for o in self.ops:
            per = {}
            for p in o.deps:
                if p.dma:
                    key = ("dma", p.sem)
                else:
                    key = p.eng
                    if p.eng == o.eng and not o.dma:
                        if p.eng == "pe":
                            continue
                if key not in per or per[key].idx < p.idx:
                    per[key] = p
            keep = {}
            lw = last_wait[o.eng]
            for key, p in per.items():
                if lw.get(key, -1) >= p.idx:
                    continue
                lw[key] = p.idx
                keep[key] = p
                p.signal = True
            need[o.idx] = keep
        cnt = {}
        for o in self.ops:
            if o.dma:
                key = ("dma", o.sem)
                cnt[key] = cnt.get(key, 0) + 16 * o.n_dma
                o.tokval = cnt[key]
            elif o.signal:
                cnt[o.eng] = cnt.get(o.eng, 0) + 1
                o.tokval = cnt[o.eng]
        run = {}
        for o in self.ops:
            w = []
            for key, p in need[o.idx].items():
                if isinstance(key, tuple):
                    w.append((key, run.get(key, p.tokval)))
                else:
                    w.append((key, p.tokval))
            o.waits = w
            if o.dma:
                run[("dma", o.sem)] = o.tokval
        self.final = dict(cnt)

    def emit(self, nc):
        keys = set()
        for o in self.ops:
            if o.dma:
                keys.add(("dma", o.sem))
            elif o.signal:
                keys.add(o.eng)
        with ExitStack() as es:
            sems = {}
            for k in sorted(keys, key=str):
                nm = "s_" + (k if isinstance(k, str) else "d_" + str(k[1]))
                sems[k] = es.enter_context(nc.semaphore(nm))
            block = es.enter_context(nc.Block())
            engobj = {"pe": block.tensor, "act": block.scalar, "dve": block.vector,
                      "pool": block.gpsimd, "sp": block.sync}
            final = self.final
            out_keys = self.out_keys

            def make(engname):
                myops = [o for o in self.ops if o.eng == engname]

                def body(e):
                    for o in myops:
                        for key, val in o.waits:
                            e.wait_ge(sems[key], val)
                        ins = o.emit(e)
                        if o.dma:
                            if not isinstance(ins, (list, tuple)):
                                ins = [ins]
                            assert len(ins) == o.n_dma
                            for i_ in ins:
                                i_.then_inc(sems[("dma", o.sem)], 16)
                        elif o.signal:
                            ins.then_inc(sems[engname], 1)
                    if engname == "sp":
                        for k in sorted(out_keys, key=str):
                            if k in final:
                                e.wait_ge(sems[k], final[k])
                return body

            for en in ENGS:
                engobj[en](make(en))


def build_program(n_groups=NG, n_sgroups=NSG, dbg=False, trunc=None):
    nc = bass.Bass("TRN2", target_bir_lowering=False)
    P = Prog()

    def din(name, shape, dt=F32):
        return nc.dram_tensor(name, list(shape), dt, kind="ExternalInput").ap()

    xf = din("xf", [S, D])
    xo = din("xo", [NOWN * 128, D])
    posf = din("posf", [S], I32)
    poso = din("poso", [NOWN * 128], I32)
    memd = din("memb", [256, D])
    w_in = din("w_in", [D, DIN])
    w_ba = din("w_ba", [D, D])
    w_bs = din("w_bs", [D, D])
    w_out = din("w_out", [D, D])
    w_xq = din("w_xq", [D, D])
    w_xo = din("w_xo", [D, D])
    w_xkv = din("w_xkv", [D, 2 * D])
    w_ff1 = din("w_ff1", [D, 4 * D])
    w_ff2 = din("w_ff2", [4 * D, D])
    vecs = din("vecs", [7, D])
    gsubd = din("g_subln", [128])
    lamd = din("lamv", [4, 64])
    sgwd = din("sg_w", [8, 128, 128])
    sgbd = din("sg_b", [8 * 128])
    cbd = din("cb", [128, 256], BF)
    cfd = din("cf", [128, 4])
    maskd = din("maskd", [128, 2 * 2 * 128], BF)
    y = nc.dram_tensor("y", [NOWN * 128, D], F32, kind="ExternalOutput").ap()

    def dscr(name, shape, dt=BF):
        return nc.dram_tensor(name, list(shape), dt).ap()

    wb_in = dscr("wb_in", [D, DIN])
    wb_ba = dscr("wb_ba", [D, D])
    wb_bs = dscr("wb_bs", [D, D])
    wb_out = dscr("wb_out", [D, D])
    wb_xq = dscr("wb_xq", [D, D])
    wb_xo = dscr("wb_xo", [D, D])
    wb_xkv = dscr("wb_xkv", [D, 2 * D])
    wb_ff1 = dscr("wb_ff1", [D, 4 * D])
    wb_ff2 = dscr("wb_ff2", [4 * D, D])
    Kt = dscr("Kt", [D, S])
    Vs = dscr("Vs", [8, 128, NB * 129])
    dbg_out = {}

    with ExitStack() as es:
        def sb(name, shape, dt):
            return es.enter_context(nc.sbuf_tensor(name, list(shape), dt))

        cb = sb("cb_s", [128, 256], BF); b_cb = Buf("cb")
        cf = sb("cf_s", [128, 4], F32); b_cf = Buf("cf")
        mask = sb("mask_s", [128, 2, 2, 128], BF); b_mask = Buf("mask")
        gsub = sb("gsub", [128, 128], F32); b_gsub = Buf("gsub")
        lams = sb("lams", [128, 4], F32); b_lams = Buf("lams")
        WsT = sb("WsT", [128, 8, 128], BF); b_WsT = Buf("WsT")
        bs2 = sb("bs2", [2, 1024], BF); b_bs2 = Buf("bs2")
        ones1 = sb("ones1", [2, 128], BF); b_ones1 = Buf("ones1")
        kxT = sb("kxT", [128, 8, 256], BF); b_kxT = Buf("kxT")
        vxa = sb("vxa", [128, 2, 4, 257], BF); b_vxa = Buf("vxa")
        NVR = 2
        vrep = [sb(f"vrep{i}", [128, 1024], F32) for i in range(NVR)]
        b_vrep = [Buf(f"vrep{i}") for i in range(NVR)]
        NWP = 4
        wp = [sb(f"wp{i}", [128, 8, 512], BF) for i in range(NWP)]
        b_wp = [Buf(f"wp{i}") for i in range(NWP)]
        hb = [sb(f"h{i}", [128, 4, 1024], F32) for i in range(2)]
        b_h = [Buf(f"h{i}") for i in range(2)]
        nbt = sb("nbt", [128, 1024], BF); b_nbt = Buf("nbt")
        nT = sb("nT", [128, 8, 512], BF); b_nT = Buf("nT")
        T = [sb(f"T{i}", [128, 512], F32) for i in range(3)]
        b_T = [Buf(f"T{i}") for i in range(3)]
        posi = T[0].bitcast(I32); b_posi = b_T[0]
        sinT = sb("sinT", [128, 512], F32); b_sinT = Buf("sinT")
        cosT = sb("cosT", [128, 512], F32); b_cosT = Buf("cosT")
        kraw = sb("kraw", [128, 512], BF); b_kraw = Buf("kraw")
        kraw2 = None
        QT = sb("QT", [128, 8, 512], BF); b_QT = Buf("QT")
        A = [sb(f"A{i}", [128, 8192], BF) for i in range(2)]
        b_A = [Buf(f"A{i}") for i in range(2)]
        b_mT = Buf("mergedT")
        b_Al = [[b_A[0], b_mT], [b_A[1]]]
        B = [sb(f"B{i}", [128, NB * 129], BF) for i in range(2)]
        b_B = [Buf(f"B{i}") for i in range(2)]
        NPT = 3
        PT = [sb(f"PT{i}", [128, 2, 512], BF) for i in range(NPT)]
        b_PT = [Buf(f"PT{i}") for i in range(NPT)]
        atok = sb("atok", [128, 4, 1024], BF); b_atok = Buf("atok")
        vg = sb("vg", [128, 1024], F32); b_vg = Buf("vg")
        vg2 = hb[1][:, 0, :]; b_vg2 = b_h[1]
        bsf = hb[1][0:1, 1, :]; b_bsf = b_h[1]
        bsf2 = hb[1][0:1, 2, :]; b_bsf2 = b_h[1]
        vn = sb("vn", [128, 1024], BF); b_vn = Buf("vn")
        bsh = nbt[0:1, :]; b_bsh = b_nbt
        bsl = vn[0:1, :]; b_bsl = b_vn
        kraw2 = [kraw, vn]; b_kraw2 = [b_kraw, b_vn]
        nbts = [nbt, vn]; b_nbts = [b_nbt, b_vn]
        lamt = hb[0][:, 0, 0:256].rearrange("p (a b) -> p a b", a=4); b_lamt = b_h[0]
        lamp = hb[0][:, 1, 0:128].rearrange("p (a b) -> p a b", a=2); b_lamp = b_h[0]
        sm = sb("sm", [128, 96], F32)
        b_ss = Buf("ss"); b_ms = Buf("ms"); b_rstd = Buf("rstd")
        b_st6 = Buf("st6"); b_mv = Buf("mv"); b_lnr = Buf("lnr")
        b_rden = Buf("rden"); b_lr = Buf("lr"); b_ss2 = Buf("ss2"); b_rs2 = Buf("rs2")
        oraws = [sb(f"oraw{i}", [128, 258], F32) for i in range(2)]
        b_oraws = [Buf(f"oraw{i}") for i in range(2)]
        o32s = [[sb(f"o32_{a_}_{q_}", [128, 128], F32) for q_ in range(4)] for a_ in range(2)]
        b_o32s = [[Buf(f"o32_{a_}_{q_}") for q_ in range(4)] for a_ in range(2)]
        t32 = sb("t32", [128, 128], F32); b_t32 = Buf("t32")
        SS = sm[:, 0:4]; MS = sm[:, 4:8]; RSTD = sm[:, 8:12]
        ST6 = sm[:, 12:24]; MV = sm[:, 24:26]; LNV = sm[:, 26:27]; LNR = sm[:, 27:28]
        RDEN = sm[:, 28:30]; LR = sm[:, 30:31]; SS2 = sm[:, 31:32]; MS2 = sm[:, 32:33]; RS2 = sm[:, 33:34]
        RDX = sm[:, 34:35]
        SS4 = [sm[:, 36:40], sm[:, 40:44]]; MS4 = [sm[:, 44:48], sm[:, 48:52]]; RS4 = [sm[:, 52:56], sm[:, 56:60]]
        b_ss4 = [Buf("ss4a"), Buf("ss4b")]; b_ms4 = [Buf("ms4a"), Buf("ms4b")]; b_rs4 = [Buf("rs4a"), Buf("rs4b")]
        b_rdx = Buf("rdx"); b_ms2 = Buf("ms2"); b_lnv = Buf("lnv")
        SSp = [sm[:, 0:4], sm[:, 64:68]]; MSp = [sm[:, 4:8], sm[:, 68:72]]; RSp = [sm[:, 8:12], sm[:, 72:76]]
        b_ssb = [[Buf(f"ss{p_}_{k_}") for k_ in range(4)] for p_ in range(2)]
        b_msb = [[Buf(f"ms{p_}_{k_}") for k_ in range(4)] for p_ in range(2)]
        b_rsb = [[Buf(f"rs{p_}_{k_}") for k_ in range(4)] for p_ in range(2)]

        aT = A[0][:, 0:4096].rearrange("p (k t) -> p k t", t=512)
        mergedT = A[0][:, 4096:8192].rearrange("p (k t) -> p k t", t=512)
        hidT = A[1][:, :].rearrange("p (k t) -> p k t", t=512)
        gatesT = B[0][:, 0:8192].rearrange("p (k t) -> p k t", t=512)
        uT = B[1][:, 0:4096].rearrange("p (k t) -> p k t", t=512)
        sgoT = B[1][:, 4096:8192].rearrange("p (k t) -> p k t", t=512)
        KTo = A[0][:, 0:4096].rearrange("p (k t) -> p k t", t=512)
        Vaug = B[0][:, 0:8 * 4 * 129].rearrange("p (h b e) -> p h b e", h=8, b=4)

        psA = es.enter_context(nc.psum_tensor("psA", [128, 4, 512], F32))
        psB = es.enter_context(nc.psum_tensor("psB", [128, 4, 512], F32))
        b_psA = [Buf(f"psA{i}", excl=True) for i in range(4)]
        b_psB = [Buf(f"psB{i}", excl=True) for i in range(4)]
        psBb = psB.bitcast(BF)

        ident = cb[:, 0:128]
        Pm = cb[:, 128:256]
        INVF = cf[:, 0:1]
        HALFPI = cf[:, 1:2]
        NEGHALF = cf[:, 2:3]

        b_Kt = [Buf(f"Kt{i}") for i in range(NSG)]
        b_Vs = [Buf(f"Vs{i}") for i in range(NSG)]
        b_y = Buf("y")

        cast_bufs = {}

        cast_order = []
        cast_dep = []

        def cast(name, src, dst, r0, r1, c0, c1):
            b = Buf("cast_" + name)
            cast_bufs[name] = b
            dep = list(cast_dep)
            cast_order.append(b)
            P.op("pool", lambda e: e.dma_start(out=dst[r0:r1, c0:c1], in_=src[r0:r1, c0:c1]),
                 reads=dep, writes=[b], dma=True, sem="c_" + name)

        cast_list = [("xkv0", w_xkv, wb_xkv, 0, D, 0, 1024), ("xkv1", w_xkv, wb_xkv, 0, D, 1024, 2048)]
        for cblk in (0, 5, 6, 3, 4):
            cast_list.append((f"in{cblk}", w_in, wb_in, 0, D, cblk * 1024, (cblk + 1) * 1024))
        cast_list += [("ba0", w_ba, wb_ba, 0, D, 0, D), ("bs0", w_bs, wb_bs, 0, D, 0, D),
                      ("out0", w_out, wb_out, 0, D, 0, D), ("xq0", w_xq, wb_xq, 0, D, 0, D),
                      ("xo0", w_xo, wb_xo, 0, D, 0, D)]
        for i in range(4):
            cast_list.append((f"ff1{i}", w_ff1, wb_ff1, 0, D, i * 1024, (i + 1) * 1024))
        for i in range(4):
            cast_list.append((f"ff2{i}", w_ff2, wb_ff2, i * 1024, (i + 1) * 1024, 0, D))

        def emit_casts(n):
            for _ in range(n):
                if cast_list:
                    cast(*cast_list.pop(0))

        st = {"wp": 0, "mm": 0, "vr": 0, "pt": 0, "tp": 0}

        def load_panel(wb, castname, kg, cg):
            s = st["wp"] % NWP
            st["wp"] += 1
            src = wb[kg * 1024:(kg + 1) * 1024, cg * 512:(cg + 1) * 512].rearrange("(kc p) c -> p kc c", p=128)
            P.op("sp", lambda e: e.dma_start(out=wp[s][:], in_=src),
                 reads=[cast_bufs[castname]], writes=[b_wp[s]], dma=True, sem=f"wp{s}")
            return s

        def mmbank():
            i = st["mm"] % 4
            st["mm"] += 1
            return i

        def load_vrep(row):
            i = st["vr"] % NVR
            st["vr"] += 1
            src = bass.AP(vecs.tensor, row * D, [[0, 128], [1, D]])
            P.op("sp", lambda e: e.dma_start(out=vrep[i][:], in_=src), writes=[b_vrep[i]], dma=True, sem=f"vr{i}")
            return i

        def pool_pow(out_ap, in_ap, rbufs, wbufs):
            P.op("act", lambda e: e.activation(out=out_ap, in_=in_ap, func=AF.Ln),
                 reads=list(rbufs), writes=list(wbufs))
            P.op("act", lambda e: e.activation(out=out_ap, in_=out_ap, func=AF.Exp, scale=-0.5),
                 reads=list(wbufs), writes=list(wbufs))

        def norm_sq(h, bh, blk, par):
            P.op("act", lambda e: e.activation(out=nbt[:], in_=h[:, blk, :], func=AF.Square,
                                               accum_out=SSp[par][:, blk:blk + 1]),
                 reads=[bh], writes=[b_nbt, b_ssb[par][blk]])

        def norm_finish(par, nblk=4):
            P.op("dve", lambda e: e.tensor_scalar(out=MSp[par][:, 0:nblk], in0=SSp[par][:, 0:nblk], scalar1=1.0 / D,
                                                  scalar2=EPS, op0=ALU.mult, op1=ALU.add),
                 reads=b_ssb[par][0:nblk], writes=b_msb[par][0:nblk])
            pool_pow(RSp[par][:, 0:nblk], MSp[par][:, 0:nblk], b_msb[par][0:nblk], b_rsb[par][0:nblk])

        def norm_finish_blk(par, blk):
            P.op("dve", lambda e: e.tensor_scalar(out=MSp[par][:, blk:blk + 1], in0=SSp[par][:, blk:blk + 1],
                                                  scalar1=1.0 / D, scalar2=EPS, op0=ALU.mult, op1=ALU.add),
                 reads=[b_ssb[par][blk]], writes=[b_msb[par][blk]])
            pool_pow(RSp[par][:, blk:blk + 1], MSp[par][:, blk:blk + 1], [b_msb[par][blk]], [b_rsb[par][blk]])

        def norm_stats(h, bh, par, nblk=4):
            for blk in range(nblk):
                norm_sq(h, bh, blk, par)
            norm_finish(par, nblk)

        def norm_apply(h, bh, vrow, par, nblk=4, dst=None, bdst=None, vi_fixed=None):
            dst = nT if dst is None else dst
            bdst = b_nT if bdst is None else bdst
            vi = load_vrep(vrow) if vi_fixed is None else vi_fixed
            for blk in range(nblk):
                nb_ = nbts[blk % 2]
                bnb_ = b_nbts[blk % 2]
                P.op("dve", lambda e, blk=blk, nb_=nb_: e.scalar_tensor_tensor(out=nb_[:], in0=h[:, blk, :],
                                                                      scalar=RSp[par][:, blk:blk + 1], in1=vrep[vi][:],
                                                                      op0=ALU.mult, op1=ALU.mult),
                     reads=[bh, b_rsb[par][blk], b_vrep[vi]], writes=[bnb_])
                tb = st["tp"] % 2
                st["tp"] += 1
                for kc in range(8):
                    P.op("pe", lambda e, kc=kc, tb=tb, nb_=nb_: e.transpose(psBb[:, tb, kc * 128:(kc + 1) * 128],
                                                                  nb_[:, kc * 128:(kc + 1) * 128], ident),
                         reads=[bnb_, b_cb], writes=[b_psB[tb]])
                P.op("act", lambda e, blk=blk, tb=tb: e.activation(
                    out=dst[:, :, blk * 128:(blk + 1) * 128],
                    in_=psBb[:, tb, :].rearrange("p (k t) -> p k t", t=128), func=AF.Copy),
                     reads=[b_psB[tb]], writes=[bdst])

        def rmsnorm_to_nT(h, bh, vrow, nblk=4, dst=None, bdst=None, par=0):
            norm_stats(h, bh, par, nblk)
            norm_apply(h, bh, vrow, par, nblk, dst, bdst)

        def transpose_tok_to_feat(src, bsrc, dst, bdst, nblk=4):
            for blk in range(nblk):
                tb = st["tp"] % 2
                st["tp"] += 1
                for kc in range(8):
                    P.op("pe", lambda e, kc=kc, tb=tb, blk=blk: e.transpose(
                        psBb[:, tb, kc * 128:(kc + 1) * 128], src[:, blk, kc * 128:(kc + 1) * 128], ident),
                         reads=[bsrc, b_cb], writes=[b_psB[tb]])
                P.op("act", lambda e, blk=blk, tb=tb: e.activation(
                    out=dst[:, :, blk * 128:(blk + 1) * 128],
                    in_=psBb[:, tb, :].rearrange("p (k t) -> p k t", t=128), func=AF.Copy),
                     reads=[b_psB[tb]], writes=[bdst])

        def rope_tables(pos_dram, off):
            src = bass.AP(pos_dram.tensor, off, [[0, 128], [1, 512]])
            P.op("sp", lambda e: e.dma_start(out=posi[:], in_=src), writes=[b_posi], dma=True, sem="pos")
            P.op("dve", lambda e: e.tensor_copy(T[0][:], posi[:]), reads=[b_posi], writes=[b_T[0]])
            P.op("dve", lambda e: e.tensor_scalar(out=T[1][:], in0=T[0][:], scalar1=INVF, scalar2=None, op0=ALU.mult),
                 reads=[b_T[0], b_cf], writes=[b_T[1]])
            P.op("dve", lambda e: e.tensor_scalar(out=T[2][:], in0=T[1][:], scalar1=1.0 / (2.0 * math.pi), scalar2=MAGIC,
                                                  op0=ALU.mult, op1=ALU.add), reads=[b_T[1]], writes=[b_T[2]])
            P.op("dve", lambda e: e.tensor_scalar(out=T[0][:], in0=T[2][:], scalar1=-MAGIC, scalar2=None, op0=ALU.add),
                 reads=[b_T[2]], writes=[b_T[0]])
            P.op("dve", lambda e: e.scalar_tensor_tensor(out=T[2][:], in0=T[0][:], scalar=-TWO_PI_HI, in1=T[1][:],
                                                         op0=ALU.mult, op1=ALU.add),
                 reads=[b_T[0], b_T[1]], writes=[b_T[2]])
            P.op("dve", lambda e: e.scalar_tensor_tensor(out=T[1][:], in0=T[0][:], scalar=-TWO_PI_LO, in1=T[2][:],
                                                         op0=ALU.mult, op1=ALU.add),
                 reads=[b_T[0], b_T[2]], writes=[b_T[1]])
            P.op("dve", lambda e: e.tensor_scalar(out=T[2][:], in0=T[1][:], scalar1=PI_SAFE, scalar2=-PI_SAFE,
                                                  op0=ALU.min, op1=ALU.max), reads=[b_T[1]], writes=[b_T[2]])
            P.op("act", lambda e: e.activation(out=sinT[:], in_=T[2][:], func=AF.Sin), reads=[b_T[2]], writes=[b_sinT])
            P.op("dve", lambda e: e.scalar_tensor_tensor(out=T[0][:], in0=T[2][:], scalar=-1.0, in1=T[2][:],
                                                         op0=ALU.mult, op1=ALU.max),
                 reads=[b_T[2]], writes=[b_T[0]])
            P.op("act", lambda e: e.activation(out=cosT[:], in_=T[0][:], func=AF.Sin, scale=-1.0, bias=HALFPI),
                 reads=[b_T[0], b_cf], writes=[b_cosT])

        def proj_rope(panel_of_ct, dst, bdst, mid_hook=None):
            def stage_a(ct):
                s, ctl = panel_of_ct(ct)
                mb = mmbank()
                for kc in range(8):
                    P.op("pe", lambda e, s=s, ctl=ctl, kc=kc, mb=mb: e.matmul(
                        psA[:, mb, :], wp[s][:, kc, ctl * 128:(ctl + 1) * 128], nT[:, kc, :],
                        start=(kc == 0), stop=(kc == 7)),
                         reads=[b_wp[s], b_nT], writes=[b_psA[mb]])
                kb_ = ct % 2
                P.op("act", lambda e, mb=mb, kb_=kb_: e.activation(out=kraw2[kb_][:, 0:512], in_=psA[:, mb, :], func=AF.Copy),
                     reads=[b_psA[mb]], writes=[b_kraw2[kb_]])
                return mb

            def stage_b(ct, mb):
                kb_ = ct % 2
                mb2 = mmbank()
                P.op("pe", lambda e, mb2=mb2, kb_=kb_: e.matmul(psA[:, mb2, :], Pm, kraw2[kb_][:, 0:512], start=True, stop=True),
                     reads=[b_kraw2[kb_], b_cb], writes=[b_psA[mb2]])
                P.op("dve", lambda e, mb=mb: e.tensor_tensor(out=T[1][:], in0=psA[:, mb, :], in1=cosT[:], op=ALU.mult),
                     reads=[b_psA[mb], b_cosT], writes=[b_T[1]])
                P.op("dve", lambda e, mb2=mb2: e.tensor_tensor(out=T[2][:], in0=psA[:, mb2, :], in1=sinT[:], op=ALU.mult),
                     reads=[b_psA[mb2], b_sinT], writes=[b_T[2]])
                P.op("dve", lambda e, ct=ct: e.tensor_tensor(out=dst[:, ct, :], in0=T[1][:], in1=T[2][:], op=ALU.add),
                     reads=[b_T[1], b_T[2]], writes=[bdst])

            prev = None
            for ct in range(8):
                mb = stage_a(ct)
                if prev is not None:
                    stage_b(*prev)
                prev = (ct, mb)
                if ct == 4 and mid_hook is not None:
                    mid_hook()
            stage_b(*prev)

        P.op("sp", lambda e: e.dma_start(out=cb[:], in_=cbd), writes=[b_cb], dma=True, sem="cst")
        P.op("sp", lambda e: e.dma_start(out=cf[:], in_=cfd), writes=[b_cf], dma=True, sem="cst")
        P.op("sp", lambda e: e.dma_start(out=mask[:].rearrange("p a b c -> p (a b c)"), in_=maskd),
             writes=[b_mask], dma=True, sem="cst")
        P.op("sp", lambda e: e.dma_start(out=gsub[:], in_=bass.AP(gsubd.tensor, 0, [[0, 128], [1, 128]])),
             writes=[b_gsub], dma=True, sem="cst")
        P.op("sp", lambda e: e.dma_start(out=hb[0][:, 0, 0:256],
                                         in_=bass.AP(lamd.tensor, 0, [[0, 128], [1, 256]])),
             writes=[b_lamt], dma=True, sem="cst")
        P.op("sp", lambda e: e.dma_start(out=bsf, in_=bass.AP(sgbd.tensor, 0, [[0, 1], [1, 1024]])),
             writes=[b_bsf], dma=True, sem="cst")
        P.op("sp", lambda e: e.dma_start(out=vg2.rearrange("p (g j) -> p g j", g=8),
                                         in_=sgwd.rearrange("g i j -> i g j")),
             writes=[b_vg2], dma=True, sem="cst")
        P.op("dve", lambda e: e.tensor_scalar(out=gsub[:], in0=gsub[:], scalar1=0.8, scalar2=None, op0=ALU.mult),
             reads=[b_gsub], writes=[b_gsub])
        P.op("dve", lambda e: e.tensor_tensor(out=lamp[:, 0, :], in0=lamt[:, 0, :], in1=lamt[:, 1, :], op=ALU.mult),
             reads=[b_lamt], writes=[b_lamp])
        P.op("dve", lambda e: e.tensor_tensor(out=lamp[:, 1, :], in0=lamt[:, 2, :], in1=lamt[:, 3, :], op=ALU.mult),
             reads=[b_lamt, b_lamp], writes=[b_lamp])
        P.op("dve", lambda e: e.tensor_reduce(out=lams[:, 0:2], in_=lamp, axis=AX.X, op=ALU.add),
             reads=[b_lamp], writes=[b_lams])
        P.op("act", lambda e: e.activation(out=lams[:, 2:4], in_=lams[:, 0:2], func=AF.Exp),
             reads=[b_lams], writes=[b_lams])
        P.op("dve", lambda e: e.tensor_tensor(out=lams[:, 0:1], in0=lams[:, 2:3], in1=lams[:, 3:4], op=ALU.subtract),
             reads=[b_lams], writes=[b_lams])
        P.op("dve", lambda e: e.tensor_scalar(out=lams[:, 3:4], in0=lams[:, 0:1], scalar1=0.2, scalar2=None, op0=ALU.add),
             reads=[b_lams], writes=[b_lams])
        LAM = lams[:, 3:4]
        P.op("dve", lambda e: e.memset(ones1[:], 1.0), writes=[b_ones1])
        P.op("dve", lambda e: e.memset(vxa[:, :, :, 256:257], 1.0), writes=[b_vxa])
        P.op("dve", lambda e: e.tensor_copy(bsh, bsf), reads=[b_bsf], writes=[b_bsh])
        P.op("dve", lambda e: e.tensor_copy(bsf2, bsh), reads=[b_bsh], writes=[b_bsf2])
        P.op("dve", lambda e: e.tensor_tensor(out=bsf2, in0=bsf, in1=bsf2, op=ALU.subtract),
             reads=[b_bsf, b_bsf2], writes=[b_bsf2])
        P.op("dve", lambda e: e.tensor_copy(bsl, bsf2), reads=[b_bsf2], writes=[b_bsl])
        P.op("sp", lambda e: [e.dma_start(out=bs2[0:1, :], in_=bsh), e.dma_start(out=bs2[1:2, :], in_=bsl)],
             reads=[b_bsh, b_bsl], writes=[b_bs2], dma=True, sem="cst", n_dma=2)
        vg2_3 = vg2.rearrange("p (g j) -> p g j", g=8)
        P.op("dve", lambda e: e.memset(vg2_3[0:64, :, 64:128], 0.0), reads=[b_vg2], writes=[b_vg2])
        P.op("dve", lambda e: e.tensor_copy(vn[:], vg2), reads=[b_vg2], writes=[b_vn])
        for g in range(8):
            P.op("pe", lambda e, g=g: e.transpose(psBb[:, 0, g * 128:(g + 1) * 128], vn[:, g * 128:(g + 1) * 128], ident),
                 reads=[b_vn, b_cb], writes=[b_psB[0]])
        P.op("dve", lambda e: e.tensor_copy(WsT[:].rearrange("p g i -> p (g i)"), psBb[:, 0, :]),
             reads=[b_psB[0]], writes=[b_WsT])

        P.op("sp", lambda e: e.dma_start(
            out=hb[0][:], in_=xf[0:512, :].rearrange("(b p) d -> p b d", p=128)),
             writes=[b_h[0]], dma=True, sem="x0")
        stg = [A[1].bitcast(F32)[:, 0:4096], B[1].bitcast(F32)[:, 0:4096]]
        b_stg = [b_A[1], b_B[1]]
        p1slots = {}
        for i_, cg_ in enumerate((4, 5, 2, 3)):
            sl = st["wp"] % NWP
            st["wp"] += 1
            p1slots[cg_] = sl
            sg_ = stg[i_ % 2]
            P.op("sp", lambda e, cg_=cg_, sg_=sg_: e.dma_start(
                out=sg_.rearrange("p (k c) -> p k c", c=512),
                in_=w_in[:, cg_ * 512:(cg_ + 1) * 512].rearrange("(kc p) c -> p kc c", p=128)),
                 writes=[b_stg[i_ % 2]], dma=True, sem=f"stg{i_ % 2}")
            if i_ % 2 == 0:
                P.op("dve", lambda e, sl=sl, sg_=sg_: e.tensor_copy(wp[sl][:].rearrange("p k c -> p (k c)"), sg_),
                     reads=[b_stg[i_ % 2]], writes=[b_wp[sl]])
            else:
                P.op("act", lambda e, sl=sl, sg_=sg_: e.activation(out=wp[sl][:].rearrange("p k c -> p (k c)"), in_=sg_,
                                                                  func=AF.Copy),
                     reads=[b_stg[i_ % 2]], writes=[b_wp[sl]])
        pk = [p1slots[2], p1slots[3]]
        pv = [p1slots[4], p1slots[5]]
        vi_p1 = load_vrep(0)
        b_tick = Buf("tick")
        P.op("dve", lambda e: e.memset(Vaug[:, :, :, 128:129], 1.0), writes=[b_B[0]])
        for sg in range(n_sgroups):
            hh = sg % 2
            h = hb[hh]
            if sg + 1 < n_sgroups:
                hn = hb[1 - hh]
                P.op("sp", lambda e, sg=sg, hn=hn: e.dma_start(
                    out=hn[:], in_=xf[(sg + 1) * 512:(sg + 2) * 512, :].rearrange("(b p) d -> p b d", p=128)),
                     writes=[b_h[1 - hh]], dma=True, sem=f"x{1 - hh}")
            P.mark("p1_xload")
            if sg == 0:
                norm_stats(h, b_h[hh], 0)
            norm_apply(h, b_h[hh], 0, sg % 2, vi_fixed=vi_p1)
            P.mark("p1_norm")
            rope_tables(posf, sg * 512)
            P.mark("p1_rope")
            for blk in range(4):
                for cg in range(2):
                    mb = mmbank()
                    s = pv[cg]
                    for kc in range(8):
                        P.op("pe", lambda e, s=s, kc=kc, mb=mb, blk=blk: e.matmul(
                            psA[:, mb, :], nT[:, kc, blk * 128:(blk + 1) * 128], wp[s][:, kc, :],
                            start=(kc == 0), stop=(kc == 7)),
                             reads=[b_wp[s], b_nT], writes=[b_psA[mb]])
                    tick = Buf(f"tick{sg}")
                    P.op("act", lambda e, mb=mb, blk=blk, cg=cg: e.activation(
                        out=Vaug[:, cg * 4:(cg + 1) * 4, blk, 0:128],
                        in_=psA[:, mb, :].rearrange("p (h e) -> p h e", e=128), func=AF.Copy),
                         reads=[b_psA[mb]], writes=[b_B[0], tick])
            P.mark("p1_vproj")
            del cast_dep[:]
            cast_dep.append(tick)
            emit_casts(2 if (sg < 4 or n_sgroups < 16) else 1)
            P.op("act", lambda e, sg=sg: e.dma_start(
                out=Vs[:, :, sg * 4 * 129:(sg + 1) * 4 * 129].rearrange("h p e -> p h e"),
                in_=Vaug.rearrange("p h b e -> p h (b e)")),
                 reads=[b_B[0]], writes=[b_Vs[sg]], dma=True, sem="vst")
            P.mark("p1_vst")
            hook = None
            if sg + 1 < n_sgroups:
                hn_, bhn_, pn_ = hb[1 - hh], b_h[1 - hh], (sg + 1) % 2
                norm_sq(hn_, bhn_, 0, pn_)
                norm_sq(hn_, bhn_, 1, pn_)

                def hook(hn_=hn_, bhn_=bhn_, pn_=pn_):
                    norm_sq(hn_, bhn_, 2, pn_)
                    norm_sq(hn_, bhn_, 3, pn_)
                    norm_finish(pn_)
            proj_rope(lambda ct: (pk[ct // 4], ct % 4), KTo, b_A[0], mid_hook=hook)
            P.mark("p1_proj")
            P.op("act", lambda e, sg=sg: e.dma_start(
                out=Kt[:, sg * 512:(sg + 1) * 512].rearrange("(c p) t -> p c t", p=128), in_=KTo),
                 reads=[b_A[0]], writes=[b_Kt[sg]], dma=True, sem="kst")
            P.mark("p1_kst")

        del cast_dep[:]
        emit_casts(len(cast_list))
        if n_groups > 0:
            P.op("sp", lambda e: e.dma_start(out=hb[0][:, 0:2, :], in_=memd.rearrange("(b p) d -> p b d", p=128)),
                 writes=[b_h[0]], dma=True, sem="x0")
            rmsnorm_to_nT(hb[0], b_h[0], 4, nblk=2)
            for cg in range(2):
                s = load_panel(wb_xkv, "xkv0", 0, cg)
                for ctl in range(4):
                    ct = cg * 4 + ctl
                    mb = mmbank()
                    for kc in range(8):
                        P.op("pe", lambda e, s=s, ctl=ctl, kc=kc, mb=mb: e.matmul(
                            psA[:, mb, 0:256], wp[s][:, kc, ctl * 128:(ctl + 1) * 128], nT[:, kc, 0:256],
                            start=(kc == 0), stop=(kc == 7)),
                             reads=[b_wp[s], b_nT], writes=[b_psA[mb]])
                    P.op("act", lambda e, mb=mb, ct=ct: e.activation(out=kxT[:, ct, :], in_=psA[:, mb, 0:256], func=AF.Copy),
                         reads=[b_psA[mb]], writes=[b_kxT])
            for cg in range(2):
                s = load_panel(wb_xkv, "xkv1", 0, 2 + cg)
                for mbk in range(2):
                    mb = mmbank()
                    for kc in range(8):
                        P.op("pe", lambda e, s=s, kc=kc, mb=mb, mbk=mbk: e.matmul(
                            psA[:, mb, :], nT[:, kc, mbk * 128:(mbk + 1) * 128], wp[s][:, kc, :],
                            start=(kc == 0), stop=(kc == 7)),
                             reads=[b_wp[s], b_nT], writes=[b_psA[mb]])
                    P.op("act", lambda e, mb=mb, mbk=mbk, cg=cg: e.activation(
                        out=vxa[:, mbk, 2 * cg:2 * cg + 2, 0:256],
                        in_=psA[:, mb, :].rearrange("p (h e) -> p h e", e=256), func=AF.Copy),
                         reads=[b_psA[mb]], writes=[b_vxa])

        hstate = {"hcount": 0, "deferred": None, "kv0_loaded": False, "epilogue": None}

        def group_body(og):
            hh = og % 2
            h = hb[hh]
            bh = b_h[hh]
            if og == 0:
                P.op("sp", lambda e: e.dma_start(
                    out=h[:], in_=xo[0:512, :].rearrange("(b p) d -> p b d", p=128)),
                     writes=[bh], dma=True, sem=f"x{hh}")
            if og == 0:
                norm_stats(h, bh, 1)
                rope_tables(poso, 0)
            norm_apply(h, bh, 0, 1)
            pq = [load_panel(wb_in, "in0", 0, 0), load_panel(wb_in, "in0", 0, 1)]
            proj_rope(lambda ct: (pq[ct // 4], ct % 4), QT, b_QT)
            if hstate["epilogue"] is not None:
                hstate["epilogue"]()
                hstate["epilogue"] = None

            nk = 8 * (og + 1)

            def load_kv(hd, nk_):
                ab = hd % 2
                need = (nk_ * 128 + 511) // 512
                P.op("sp", lambda e: e.dma_start(
                    out=A[ab][:, 0:nk_ * 128], in_=Kt[hd * 128:(hd + 1) * 128, 0:nk_ * 128]),
                     reads=b_Kt[:need], writes=list(b_Al[ab]), dma=True, sem=f"A{ab}")
                P.op("sp", lambda e: e.dma_start(
                    out=B[ab][:, 0:nk_ * 129], in_=Vs[hd, :, 0:nk_ * 129]),
                     reads=b_Vs[:need], writes=[b_B[ab]], dma=True, sem=f"B{ab}")

            fns = []
            if not hstate["kv0_loaded"]:
                load_kv(0, nk)
            hstate["kv0_loaded"] = False
            for hd in range(8):
                ab = hd % 2
                hp_ = ab

                def qk_step(kb, hd=hd, ab=ab):
                    imin = max(4 * og, (kb) // 2)
                    q0 = imin - 4 * og
                    c0 = q0 * 128
                    sbk = kb % 2
                    for m in range(2):
                        P.op("pe", lambda e, m=m, sbk=sbk, c0=c0: e.matmul(
                            psA[:, 2 * sbk + m, c0:512], A[ab][m * 64:(m + 1) * 64, kb * 128:(kb + 1) * 128],
                            QT[m * 64:(m + 1) * 64, hd, c0:512], start=True, stop=True),
                             reads=b_Al[ab] + [b_QT], writes=[b_psA[2 * sbk + m]])
                    pb = st["pt"] % NPT
                    st["pt"] += 1
                    P.op("act", lambda e, sbk=sbk, pb=pb, c0=c0: e.activation(
                        out=PT[pb][:, :, c0:512], in_=psA[:, 2 * sbk:2 * sbk + 2, c0:512], func=AF.Exp, scale=0.125),
                         reads=[b_psA[2 * sbk], b_psA[2 * sbk + 1]], writes=[b_PT[pb]])
                    if kb >= 8 * og:
                        i = kb // 2
                        qi = i - 4 * og
                        mk0 = mask[:, i % 2, kb % 2, :]
                        mkb = bass.AP(mk0.tensor, mk0.offset, [list(mk0.ap[0]), [0, 2], [1, 128]])
                        P.op("pool", lambda e, pb=pb, qi=qi, mkb=mkb: e.tensor_tensor(
                            out=PT[pb][:, :, qi * 128:(qi + 1) * 128], in0=PT[pb][:, :, qi * 128:(qi + 1) * 128],
                            in1=mkb, op=ALU.mult),
                             reads=[b_PT[pb], b_mask], writes=[b_PT[pb]])
                    return pb, q0

                hp = hp_

                def pv_step(kb, pb, q0, hd=hd, ab=ab, hp=hp):
                    for qi in range(q0, 4):
                        i = 4 * og + qi
                        last = 2 * i + 1
                        for m in range(2):
                            P.op("pe", lambda e, qi=qi, m=m, last=last: e.matmul(
                                psB[:, qi, m * 129:(m + 1) * 129], PT[pb][:, m, qi * 128:(qi + 1) * 128],
                                B[ab][:, kb * 129:(kb + 1) * 129], start=(kb == 0 and m == 0),
                                stop=(kb == last and m == 1), skip_group_check=True),
                                 reads=[b_PT[pb], b_B[ab]], writes=[b_psB[qi]])
                        if kb == last:
                            oq = o32s[hp][qi]
                            boq = b_o32s[hp][qi]
                            oraw = oraws[qi % 2]
                            b_oraw = b_oraws[qi % 2]
                            P.op("dve", lambda e, qi=qi, oraw=oraw: e.tensor_copy(oraw[:], psB[:, qi, 0:258]),
                                 reads=[b_psB[qi]], writes=[b_oraw])
                            Ov = oraw[:].rearrange("p (m e) -> p m e", e=129)
                            P.op("dve", lambda e, Ov=Ov: e.reciprocal(RDEN.rearrange("p (m o) -> p m o", o=1), Ov[:, :, 128:129]),
                                 reads=[b_oraw], writes=[b_rden])
                            P.op("dve", lambda e: e.tensor_tensor(out=LR, in0=RDEN[:, 1:2], in1=LAM, op=ALU.mult),
                                 reads=[b_rden, b_lams], writes=[b_lr])
                            P.op("dve", lambda e, oraw=oraw: e.tensor_scalar(out=t32[:], in0=oraw[:, 129:257], scalar1=LR,
                                                                  scalar2=None, op0=ALU.mult),
                                 reads=[b_oraw, b_lr], writes=[b_t32])
                            P.op("dve", lambda e, oq=oq, oraw=oraw: e.scalar_tensor_tensor(
                                out=oq[:], in0=oraw[:, 0:128], scalar=RDEN[:, 0:1], in1=t32[:],
                                op0=ALU.mult, op1=ALU.subtract),
                                 reads=[b_oraw, b_rden, b_t32], writes=[boq])
                            P.op("dve", lambda e, oq=oq: e.tensor_tensor(out=t32[:], in0=oq[:], in1=oq[:], op=ALU.mult),
                                 reads=[boq], writes=[b_t32])
                            P.op("dve", lambda e, qi=qi, hp=hp: e.tensor_reduce(out=SS4[hp][:, qi:qi + 1], in_=t32[:], axis=AX.X, op=ALU.add),
                                 reads=[b_t32], writes=[b_ss4[hp]])

                def finalize2(hd=hd, hp=hp):
                    P.op("dve", lambda e: e.tensor_scalar(out=MS4[hp], in0=SS4[hp], scalar1=1.0 / 128, scalar2=EPS,
                                                          op0=ALU.mult, op1=ALU.add), reads=[b_ss4[hp]], writes=[b_ms4[hp]])
                    pool_pow(RS4[hp], MS4[hp], [b_ms4[hp]], [b_rs4[hp]])
                    for qi in range(4):
                        P.op("dve", lambda e, qi=qi: e.scalar_tensor_tensor(
                            out=atok[:, qi, hd * 128:(hd + 1) * 128], in0=o32s[hp][qi][:], scalar=RS4[hp][:, qi:qi + 1],
                            in1=gsub[:], op0=ALU.mult, op1=ALU.mult),
                             reads=[b_o32s[hp][qi], b_rs4[hp], b_gsub], writes=[b_atok])

                fns.append((qk_step, pv_step, finalize2))

            seq = [(hd_, kb_) for hd_ in range(8) for kb_ in range(nk)]
            pend = {}
            LAG = 2
            deferred = []
            for idx in range(len(seq) + LAG):
                if idx < len(seq):
                    hd_, kb_ = seq[idx]
                    if idx == 0:
                        load_kv(1, nk)
                    pend[idx] = fns[hd_][0](kb_)
                j = idx - LAG
                if j >= 0:
                    hd_, kb_ = seq[j]
                    fns[hd_][1](kb_, *pend.pop(j))
                    if kb_ == nk - 1:
                        deferred.append((idx + 2, fns[hd_][2]))
                        if hd_ + 2 < 8:
                            load_kv(hd_ + 2, nk)
                while deferred and deferred[0][0] <= idx:
                    deferred.pop(0)[1]()
            while deferred:
                deferred.pop(0)[1]()

            if dbg and og == 0:
                dbg_out["atok"] = nc.dram_tensor("dbg_atok", [128, 4 * 1024], BF, kind="ExternalOutput").ap()
                P.op("pool", lambda e: e.dma_start(out=dbg_out["atok"], in_=atok[:].rearrange("p a b -> p (a b)")),
                     reads=[b_atok], writes=[b_y], dma=True, sem="yout")
                dbg_out["QT"] = nc.dram_tensor("dbg_QT", [128, 8 * 512], BF, kind="ExternalOutput").ap()
                P.op("pool", lambda e: e.dma_start(out=dbg_out["QT"], in_=QT[:].rearrange("p a b -> p (a b)")),
                     reads=[b_QT], writes=[b_y], dma=True, sem="yout")

            for cg in range(4):
                s = load_panel(wb_in, "in5" if cg < 2 else "in6", 0, 10 + cg)
                for ctl in range(4):
                    ct = cg * 4 + ctl
                    mb = mmbank()
                    for kc in range(8):
                        P.op("pe", lambda e, s=s, ctl=ctl, kc=kc, mb=mb: e.matmul(
                            psA[:, mb, :], wp[s][:, kc, ctl * 128:(ctl + 1) * 128], nT[:, kc, :],
                            start=(kc == 0), stop=(kc == 7)),
                             reads=[b_wp[s], b_nT], writes=[b_psA[mb]])
                    P.op("act", lambda e, mb=mb, ct=ct: e.activation(out=gatesT[:, ct, :], in_=psA[:, mb, :],
                                                                     func=AF.Sigmoid),
                         reads=[b_psA[mb]], writes=[b_B[0]])
            transpose_tok_to_feat(atok, b_atok, aT, b_A[0])

            for cg in range(2):
                s = load_panel(wb_in, "in3", 0, 6 + cg)
                for ctl in range(4):
                    ct = cg * 4 + ctl
                    mb = mmbank()
                    for kc in range(8):
                        P.op("pe", lambda e, s=s, ctl=ctl, kc=kc, mb=mb: e.matmul(
                            psA[:, mb, :], wp[s][:, kc, ctl * 128:(ctl + 1) * 128], nT[:, kc, :],
                            start=(kc == 0), stop=(kc == 7)),
                             reads=[b_wp[s], b_nT], writes=[b_psA[mb]])
                    P.op("act", lambda e, mb=mb, ct=ct: e.activation(out=uT[:, ct, :], in_=psA[:, mb, :],
                                                                     func=AF.Gelu_apprx_tanh),
                         reads=[b_psA[mb]], writes=[b_B[1]])
            pvv = [load_panel(wb_in, "in4", 0, 8), load_panel(wb_in, "in4", 0, 9)]
            vi_g = load_vrep(5)
            vi_b = load_vrep(6)
            SG3 = psB[:, 2:4, :].rearrange("p a (g i) -> p (a g) i", i=128)
            vgb = [vg[:], hb[1 - hh][:, 0, :], QT.bitcast(F32)[:, 0:4, :].rearrange("p a b -> p (a b)")]
            b_vgb = [b_vg, b_h[1 - hh], b_QT]

            def vproj(blk):
                vgt = vgb[blk % 3]
                bvg = b_vgb[blk % 3]
                for cg in range(2):
                    mb = mmbank()
                    s = pvv[cg]
                    for kc in range(8):
                        P.op("pe", lambda e, s=s, kc=kc, mb=mb: e.matmul(
                            psA[:, mb, :], nT[:, kc, blk * 128:(blk + 1) * 128], wp[s][:, kc, :],
                            start=(kc == 0), stop=(kc == 7)),
                             reads=[b_wp[s], b_nT], writes=[b_psA[mb]])
                    P.op("act", lambda e, mb=mb, cg=cg: e.activation(out=vgt[:, cg * 512:(cg + 1) * 512], in_=psA[:, mb, :],
                                                                     func=AF.Gelu_apprx_tanh),
                         reads=[b_psA[mb]], writes=[bvg])

            def ln_spatial(blk):
                vgt = vgb[blk % 3]
                bvg = b_vgb[blk % 3]
                P.op("dve", lambda e: e.bn_stats(ST6[:, 0:6], vgt[:, 0:512]), reads=[bvg], writes=[b_st6])
                P.op("dve", lambda e: e.bn_stats(ST6[:, 6:12], vgt[:, 512:1024]), reads=[bvg, b_st6], writes=[b_st6])
                P.op("dve", lambda e: e.bn_aggr(MV, ST6), reads=[b_st6], writes=[b_mv])
                P.op("dve", lambda e: e.tensor_scalar(out=LNV, in0=MV[:, 1:2], scalar1=1e-5, scalar2=None, op0=ALU.add),
                     reads=[b_mv], writes=[b_lnv])
                pool_pow(LNR, LNV, [b_lnv], [b_lnr])
                P.op("dve", lambda e: e.scalar_tensor_tensor(out=vgt, in0=vgt, scalar=MV[:, 0:1], in1=vrep[vi_g][:],
                                                             op0=ALU.subtract, op1=ALU.mult),
                     reads=[bvg, b_mv, b_vrep[vi_g]], writes=[bvg])
                P.op("dve", lambda e: e.scalar_tensor_tensor(out=vn[:], in0=vgt, scalar=LNR, in1=vrep[vi_b][:],
                                                             op0=ALU.mult, op1=ALU.add),
                     reads=[bvg, b_lnr, b_vrep[vi_b]], writes=[b_vn])
                for g in range(8):
                    bk = 2 + g // 4
                    P.op("pe", lambda e, g=g: e.matmul(SG3[:, g, :], vn[:, g * 128:(g + 1) * 128], WsT[:, g, :],
                                                       start=(g % 4 == 0), stop=False, skip_group_check=True),
                         reads=[b_vn, b_WsT], writes=[b_psB[bk]])
                    P.op("pe", lambda e, g=g: e.matmul(SG3[:, g, :], ones1[:], bs2[:, g * 128:(g + 1) * 128],
                                                       start=False, stop=True, skip_group_check=True),
                         reads=[b_ones1, b_bs2], writes=[b_psB[bk]])
                P.op("dve", lambda e: e.tensor_tensor(out=sgoT[:, :, blk * 128:(blk + 1) * 128], in0=SG3,
                                                      in1=uT[:, :, blk * 128:(blk + 1) * 128], op=ALU.mult),
                     reads=[b_psB[2], b_psB[3], b_B[1]], writes=[b_B[1]])

            vproj(0)
            vproj(1)
            for blk in range(4):
                if blk + 2 < 4:
                    vproj(blk + 2)
                ln_spatial(blk)
            for cg in range(2):
                sa = load_panel(wb_ba, "ba0", 0, cg)
                ss_ = load_panel(wb_bs, "bs0", 0, cg)
                for ctl in range(4):
                    ct = cg * 4 + ctl
                    m1 = mmbank()
                    for kc in range(8):
                        P.op("pe", lambda e, sa=sa, ctl=ctl, kc=kc, m1=m1: e.matmul(
                            psA[:, m1, :], wp[sa][:, kc, ctl * 128:(ctl + 1) * 128], aT[:, kc, :],
                            start=(kc == 0), stop=(kc == 7)),
                             reads=[b_wp[sa], b_A[0]], writes=[b_psA[m1]])
                    m2 = mmbank()
                    for kc in range(8):
                        P.op("pe", lambda e, ss_=ss_, ctl=ctl, kc=kc, m2=m2: e.matmul(
                            psA[:, m2, :], wp[ss_][:, kc, ctl * 128:(ctl + 1) * 128], sgoT[:, kc, :],
                            start=(kc == 0), stop=(kc == 7)),
                             reads=[b_wp[ss_], b_B[1]], writes=[b_psA[m2]])
                    P.op("dve", lambda e, m1=m1, ct=ct: e.tensor_tensor(out=T[1][:], in0=psA[:, m1, :], in1=gatesT[:, ct, :],
                                                                        op=ALU.mult),
                         reads=[b_psA[m1], b_B[0]], writes=[b_T[1]])
                    P.op("dve", lambda e, m2=m2, ct=ct: e.tensor_tensor(out=T[2][:], in0=psA[:, m2, :], in1=gatesT[:, 8 + ct, :],
                                                                        op=ALU.mult),
                         reads=[b_psA[m2], b_B[0]], writes=[b_T[2]])
                    P.op("dve", lambda e, ct=ct: e.tensor_tensor(out=mergedT[:, ct, :], in0=T[1][:], in1=T[2][:], op=ALU.add),
                         reads=[b_T[1], b_T[2]], writes=[b_mT])

            def out_proj(wb, castname, srcT, bsrc, stats_par=None):
                ss2 = [load_panel(wb, castname, 0, cg) for cg in range(2)]
                for blk in range(4):
                    for cg in range(2):
                        s = ss2[cg]
                        mb = mmbank()
                        for kc in range(8):
                            P.op("pe", lambda e, s=s, kc=kc, mb=mb, blk=blk: e.matmul(
                                psA[:, mb, :], srcT[:, kc, blk * 128:(blk + 1) * 128], wp[s][:, kc, :],
                                start=(kc == 0), stop=(kc == 7)),
                                 reads=[b_wp[s], bsrc], writes=[b_psA[mb]])
                        P.op("dve", lambda e, mb=mb, blk=blk, cg=cg: e.tensor_tensor(
                            out=h[:, blk, cg * 512:(cg + 1) * 512], in0=psA[:, mb, :],
                            in1=h[:, blk, cg * 512:(cg + 1) * 512], op=ALU.add),
                             reads=[b_psA[mb], bh], writes=[bh])
                    if stats_par is not None:
                        norm_sq(h, bh, blk, stats_par)
                        norm_finish_blk(stats_par, blk)

            out_proj(wb_out, "out0", mergedT, b_mT, stats_par=0)

            if dbg and og == 0:
                dbg_out["h1"] = nc.dram_tensor("dbg_h1", [128, 4 * 1024], F32, kind="ExternalOutput").ap()
                P.op("pool", lambda e: e.dma_start(out=dbg_out["h1"], in_=h[:].rearrange("p a b -> p (a b)")),
                     reads=[bh], writes=[b_y], dma=True, sem="yout")

            norm_apply(h, bh, 1, 0)
            for cg in range(2):
                s = load_panel(wb_xq, "xq0", 0, cg)
                for ctl in range(4):
                    ct = cg * 4 + ctl
                    mb = mmbank()
                    for kc in range(8):
                        P.op("pe", lambda e, s=s, ctl=ctl, kc=kc, mb=mb: e.matmul(
                            psA[:, mb, :], wp[s][:, kc, ctl * 128:(ctl + 1) * 128], nT[:, kc, :],
                            start=(kc == 0), stop=(kc == 7)),
                             reads=[b_wp[s], b_nT], writes=[b_psA[mb]])
                    P.op("act", lambda e, mb=mb, ct=ct: e.activation(out=QT[:, ct, :], in_=psA[:, mb, :], func=AF.Copy),
                         reads=[b_psA[mb]], writes=[b_QT])
            for hx in range(4):
                sbk = hx % 2
                for mbk in range(2):
                    for c in range(2):
                        P.op("pe", lambda e, sbk=sbk, mbk=mbk, c=c, hx=hx: e.matmul(
                            psA[:, 2 * sbk + mbk, :], kxT[:, hx * 2 + c, mbk * 128:(mbk + 1) * 128], QT[:, hx * 2 + c, :],
                            start=(c == 0), stop=(c == 1)),
                             reads=[b_kxT, b_QT], writes=[b_psA[2 * sbk + mbk]])
                pb = st["pt"] % NPT
                st["pt"] += 1
                P.op("act", lambda e, sbk=sbk, pb=pb: e.activation(
                    out=PT[pb][:], in_=psA[:, 2 * sbk:2 * sbk + 2, :], func=AF.Exp, scale=1.0 / 16.0),
                     reads=[b_psA[2 * sbk], b_psA[2 * sbk + 1]], writes=[b_PT[pb]])
                for blk in range(4):
                    for mbk in range(2):
                        P.op("pe", lambda e, blk=blk, mbk=mbk, pb=pb, hx=hx: e.matmul(
                            psB[:, blk, 0:257], PT[pb][:, mbk, blk * 128:(blk + 1) * 128], vxa[:, mbk, hx, :],
                            start=(mbk == 0), stop=(mbk == 1)),
                             reads=[b_PT[pb], b_vxa], writes=[b_psB[blk]])
                    P.op("dve", lambda e, blk=blk: e.reciprocal(RDX, psB[:, blk, 256:257]),
                         reads=[b_psB[blk]], writes=[b_rdx])
                    P.op("dve", lambda e, blk=blk, hx=hx: e.tensor_scalar(
                        out=atok[:, blk, hx * 256:(hx + 1) * 256], in0=psB[:, blk, 0:256], scalar1=RDX, scalar2=None,
                        op0=ALU.mult),
                         reads=[b_psB[blk], b_rdx], writes=[b_atok])
            transpose_tok_to_feat(atok, b_atok, aT, b_A[0])
            out_proj(wb_xo, "xo0", aT, b_A[0], stats_par=0)

            if og + 1 < n_groups:
                hn = hb[1 - hh]
                P.op("sp", lambda e: e.dma_start(
                    out=hn[:], in_=xo[(og + 1) * 512:(og + 2) * 512, :].rearrange("(b p) d -> p b d", p=128)),
                     writes=[b_h[1 - hh]], dma=True, sem=f"x{1 - hh}")
                load_kv(0, 8 * (og + 2))
                hstate["kv0_loaded"] = True
            norm_apply(h, bh, 2, 0)
            for half in range(2):
                for cgl in range(4):
                    cg = half * 4 + cgl
                    s = load_panel(wb_ff1, f"ff1{cg // 2}", 0, cg)
                    for ctl in range(4):
                        ctloc = cgl * 4 + ctl
                        mb = mmbank()
                        for kc in range(8):
                            P.op("pe", lambda e, s=s, ctl=ctl, kc=kc, mb=mb: e.matmul(
                                psA[:, mb, :], wp[s][:, kc, ctl * 128:(ctl + 1) * 128], nT[:, kc, :],
                                start=(kc == 0), stop=(kc == 7)),
                                 reads=[b_wp[s], b_nT], writes=[b_psA[mb]])
                        P.op("act", lambda e, mb=mb: e.activation(out=kraw[:], in_=psA[:, mb, :], func=AF.Square),
                             reads=[b_psA[mb]], writes=[b_kraw])
                        P.op("dve", lambda e, mb=mb, ctloc=ctloc: e.scalar_tensor_tensor(
                            out=hidT[:, ctloc, :], in0=psA[:, mb, :], scalar=0.0, in1=kraw[:],
                            op0=ALU.is_gt, op1=ALU.mult),
                             reads=[b_psA[mb], b_kraw], writes=[b_A[1]])
                if half == 1 and og + 1 < n_groups:
                    rope_tables(poso, (og + 1) * 512)
                    norm_stats(hb[1 - hh], b_h[1 - hh], 1)
                sp4 = [[load_panel(wb_ff2, f"ff2{half * 2 + kgl}", half * 2 + kgl, cg) for kgl in range(2)]
                       for cg in range(2)]
                for blk in range(4):
                    for cg in range(2):
                        sp_ = sp4[cg]
                        mb = mmbank()
                        for kgl in range(2):
                            for kc in range(8):
                                P.op("pe", lambda e, kgl=kgl, kc=kc, mb=mb, blk=blk, sp_=sp_: e.matmul(
                                    psA[:, mb, :], hidT[:, kgl * 8 + kc, blk * 128:(blk + 1) * 128], wp[sp_[kgl]][:, kc, :],
                                    start=(kgl == 0 and kc == 0), stop=(kgl == 1 and kc == 7)),
                                     reads=[b_wp[sp_[kgl]], b_A[1]], writes=[b_psA[mb]])
                        P.op("dve", lambda e, mb=mb, blk=blk, cg=cg: e.tensor_tensor(
                            out=h[:, blk, cg * 512:(cg + 1) * 512], in0=psA[:, mb, :],
                            in1=h[:, blk, cg * 512:(cg + 1) * 512], op=ALU.add),
                             reads=[b_psA[mb], bh], writes=[bh])
                    if half == 1:
                        norm_sq(h, bh, blk, 0)
                        norm_finish_blk(0, blk)

            def epilogue():
                vi = load_vrep(3)
                for blk in range(4):
                    P.op("dve", lambda e, blk=blk: e.scalar_tensor_tensor(out=vg[:], in0=h[:, blk, :],
                                                                          scalar=RSp[0][:, blk:blk + 1], in1=vrep[vi][:],
                                                                          op0=ALU.mult, op1=ALU.mult),
                         reads=[bh, b_rsb[0][blk], b_vrep[vi]], writes=[b_vg])
                    r0 = (og * 4 + blk) * 128
                    P.op("pool", lambda e, r0=r0: e.dma_start(out=y[r0:r0 + 128, :], in_=vg[:]),
                         reads=[b_vg], writes=[b_y], dma=True, sem="yout")

            hstate["epilogue"] = epilogue

        for og_ in range(n_groups):
            group_body(og_)
        if hstate["epilogue"] is not None:
            hstate["epilogue"]()

        if dbg:
            dbg_out["Kt"] = Kt
        P.out_keys = [("dma", "yout")] if n_groups > 0 else [("dma", "kst"), ("dma", "vst")] + [
            ("dma", "c_" + nm) for nm in cast_bufs]
        if trunc is not None:
            P.ops = P.ops[:P.marks[trunc]]
            P.out_keys = None
        P.analyze()
        if P.out_keys is None:
            P.out_keys = [k for k in P.final if isinstance(k, tuple)]
        P.emit(nc)
    return nc, P


def own_blocks(hf):
    out = []
    for i in range(NOWN):
        if i % 2 == 0:
            out.append(2 * i + (0 if hf == 0 else 1))
        else:
            out.append(2 * i + (1 if hf == 0 else 0))
    return out


def make_consts():
    cbm = np.zeros((128, 256), np.float32)
    cbm[:, :128] = np.eye(128, dtype=np.float32)
    Pm = np.zeros((128, 128), np.float32)
    for base in (0, 64):
        for d in range(8):
            m = base + d
            Pm[base + d + 8, m] = -1.0
            m2 = base + 8 + d
            Pm[base + d, m2] = 1.0
    cbm[:, 128:] = Pm
    cfm = np.zeros((128, 4), np.float32)
    idx = np.arange(0, 16, 2, dtype=np.float32)
    inv_freq = np.power(np.float32(500000.0), -idx / np.float32(16)).astype(np.float32)
    for r in range(128):
        d = r % 64
        if d < 16:
            cfm[r, 0] = inv_freq[d % 8]
    cfm[:, 1] = np.float32(math.pi / 2)
    cfm[:, 2] = -0.5
    return cbm.astype(ml_dtypes.bfloat16), cfm


def make_mask(hf):
    Dm = np.ones((128, 128), np.float32)
    Dm[64:, :64] = 0.0
    ones = np.ones((128, 128), np.float32)
    zeros = np.zeros((128, 128), np.float32)
    typeA = (Dm, zeros)
    typeB = (ones, Dm)
    m = np.zeros((128, 2, 2, 128), np.float32)
    for par in range(2):
        j_is_even = (par == 0 and hf == 0) or (par == 1 and hf == 1)
        t = typeA if j_is_even else typeB
        m[:, par, 0, :] = t[0]
        m[:, par, 1, :] = t[1]
    return m.reshape(128, 512).astype(ml_dtypes.bfloat16)


_CACHE = {}


def kernel(x, mem, positions, g_mix, w_in, lam_q1, lam_k1, lam_q2, lam_k2, g_subln,
           sg_ln_g, sg_ln_b, sg_w, sg_b, w_branch_attn, w_branch_sg, w_out,
           g_xa, g_mem, w_xq, w_xkv, w_xo, g_ffn, w_ff1, w_ff2, g_final):
    f = lambda a: np.ascontiguousarray(np.asarray(a, dtype=np.float32))
    x = f(x); mem = f(mem)
    positions = np.ascontiguousarray(np.asarray(positions, dtype=np.int32))
    if "nc" not in _CACHE:
        _CACHE["nc"] = build_program()[0]
    nc = _CACHE["nc"]
    cbm, cfm = make_consts()
    vecs = np.stack([f(g_mix)[0], f(g_xa)[0], f(g_ffn)[0], f(g_final), f(g_mem)[0], f(sg_ln_g)[0], f(sg_ln_b)[0]], 0)
    lamv = np.stack([f(lam_q1)[0], f(lam_k1)[0], f(lam_q2)[0], f(lam_k2)[0]], 0)
    shared = {
        "w_in": f(w_in)[0], "w_ba": f(w_branch_attn)[0], "w_bs": f(w_branch_sg)[0], "w_out": f(w_out)[0],
        "w_xq": f(w_xq)[0], "w_xo": f(w_xo)[0], "w_xkv": f(w_xkv)[0], "w_ff1": f(w_ff1)[0], "w_ff2": f(w_ff2)[0],
        "vecs": np.ascontiguousarray(vecs), "g_subln": f(g_subln)[0], "lamv": np.ascontiguousarray(lamv),
        "sg_w": f(sg_w)[0], "sg_b": f(sg_b)[0].reshape(-1), "cb": cbm, "cf": cfm,
    }
    in_maps = []
    owns = []
    for c in range(8):
        b, hf = c // 2, c % 2
        ob = own_blocks(hf)
        owns.append(ob)
        rows = np.concatenate([np.arange(j * 128, (j + 1) * 128) for j in ob])
        m = dict(shared)
        m["xf"] = x[b]
        m["xo"] = np.ascontiguousarray(x[b][rows])
        m["posf"] = positions[b]
        m["poso"] = np.ascontiguousarray(positions[b][rows])
        m["memb"] = mem[b]
        m["maskd"] = make_mask(hf)
        in_maps.append(m)
    res = run_bass_kernel_spmd(nc, in_maps, core_ids=list(range(8)))
    out = np.empty((4, S, D), np.float32)
    for c in range(8):
        b = c // 2
        yc = np.asarray(res.results[c]["y"], dtype=np.float32)
        for i, j in enumerate(owns[c]):
            out[b, j * 128:(j + 1) * 128, :] = yc[i * 128:(i + 1) * 128, :]
    return out
```

```python
import math
from contextlib import ExitStack

import numpy as np
import ml_dtypes

import concourse.bass as bass
import concourse.mybir as mybir
from concourse.bass_utils import run_bass_kernel_spmd

F32 = mybir.dt.float32
BF = mybir.dt.bfloat16
I32 = mybir.dt.int32
AF = mybir.ActivationFunctionType
ALU = mybir.AluOpType
AX = mybir.AxisListType

D = 1024
S = 8192
NB = 64
NOWN = 32
NG = 8
NSG = 16
EPS = 1e-6
DIN = 7168
MAGIC = 12582912.0
TWO_PI_HI = 6.28125
TWO_PI_LO = 2.0 * math.pi - 6.28125
PI_SAFE = 3.1415925

ENGS = ("pe", "act", "dve", "pool", "sp")


class Buf:
    __slots__ = ("name", "writer", "readers", "excl")

    def __init__(self, name, excl=False):
        self.name = name
        self.writer = None
        self.readers = []
        self.excl = excl


class Op:
    __slots__ = ("eng", "dma", "reads", "writes", "emit", "deps", "signal",
                 "tokval", "sem", "idx", "waits", "n_dma")

    def __init__(self, eng, dma, reads, writes, emit, sem=None, n_dma=1):
        self.eng = eng
        self.dma = dma
        self.reads = reads
        self.writes = writes
        self.emit = emit
        self.deps = []
        self.signal = dma
        self.tokval = None
        self.sem = sem
        self.waits = None
        self.n_dma = n_dma


def _is_raw(p, o):
    for b in p.writes:
        for r in o.reads:
            if r is b:
                return True
    return False


class Prog:
    def __init__(self):
        self.ops = []
        self.out_keys = []
        self.marks = {}

    def mark(self, name):
        if name not in self.marks:
            self.marks[name] = len(self.ops)

    def op(self, eng, emit, reads=(), writes=(), dma=False, sem=None, n_dma=1):
        reads = list(reads)
        writes = list(writes)
        for b in list(reads):
            if b.excl:
                reads.remove(b)
                if b not in writes:
                    writes.append(b)
        o = Op(eng, dma, reads, writes, emit, sem, n_dma)
        o.idx = len(self.ops)
        self.ops.append(o)
        return o

    def analyze(self):
        for o in self.ops:
            deps = {}
            for b in o.reads:
                p = b.writer
                if p is not None:
                    deps[p.idx] = p
            for b in o.writes:
                p = b.writer
                if p is not None:
                    deps[p.idx] = p
                for r in b.readers:
                    deps[r.idx] = r
            deps.pop(o.idx, None)
            for b in o.reads:
                b.readers.append(o)
            for b in o.writes:
                b.writer = o
                b.readers = []
            o.deps = list(deps.values())
        last_wait = {e: {} for e in ENGS}
        need = {}
        for o in self.ops:
            per = {}
            for p in o.deps:
                if p.dma:
                    key = ("dma", p.sem)
                else:
                    key = p.eng
                    if p.eng == o.eng and not o.dma:
                        if p.eng == "pe":
                            continue
                if key not in per or per[key].idx < p.idx:
                    per[key] = p
            keep = {}
            lw = last_wait[o.eng]
            for key, p in per.items():
                if lw.get(key, -1) >= p.idx:
                    continue
                lw[key] = p.idx
                keep[key] = p
                p.signal = True
            need[o.idx] = keep
        cnt = {}
        for o in self.ops:
            if o.dma:
                key = ("dma", o.sem)
                cnt[key] = cnt.get(key, 0) + 16 * o.n_dma
                o.tokval = cnt[key]
            elif o.signal:
                cnt[o.eng] = cnt.get(o.eng, 0) + 1
                o.tokval = cnt[o.eng]
        run = {}
        for o in self.ops:
            w = []
            for key, p in need[o.idx].items():
                if isinstance(key, tuple):
                    w.append((key, run.get(key, p.tokval)))
                else:
                    w.append((key, p.tokval))
            o.waits = w
            if o.dma:
                run[("dma", o.sem)] = o.tokval
        self.final = dict(cnt)

    def emit(self, nc):
        keys = set()
        for o in self.ops:
            if o.dma:
                keys.add(("dma", o.sem))
            elif o.signal:
                keys.add(o.eng)
        with ExitStack() as es:
            sems = {}
            for k in sorted(keys, key=str):
                nm = "s_" + (k if isinstance(k, str) else "d_" + str(k[1]))
                sems[k] = es.enter_context(nc.semaphore(nm))
            block = es.enter_context(nc.Block())
            engobj = {"pe": block.tensor, "act": block.scalar, "dve": block.vector,
                      "pool": block.gpsimd, "sp": block.sync}
            final = self.final
            out_keys = self.out_keys

            def make(engname):
                myops = [o for o in self.ops if o.eng == engname]

                def body(e):
                    for o in myops:
                        for key, val in o.waits:
                            e.wait_ge(sems[key], val)
                        ins = o.emit(e)
                        if o.dma:
                            if not isinstance(ins, (list, tuple)):
                                ins = [ins]
                            assert len(ins) == o.n_dma
                            for i_ in ins:
                                i_.then_inc(sems[("dma", o.sem)], 16)
                        elif o.signal:
                            ins.then_inc(sems[engname], 1)
                    if engname == "sp":
                        for k in sorted(out_keys, key=str):
                            if k in final:
                                e.wait_ge(sems[k], final[k])
                return body

            for en in ENGS:
                engobj[en](make(en))


def build_program(n_groups=NG, n_sgroups=NSG, dbg=False, trunc=None):
    nc = bass.Bass("TRN2", target_bir_lowering=False)
    P = Prog()

    def din(name, shape, dt=F32):
        return nc.dram_tensor(name, list(shape), dt, kind="ExternalInput").ap()

    xf = din("xf", [S, D])
    xo = din("xo", [NOWN * 128, D])
    posf = din("posf", [S], I32)
    poso = din("poso", [NOWN * 128], I32)
    memd = din("memb", [256, D])
    w_in = din("w_in", [D, DIN])
    w_ba = din("w_ba", [D, D])
    w_bs = din("w_bs", [D, D])
    w_out = din("w_out", [D, D])
    w_xq = din("w_xq", [D, D])
    w_xo = din("w_xo", [D, D])
    w_xkv = din("w_xkv", [D, 2 * D])
    w_ff1 = din("w_ff1", [D, 4 * D])
    w_ff2 = din("w_ff2", [4 * D, D])
    vecs = din("vecs", [7, D])
    gsubd = din("g_subln", [128])
    lamd = din("lamv", [4, 64])
    sgwd = din("sg_w", [8, 128, 128])
    sgbd = din("sg_b", [8 * 128])
    cbd = din("cb", [128, 256], BF)
    cfd = din("cf", [128, 4])
    maskd = din("maskd", [128, 2 * 2 * 128], BF)
    y = nc.dram_tensor("y", [NOWN * 128, D], F32, kind="ExternalOutput").ap()

    def dscr(name, shape, dt=BF):
        return nc.dram_tensor(name, list(shape), dt).ap()

    wb_in = dscr("wb_in", [D, DIN])
    wb_ba = dscr("wb_ba", [D, D])
    wb_bs = dscr("wb_bs", [D, D])
    wb_out = dscr("wb_out", [D, D])
    wb_xq = dscr("wb_xq", [D, D])
    wb_xo = dscr("wb_xo", [D, D])
    wb_xkv = dscr("wb_xkv", [D, 2 * D])
    wb_ff1 = dscr("wb_ff1", [D, 4 * D])
    wb_ff2 = dscr("wb_ff2", [4 * D, D])
    Kt = dscr("Kt", [D, S])
    Vs = dscr("Vs", [8, 128, NB * 129])
    dbg_out = {}

    with ExitStack() as es:
        def sb(name, shape, dt):
            return es.enter_context(nc.sbuf_tensor(name, list(shape), dt))

        cb = sb("cb_s", [128, 256], BF); b_cb = Buf("cb")
        cf = sb("cf_s", [128, 4], F32); b_cf = Buf("cf")
        mask = sb("mask_s", [128, 2, 2, 128], BF); b_mask = Buf("mask")
        gsub = sb("gsub", [128, 128], F32); b_gsub = Buf("gsub")
        lams = sb("lams", [128, 4], F32); b_lams = Buf("lams")
        WsT = sb("WsT", [128, 8, 128], BF); b_WsT = Buf("WsT")
        bs2 = sb("bs2", [2, 1024], BF); b_bs2 = Buf("bs2")
        ones1 = sb("ones1", [2, 128], BF); b_ones1 = Buf("ones1")
        kxT = sb("kxT", [128, 8, 256], BF); b_kxT = Buf("kxT")
        vxa = sb("vxa", [128, 2, 4, 257], BF); b_vxa = Buf("vxa")
        NVR = 2
        vrep = [sb(f"vrep{i}", [128, 1024], F32) for i in range(NVR)]
        b_vrep = [Buf(f"vrep{i}") for i in range(NVR)]
        NWP = 4
        wp = [sb(f"wp{i}", [128, 8, 512], BF) for i in range(NWP)]
        b_wp = [Buf(f"wp{i}") for i in range(NWP)]
        hb = [sb(f"h{i}", [128, 4, 1024], F32) for i in range(2)]
        b_h = [Buf(f"h{i}") for i in range(2)]
        nbt = sb("nbt", [128, 1024], BF); b_nbt = Buf("nbt")
        nT = sb("nT", [128, 8, 512], BF); b_nT = Buf("nT")
        T = [sb(f"T{i}", [128, 512], F32) for i in range(3)]
        b_T = [Buf(f"T{i}") for i in range(3)]
        posi = T[0].bitcast(I32); b_posi = b_T[0]
        sinT = sb("sinT", [128, 512], F32); b_sinT = Buf("sinT")
        cosT = sb("cosT", [128, 512], F32); b_cosT = Buf("cosT")
        kraw = sb("kraw", [128, 512], BF); b_kraw = Buf("kraw")
        kraw2 = None
        QT = sb("QT", [128, 8, 512], BF); b_QT = Buf("QT")
        A = [sb(f"A{i}", [128, 8192], BF) for i in range(2)]
        b_A = [Buf(f"A{i}") for i in range(2)]
        b_mT = Buf("mergedT")
        b_Al = [[b_A[0], b_mT], [b_A[1]]]
        B = [sb(f"B{i}", [128, NB * 129], BF) for i in range(2)]
        b_B = [Buf(f"B{i}") for i in range(2)]
        NPT = 3
        PT = [sb(f"PT{i}", [128, 2, 512], BF) for i in range(NPT)]
        b_PT = [Buf(f"PT{i}") for i in range(NPT)]
        atok = sb("atok", [128, 4, 1024], BF); b_atok = Buf("atok")
        vg = sb("vg", [128, 1024], F32); b_vg = Buf("vg")
        vg2 = hb[1][:, 0, :]; b_vg2 = b_h[1]
        bsf = hb[1][0:1, 1, :]; b_bsf = b_h[1]
        bsf2 = hb[1][0:1, 2, :]; b_bsf2 = b_h[1]
        vn = sb("vn", [128, 1024], BF); b_vn = Buf("vn")
        bsh = nbt[0:1, :]; b_bsh = b_nbt
        bsl = vn[0:1, :]; b_bsl = b_vn
        kraw2 = [kraw, vn]; b_kraw2 = [b_kraw, b_vn]
        nbts = [nbt, vn]; b_nbts = [b_nbt, b_vn]
        lamt = hb[0][:, 0, 0:256].rearrange("p (a b) -> p a b", a=4); b_lamt = b_h[0]
        lamp = hb[0][:, 1, 0:128].rearrange("p (a b) -> p a b", a=2); b_lamp = b_h[0]
        sm = sb("sm", [128, 96], F32)
        b_ss = Buf("ss"); b_ms = Buf("ms"); b_rstd = Buf("rstd")
        b_st6 = Buf("st6"); b_mv = Buf("mv"); b_lnr = Buf("lnr")
        b_rden = Buf("rden"); b_lr = Buf("lr"); b_ss2 = Buf("ss2"); b_rs2 = Buf("rs2")
        oraws = [sb(f"oraw{i}", [128, 258], F32) for i in range(2)]
        b_oraws = [Buf(f"oraw{i}") for i in range(2)]
        o32s = [[sb(f"o32_{a_}_{q_}", [128, 128], F32) for q_ in range(4)] for a_ in range(2)]
        b_o32s = [[Buf(f"o32_{a_}_{q_}") for q_ in range(4)] for a_ in range(2)]
        t32 = sb("t32", [128, 128], F32); b_t32 = Buf("t32")
        SS = sm[:, 0:4]; MS = sm[:, 4:8]; RSTD = sm[:, 8:12]
        ST6 = sm[:, 12:24]; MV = sm[:, 24:26]; LNV = sm[:, 26:27]; LNR = sm[:, 27:28]
        RDEN = sm[:, 28:30]; LR = sm[:, 30:31]; SS2 = sm[:, 31:32]; MS2 = sm[:, 32:33]; RS2 = sm[:, 33:34]
        RDX = sm[:, 34:35]
        SS4 = [sm[:, 36:40], sm[:, 40:44]]; MS4 = [sm[:, 44:48], sm[:, 48:52]]; RS4 = [sm[:, 52:56], sm[:, 56:60]]
        b_ss4 = [Buf("ss4a"), Buf("ss4b")]; b_ms4 = [Buf("ms4a"), Buf("ms4b")]; b_rs4 = [Buf("rs4a"), Buf("rs4b")]
        b_rdx = Buf("rdx"); b_ms2 = Buf("ms2"); b_lnv = Buf("lnv")
        SSp = [sm[:, 0:4], sm[:, 64:68]]; MSp = [sm[:, 4:8], sm[:, 68:72]]; RSp = [sm[:, 8:12], sm[:, 72:76]]
        b_ssb = [[Buf(f"ss{p_}_{k_}") for k_ in range(4)] for p_ in range(2)]
        b_msb = [[Buf(f"ms{p_}_{k_}") for k_ in range(4)] for p_ in range(2)]
        b_rsb = [[Buf(f"rs{p_}_{k_}") for k_ in range(4)] for p_ in range(2)]

        aT = A[0][:, 0:4096].rearrange("p (k t) -> p k t", t=512)
        mergedT = A[0][:, 4096:8192].rearrange("p (k t) -> p k t", t=512)
        hidT = A[1][:, :].rearrange("p (k t) -> p k t", t=512)
        gatesT = B[0][:, 0:8192].rearrange("p (k t) -> p k t", t=512)
        uT = B[1][:, 0:4096].rearrange("p (k t) -> p k t", t=512)
        sgoT = B[1][:, 4096:8192].rearrange("p (k t) -> p k t", t=512)
        KTo = A[0][:, 0:4096].rearrange("p (k t) -> p k t", t=512)
        Vaug = B[0][:, 0:8 * 4 * 129].rearrange("p (h b e) -> p h b e", h=8, b=4)

        psA = es.enter_context(nc.psum_tensor("psA", [128, 4, 512], F32))
        psB = es.enter_context(nc.psum_tensor("psB", [128, 4, 512], F32))
        b_psA = [Buf(f"psA{i}", excl=True) for i in range(4)]
        b_psB = [Buf(f"psB{i}", excl=True) for i in range(4)]
        psBb = psB.bitcast(BF)

        ident = cb[:, 0:128]
        Pm = cb[:, 128:256]
        INVF = cf[:, 0:1]
        HALFPI = cf[:, 1:2]
        NEGHALF = cf[:, 2:3]

        b_Kt = [Buf(f"Kt{i}") for i in range(NSG)]
        b_Vs = [Buf(f"Vs{i}") for i in range(NSG)]
        b_y = Buf("y")

        cast_bufs = {}

        cast_order = []
        cast_dep = []

        def cast(name, src, dst, r0, r1, c0, c1):
            b = Buf("cast_" + name)
            cast_bufs[name] = b
            dep = list(cast_dep)
            cast_order.append(b)
            P.op("pool", lambda e: e.dma_start(out=dst[r0:r1, c0:c1], in_=src[r0:r1, c0:c1]),
                 reads=dep, writes=[b], dma=True, sem="c_" + name)

        cast_list = [("xkv0", w_xkv, wb_xkv, 0, D, 0, 1024), ("xkv1", w_xkv, wb_xkv, 0, D, 1024, 2048)]
        for cblk in (0, 5, 6, 3, 4):
            cast_list.append((f"in{cblk}", w_in, wb_in, 0, D, cblk * 1024, (cblk + 1) * 1024))
        cast_list += [("ba0", w_ba, wb_ba, 0, D, 0, D), ("bs0", w_bs, wb_bs, 0, D, 0, D),
                      ("out0", w_out, wb_out, 0, D, 0, D), ("xq0", w_xq, wb_xq, 0, D, 0, D),
                      ("xo0", w_xo, wb_xo, 0, D, 0, D)]
        for i in range(4):
            cast_list.append((f"ff1{i}", w_ff1, wb_ff1, 0, D, i * 1024, (i + 1) * 1024))
        for i in range(4):
            cast_list.append((f"ff2{i}", w_ff2, wb_ff2, i * 1024, (i + 1) * 1024, 0, D))

        def emit_casts(n):
            for _ in range(n):
                if cast_list:
                    cast(*cast_list.pop(0))

        st = {"wp": 0, "mm": 0, "vr": 0, "pt": 0, "tp": 0}

        def load_panel(wb, castname, kg, cg):
            s = st["wp"] % NWP
            st["wp"] += 1
            src = wb[kg * 1024:(kg + 1) * 1024, cg * 512:(cg + 1) * 512].rearrange("(kc p) c -> p kc c", p=128)
            P.op("sp", lambda e: e.dma_start(out=wp[s][:], in_=src),
                 reads=[cast_bufs[castname]], writes=[b_wp[s]], dma=True, sem=f"wp{s}")
            return s

        def mmbank():
            i = st["mm"] % 4
            st["mm"] += 1
            return i

        def load_vrep(row):
            i = st["vr"] % NVR
            st["vr"] += 1
            src = bass.AP(vecs.tensor, row * D, [[0, 128], [1, D]])
            P.op("sp", lambda e: e.dma_start(out=vrep[i][:], in_=src), writes=[b_vrep[i]], dma=True, sem=f"vr{i}")
            return i

        def pool_pow(out_ap, in_ap, rbufs, wbufs):
            P.op("act", lambda e: e.activation(out=out_ap, in_=in_ap, func=AF.Ln),
                 reads=list(rbufs), writes=list(wbufs))
            P.op("act", lambda e: e.activation(out=out_ap, in_=out_ap, func=AF.Exp, scale=-0.5),
                 reads=list(wbufs), writes=list(wbufs))

        def norm_sq(h, bh, blk, par, junk=None, bjunk=None):
            jk = nbt[:] if junk is None else junk
            bjk = b_nbt if bjunk is None else bjunk
            P.op("act", lambda e: e.activation(out=jk, in_=h[:, blk, :], func=AF.Square,
                                               accum_out=SSp[par][:, blk:blk + 1]),
                 reads=[bh], writes=[bjk, b_ssb[par][blk]])

        def norm_pre(h, bh, vi, par, blk):
            nb_ = nbts[blk % 2]
            P.op("dve", lambda e: e.scalar_tensor_tensor(out=nb_[:], in0=h[:, blk, :],
                                                         scalar=RSp[par][:, blk:blk + 1], in1=vrep[vi][:],
                                                         op0=ALU.mult, op1=ALU.mult),
                 reads=[bh, b_rsb[par][blk], b_vrep[vi]], writes=[b_nbts[blk % 2]])

        def norm_finish(par, nblk=4):
            P.op("dve", lambda e: e.tensor_scalar(out=MSp[par][:, 0:nblk], in0=SSp[par][:, 0:nblk], scalar1=1.0 / D,
                                                  scalar2=EPS, op0=ALU.mult, op1=ALU.add),
                 reads=b_ssb[par][0:nblk], writes=b_msb[par][0:nblk])
            pool_pow(RSp[par][:, 0:nblk], MSp[par][:, 0:nblk], b_msb[par][0:nblk], b_rsb[par][0:nblk])

        def norm_finish_blk(par, blk):
            P.op("dve", lambda e: e.tensor_scalar(out=MSp[par][:, blk:blk + 1], in0=SSp[par][:, blk:blk + 1],
                                                  scalar1=1.0 / D, scalar2=EPS, op0=ALU.mult, op1=ALU.add),
                 reads=[b_ssb[par][blk]], writes=[b_msb[par][blk]])
            pool_pow(RSp[par][:, blk:blk + 1], MSp[par][:, blk:blk + 1], [b_msb[par][blk]], [b_rsb[par][blk]])

        def norm_stats(h, bh, par, nblk=4):
            for blk in range(nblk):
                norm_sq(h, bh, blk, par)
            norm_finish(par, nblk)

        def norm_apply(h, bh, vrow, par, nblk=4, dst=None, bdst=None, vi_fixed=None, pre_done=0):
            dst = nT if dst is None else dst
            bdst = b_nT if bdst is None else bdst
            vi = load_vrep(vrow) if vi_fixed is None else vi_fixed
            for blk in range(nblk):
                nb_ = nbts[blk % 2]
                bnb_ = b_nbts[blk % 2]
                if blk >= pre_done:
                    norm_pre(h, bh, vi, par, blk)
                tb = st["tp"] % 2
                st["tp"] += 1
                for kc in range(8):
                    P.op("pe", lambda e, kc=kc, tb=tb, nb_=nb_: e.transpose(psBb[:, tb, kc * 128:(kc + 1) * 128],
                                                                  nb_[:, kc * 128:(kc + 1) * 128], ident),
                         reads=[bnb_, b_cb], writes=[b_psB[tb]])
                P.op("act", lambda e, blk=blk, tb=tb: e.activation(
                    out=dst[:, :, blk * 128:(blk + 1) * 128],
                    in_=psBb[:, tb, :].rearrange("p (k t) -> p k t", t=128), func=AF.Copy),
                     reads=[b_psB[tb]], writes=[bdst])

        def rmsnorm_to_nT(h, bh, vrow, nblk=4, dst=None, bdst=None, par=0):
            norm_stats(h, bh, par, nblk)
            norm_apply(h, bh, vrow, par, nblk, dst, bdst)

        def transpose_tok_to_feat(src, bsrc, dst, bdst, nblk=4):
            for blk in range(nblk):
                tb = st["tp"] % 2
                st["tp"] += 1
                for kc in range(8):
                    P.op("pe", lambda e, kc=kc, tb=tb, blk=blk: e.transpose(
                        psBb[:, tb, kc * 128:(kc + 1) * 128], src[:, blk, kc * 128:(kc + 1) * 128], ident),
                         reads=[bsrc, b_cb], writes=[b_psB[tb]])
                P.op("act", lambda e, blk=blk, tb=tb: e.activation(
                    out=dst[:, :, blk * 128:(blk + 1) * 128],
                    in_=psBb[:, tb, :].rearrange("p (k t) -> p k t", t=128), func=AF.Copy),
                     reads=[b_psB[tb]], writes=[bdst])

        def rope_tables(pos_dram, off):
            src = bass.AP(pos_dram.tensor, off, [[0, 128], [1, 512]])
            P.op("sp", lambda e: e.dma_start(out=posi[:], in_=src), writes=[b_posi], dma=True, sem="pos")
            P.op("dve", lambda e: e.tensor_copy(T[0][:], posi[:]), reads=[b_posi], writes=[b_T[0]])
            P.op("dve", lambda e: e.tensor_scalar(out=T[1][:], in0=T[0][:], scalar1=INVF, scalar2=None, op0=ALU.mult),
                 reads=[b_T[0], b_cf], writes=[b_T[1]])
            P.op("dve", lambda e: e.tensor_scalar(out=T[2][:], in0=T[1][:], scalar1=1.0 / (2.0 * math.pi), scalar2=MAGIC,
                                                  op0=ALU.mult, op1=ALU.add), reads=[b_T[1]], writes=[b_T[2]])
            P.op("dve", lambda e: e.tensor_scalar(out=T[0][:], in0=T[2][:], scalar1=-MAGIC, scalar2=None, op0=ALU.add),
                 reads=[b_T[2]], writes=[b_T[0]])
            P.op("dve", lambda e: e.scalar_tensor_tensor(out=T[2][:], in0=T[0][:], scalar=-TWO_PI_HI, in1=T[1][:],
                                                         op0=ALU.mult, op1=ALU.add),
                 reads=[b_T[0], b_T[1]], writes=[b_T[2]])
            P.op("dve", lambda e: e.scalar_tensor_tensor(out=T[1][:], in0=T[0][:], scalar=-TWO_PI_LO, in1=T[2][:],
                                                         op0=ALU.mult, op1=ALU.add),
                 reads=[b_T[0], b_T[2]], writes=[b_T[1]])
            P.op("dve", lambda e: e.tensor_scalar(out=T[2][:], in0=T[1][:], scalar1=PI_SAFE, scalar2=-PI_SAFE,
                                                  op0=ALU.min, op1=ALU.max), reads=[b_T[1]], writes=[b_T[2]])
            P.op("act", lambda e: e.activation(out=sinT[:], in_=T[2][:], func=AF.Sin), reads=[b_T[2]], writes=[b_sinT])
            P.op("dve", lambda e: e.scalar_tensor_tensor(out=T[0][:], in0=T[2][:], scalar=-1.0, in1=T[2][:],
                                                         op0=ALU.mult, op1=ALU.max),
                 reads=[b_T[2]], writes=[b_T[0]])
            P.op("act", lambda e: e.activation(out=cosT[:], in_=T[0][:], func=AF.Sin, scale=-1.0, bias=HALFPI),
                 reads=[b_T[0], b_cf], writes=[b_cosT])

        def proj_rope(panel_of_ct, dst, bdst, mid_hook=None):
            def stage_a(ct):
                s, ctl = panel_of_ct(ct)
                mb = mmbank()
                for kc in range(8):
                    P.op("pe", lambda e, s=s, ctl=ctl, kc=kc, mb=mb: e.matmul(
                        psA[:, mb, :], wp[s][:, kc, ctl * 128:(ctl + 1) * 128], nT[:, kc, :],
                        start=(kc == 0), stop=(kc == 7)),
                         reads=[b_wp[s], b_nT], writes=[b_psA[mb]])
                kb_ = ct % 2
                P.op("act", lambda e, mb=mb, kb_=kb_: e.activation(out=kraw2[kb_][:, 0:512], in_=psA[:, mb, :], func=AF.Copy),
                     reads=[b_psA[mb]], writes=[b_kraw2[kb_]])
                return mb

            def stage_b(ct, mb):
                kb_ = ct % 2
                mb2 = mmbank()
                P.op("pe", lambda e, mb2=mb2, kb_=kb_: e.matmul(psA[:, mb2, :], Pm, kraw2[kb_][:, 0:512], start=True, stop=True),
                     reads=[b_kraw2[kb_], b_cb], writes=[b_psA[mb2]])
                P.op("dve", lambda e, mb=mb: e.tensor_tensor(out=T[1][:], in0=psA[:, mb, :], in1=cosT[:], op=ALU.mult),
                     reads=[b_psA[mb], b_cosT], writes=[b_T[1]])
                P.op("dve", lambda e, mb2=mb2: e.tensor_tensor(out=T[2][:], in0=psA[:, mb2, :], in1=sinT[:], op=ALU.mult),
                     reads=[b_psA[mb2], b_sinT], writes=[b_T[2]])
                P.op("dve", lambda e, ct=ct: e.tensor_tensor(out=dst[:, ct, :], in0=T[1][:], in1=T[2][:], op=ALU.add),
                     reads=[b_T[1], b_T[2]], writes=[bdst])

            prev = None
            for ct in range(8):
                mb = stage_a(ct)
                if prev is not None:
                    stage_b(*prev)
                prev = (ct, mb)
                if ct == 4 and mid_hook is not None:
                    mid_hook()
            stage_b(*prev)

        P.op("sp", lambda e: e.dma_start(out=cb[:], in_=cbd), writes=[b_cb], dma=True, sem="cst")
        P.op("sp", lambda e: e.dma_start(out=cf[:], in_=cfd), writes=[b_cf], dma=True, sem="cst")
        P.op("sp", lambda e: e.dma_start(out=mask[:].rearrange("p a b c -> p (a b c)"), in_=maskd),
             writes=[b_mask], dma=True, sem="cst")
        P.op("sp", lambda e: e.dma_start(out=gsub[:], in_=bass.AP(gsubd.tensor, 0, [[0, 128], [1, 128]])),
             writes=[b_gsub], dma=True, sem="cst")
        P.op("sp", lambda e: e.dma_start(out=hb[0][:, 0, 0:256],
                                         in_=bass.AP(lamd.tensor, 0, [[0, 128], [1, 256]])),
             writes=[b_lamt], dma=True, sem="cst")
        P.op("sp", lambda e: e.dma_start(out=bsf, in_=bass.AP(sgbd.tensor, 0, [[0, 1], [1, 1024]])),
             writes=[b_bsf], dma=True, sem="cst")
        P.op("sp", lambda e: e.dma_start(out=vg2.rearrange("p (g j) -> p g j", g=8),
                                         in_=sgwd.rearrange("g i j -> i g j")),
             writes=[b_vg2], dma=True, sem="cst")
        P.op("dve", lambda e: e.tensor_scalar(out=gsub[:], in0=gsub[:], scalar1=0.8, scalar2=None, op0=ALU.mult),
             reads=[b_gsub], writes=[b_gsub])
        P.op("dve", lambda e: e.tensor_tensor(out=lamp[:, 0, :], in0=lamt[:, 0, :], in1=lamt[:, 1, :], op=ALU.mult),
             reads=[b_lamt], writes=[b_lamp])
        P.op("dve", lambda e: e.tensor_tensor(out=lamp[:, 1, :], in0=lamt[:, 2, :], in1=lamt[:, 3, :], op=ALU.mult),
             reads=[b_lamt, b_lamp], writes=[b_lamp])
        P.op("dve", lambda e: e.tensor_reduce(out=lams[:, 0:2], in_=lamp, axis=AX.X, op=ALU.add),
             reads=[b_lamp], writes=[b_lams])
        P.op("act", lambda e: e.activation(out=lams[:, 2:4], in_=lams[:, 0:2], func=AF.Exp),
             reads=[b_lams], writes=[b_lams])
        P.op("dve", lambda e: e.tensor_tensor(out=lams[:, 0:1], in0=lams[:, 2:3], in1=lams[:, 3:4], op=ALU.subtract),
             reads=[b_lams], writes=[b_lams])
        P.op("dve", lambda e: e.tensor_scalar(out=lams[:, 3:4], in0=lams[:, 0:1], scalar1=0.2, scalar2=None, op0=ALU.add),
             reads=[b_lams], writes=[b_lams])
        LAM = lams[:, 3:4]
        P.op("dve", lambda e: e.memset(ones1[:], 1.0), writes=[b_ones1])
        P.op("dve", lambda e: e.memset(vxa[:, :, :, 256:257], 1.0), writes=[b_vxa])
        P.op("dve", lambda e: e.tensor_copy(bsh, bsf), reads=[b_bsf], writes=[b_bsh])
        P.op("dve", lambda e: e.tensor_copy(bsf2, bsh), reads=[b_bsh], writes=[b_bsf2])
        P.op("dve", lambda e: e.tensor_tensor(out=bsf2, in0=bsf, in1=bsf2, op=ALU.subtract),
             reads=[b_bsf, b_bsf2], writes=[b_bsf2])
        P.op("dve", lambda e: e.tensor_copy(bsl, bsf2), reads=[b_bsf2], writes=[b_bsl])
        P.op("sp", lambda e: [e.dma_start(out=bs2[0:1, :], in_=bsh), e.dma_start(out=bs2[1:2, :], in_=bsl)],
             reads=[b_bsh, b_bsl], writes=[b_bs2], dma=True, sem="cst", n_dma=2)
        vg2_3 = vg2.rearrange("p (g j) -> p g j", g=8)
        P.op("dve", lambda e: e.memset(vg2_3[0:64, :, 64:128], 0.0), reads=[b_vg2], writes=[b_vg2])
        P.op("dve", lambda e: e.tensor_copy(vn[:], vg2), reads=[b_vg2], writes=[b_vn])
        for g in range(8):
            P.op("pe", lambda e, g=g: e.transpose(psBb[:, 0, g * 128:(g + 1) * 128], vn[:, g * 128:(g + 1) * 128], ident),
                 reads=[b_vn, b_cb], writes=[b_psB[0]])
        P.op("dve", lambda e: e.tensor_copy(WsT[:].rearrange("p g i -> p (g i)"), psBb[:, 0, :]),
             reads=[b_psB[0]], writes=[b_WsT])

        P.op("sp", lambda e: e.dma_start(
            out=hb[0][:], in_=xf[0:512, :].rearrange("(b p) d -> p b d", p=128)),
             writes=[b_h[0]], dma=True, sem="x0")
        stg = [A[1].bitcast(F32)[:, 0:4096], B[1].bitcast(F32)[:, 0:4096]]
        b_stg = [b_A[1], b_B[1]]
        p1slots = {}
        for i_, cg_ in enumerate((4, 5, 2, 3)):
            sl = st["wp"] % NWP
            st["wp"] += 1
            p1slots[cg_] = sl
            sg_ = stg[i_ % 2]
            P.op("sp", lambda e, cg_=cg_, sg_=sg_: e.dma_start(
                out=sg_.rearrange("p (k c) -> p k c", c=512),
                in_=w_in[:, cg_ * 512:(cg_ + 1) * 512].rearrange("(kc p) c -> p kc c", p=128)),
                 writes=[b_stg[i_ % 2]], dma=True, sem=f"stg{i_ % 2}")
            if i_ % 2 == 0:
                P.op("dve", lambda e, sl=sl, sg_=sg_: e.tensor_copy(wp[sl][:].rearrange("p k c -> p (k c)"), sg_),
                     reads=[b_stg[i_ % 2]], writes=[b_wp[sl]])
            else:
                P.op("act", lambda e, sl=sl, sg_=sg_: e.activation(out=wp[sl][:].rearrange("p k c -> p (k c)"), in_=sg_,
                                                                  func=AF.Copy),
                     reads=[b_stg[i_ % 2]], writes=[b_wp[sl]])
        pk = [p1slots[2], p1slots[3]]
        pv = [p1slots[4], p1slots[5]]
        vi_p1 = load_vrep(0)
        b_tick = Buf("tick")
        P.op("dve", lambda e: e.memset(Vaug[:, :, :, 128:129], 1.0), writes=[b_B[0]])
        for sg in range(n_sgroups):
            hh = sg % 2
            h = hb[hh]
            if sg + 1 < n_sgroups:
                hn = hb[1 - hh]
                P.op("sp", lambda e, sg=sg, hn=hn: e.dma_start(
                    out=hn[:], in_=xf[(sg + 1) * 512:(sg + 2) * 512, :].rearrange("(b p) d -> p b d", p=128)),
                     writes=[b_h[1 - hh]], dma=True, sem=f"x{1 - hh}")
            P.mark("p1_xload")
            if sg == 0:
                norm_stats(h, b_h[hh], 0)
            norm_apply(h, b_h[hh], 0, sg % 2, vi_fixed=vi_p1, pre_done=(1 if sg > 0 else 0))
            P.mark("p1_norm")
            rope_tables(posf, sg * 512)
            P.mark("p1_rope")
            for blk in range(4):
                for cg in range(2):
                    mb = mmbank()
                    s = pv[cg]
                    for kc in range(8):
                        P.op("pe", lambda e, s=s, kc=kc, mb=mb, blk=blk: e.matmul(
                            psA[:, mb, :], nT[:, kc, blk * 128:(blk + 1) * 128], wp[s][:, kc, :],
                            start=(kc == 0), stop=(kc == 7)),
                             reads=[b_wp[s], b_nT], writes=[b_psA[mb]])
                    tick = Buf(f"tick{sg}")
                    P.op("act", lambda e, mb=mb, blk=blk, cg=cg: e.activation(
                        out=Vaug[:, cg * 4:(cg + 1) * 4, blk, 0:128],
                        in_=psA[:, mb, :].rearrange("p (h e) -> p h e", e=128), func=AF.Copy),
                         reads=[b_psA[mb]], writes=[b_B[0], tick])
            P.mark("p1_vproj")
            del cast_dep[:]
            cast_dep.append(tick)
            emit_casts(2 if (sg < 4 or n_sgroups < 16) else 1)
            P.op("act", lambda e, sg=sg: e.dma_start(
                out=Vs[:, :, sg * 4 * 129:(sg + 1) * 4 * 129].rearrange("h p e -> p h e"),
                in_=Vaug.rearrange("p h b e -> p h (b e)")),
                 reads=[b_B[0]], writes=[b_Vs[sg]], dma=True, sem="vst")
            P.mark("p1_vst")
            hook = None
            if sg + 1 < n_sgroups:
                hn_, bhn_, pn_ = hb[1 - hh], b_h[1 - hh], (sg + 1) % 2
                norm_sq(hn_, bhn_, 0, pn_)
                norm_sq(hn_, bhn_, 1, pn_)

                def hook(hn_=hn_, bhn_=bhn_, pn_=pn_):
                    norm_sq(hn_, bhn_, 2, pn_)
                    norm_sq(hn_, bhn_, 3, pn_)
                    norm_finish(pn_)
                    norm_pre(hn_, bhn_, vi_p1, pn_, 0)
            proj_rope(lambda ct: (pk[ct // 4], ct % 4), KTo, b_A[0], mid_hook=hook)
            P.mark("p1_proj")
            P.op("act", lambda e, sg=sg: e.dma_start(
                out=Kt[:, sg * 512:(sg + 1) * 512].rearrange("(c p) t -> p c t", p=128), in_=KTo),
                 reads=[b_A[0]], writes=[b_Kt[sg]], dma=True, sem="kst")
            P.mark("p1_kst")

        del cast_dep[:]
        emit_casts(len(cast_list))
        if n_groups > 0:
            P.op("sp", lambda e: e.dma_start(out=hb[0][:, 0:2, :], in_=memd.rearrange("(b p) d -> p b d", p=128)),
                 writes=[b_h[0]], dma=True, sem="x0")
            rmsnorm_to_nT(hb[0], b_h[0], 4, nblk=2)
            for cg in range(2):
                s = load_panel(wb_xkv, "xkv0", 0, cg)
                for ctl in range(4):
                    ct = cg * 4 + ctl
                    mb = mmbank()
                    for kc in range(8):
                        P.op("pe", lambda e, s=s, ctl=ctl, kc=kc, mb=mb: e.matmul(
                            psA[:, mb, 0:256], wp[s][:, kc, ctl * 128:(ctl + 1) * 128], nT[:, kc, 0:256],
                            start=(kc == 0), stop=(kc == 7)),
                             reads=[b_wp[s], b_nT], writes=[b_psA[mb]])
                    P.op("act", lambda e, mb=mb, ct=ct: e.activation(out=kxT[:, ct, :], in_=psA[:, mb, 0:256], func=AF.Copy),
                         reads=[b_psA[mb]], writes=[b_kxT])
            for cg in range(2):
                s = load_panel(wb_xkv, "xkv1", 0, 2 + cg)
                for mbk in range(2):
                    mb = mmbank()
                    for kc in range(8):
                        P.op("pe", lambda e, s=s, kc=kc, mb=mb, mbk=mbk: e.matmul(
                            psA[:, mb, :], nT[:, kc, mbk * 128:(mbk + 1) * 128], wp[s][:, kc, :],
                            start=(kc == 0), stop=(kc == 7)),
                             reads=[b_wp[s], b_nT], writes=[b_psA[mb]])
                    P.op("act", lambda e, mb=mb, mbk=mbk, cg=cg: e.activation(
                        out=vxa[:, mbk, 2 * cg:2 * cg + 2, 0:256],
                        in_=psA[:, mb, :].rearrange("p (h e) -> p h e", e=256), func=AF.Copy),
                         reads=[b_psA[mb]], writes=[b_vxa])

        hstate = {"hcount": 0, "deferred": None, "kv0_loaded": False, "epilogue": None}

        def group_body(og):
            hh = og % 2
            h = hb[hh]
            bh = b_h[hh]
            if og == 0:
                P.op("sp", lambda e: e.dma_start(
                    out=h[:], in_=xo[0:512, :].rearrange("(b p) d -> p b d", p=128)),
                     writes=[bh], dma=True, sem=f"x{hh}")
            if og == 0:
                norm_stats(h, bh, 1)
                rope_tables(poso, 0)
            norm_apply(h, bh, 0, 1)
            pq = [load_panel(wb_in, "in0", 0, 0), load_panel(wb_in, "in0", 0, 1)]
            proj_rope(lambda ct: (pq[ct // 4], ct % 4), QT, b_QT)
            if hstate["epilogue"] is not None:
                hstate["epilogue"]()
                hstate["epilogue"] = None

            nk = 8 * (og + 1)

            def load_kv(hd, nk_):
                ab = hd % 2
                need = (nk_ * 128 + 511) // 512
                P.op("sp", lambda e: e.dma_start(
                    out=A[ab][:, 0:nk_ * 128], in_=Kt[hd * 128:(hd + 1) * 128, 0:nk_ * 128]),
                     reads=b_Kt[:need], writes=list(b_Al[ab]), dma=True, sem=f"A{ab}")
                P.op("sp", lambda e: e.dma_start(
                    out=B[ab][:, 0:nk_ * 129], in_=Vs[hd, :, 0:nk_ * 129]),
                     reads=b_Vs[:need], writes=[b_B[ab]], dma=True, sem=f"B{ab}")

            fns = []
            if not hstate["kv0_loaded"]:
                load_kv(0, nk)
            hstate["kv0_loaded"] = False
            for hd in range(8):
                ab = hd % 2
                hp_ = ab

                def qk_step(kb, hd=hd, ab=ab):
                    imin = max(4 * og, (kb) // 2)
                    q0 = imin - 4 * og
                    c0 = q0 * 128
                    sbk = kb % 2
                    for m in range(2):
                        P.op("pe", lambda e, m=m, sbk=sbk, c0=c0: e.matmul(
                            psA[:, 2 * sbk + m, c0:512], A[ab][m * 64:(m + 1) * 64, kb * 128:(kb + 1) * 128],
                            QT[m * 64:(m + 1) * 64, hd, c0:512], start=True, stop=True),
                             reads=b_Al[ab] + [b_QT], writes=[b_psA[2 * sbk + m]])
                    pb = st["pt"] % NPT
                    st["pt"] += 1
                    P.op("act", lambda e, sbk=sbk, pb=pb, c0=c0: e.activation(
                        out=PT[pb][:, :, c0:512], in_=psA[:, 2 * sbk:2 * sbk + 2, c0:512], func=AF.Exp, scale=0.125),
                         reads=[b_psA[2 * sbk], b_psA[2 * sbk + 1]], writes=[b_PT[pb]])
                    if kb >= 8 * og:
                        i = kb // 2
                        qi = i - 4 * og
                        mk0 = mask[:, i % 2, kb % 2, :]
                        mkb = bass.AP(mk0.tensor, mk0.offset, [list(mk0.ap[0]), [0, 2], [1, 128]])
                        P.op("pool", lambda e, pb=pb, qi=qi, mkb=mkb: e.tensor_tensor(
                            out=PT[pb][:, :, qi * 128:(qi + 1) * 128], in0=PT[pb][:, :, qi * 128:(qi + 1) * 128],
                            in1=mkb, op=ALU.mult),
                             reads=[b_PT[pb], b_mask], writes=[b_PT[pb]])
                    return pb, q0

                hp = hp_

                def pv_step(kb, pb, q0, hd=hd, ab=ab, hp=hp):
                    for qi in range(q0, 4):
                        i = 4 * og + qi
                        last = 2 * i + 1
                        for m in range(2):
                            P.op("pe", lambda e, qi=qi, m=m, last=last: e.matmul(
                                psB[:, qi, m * 129:(m + 1) * 129], PT[pb][:, m, qi * 128:(qi + 1) * 128],
                                B[ab][:, kb * 129:(kb + 1) * 129], start=(kb == 0 and m == 0),
                                stop=(kb == last and m == 1), skip_group_check=True),
                                 reads=[b_PT[pb], b_B[ab]], writes=[b_psB[qi]])
                        if kb == last:
                            oq = o32s[hp][qi]
                            boq = b_o32s[hp][qi]
                            oraw = oraws[qi % 2]
                            b_oraw = b_oraws[qi % 2]
                            P.op("dve", lambda e, qi=qi, oraw=oraw: e.tensor_copy(oraw[:], psB[:, qi, 0:258]),
                                 reads=[b_psB[qi]], writes=[b_oraw])
                            Ov = oraw[:].rearrange("p (m e) -> p m e", e=129)
                            P.op("dve", lambda e, Ov=Ov: e.reciprocal(RDEN.rearrange("p (m o) -> p m o", o=1), Ov[:, :, 128:129]),
                                 reads=[b_oraw], writes=[b_rden])
                            P.op("dve", lambda e: e.tensor_tensor(out=LR, in0=RDEN[:, 1:2], in1=LAM, op=ALU.mult),
                                 reads=[b_rden, b_lams], writes=[b_lr])
                            P.op("dve", lambda e, oraw=oraw: e.tensor_scalar(out=t32[:], in0=oraw[:, 129:257], scalar1=LR,
                                                                  scalar2=None, op0=ALU.mult),
                                 reads=[b_oraw, b_lr], writes=[b_t32])
                            P.op("dve", lambda e, oq=oq, oraw=oraw: e.scalar_tensor_tensor(
                                out=oq[:], in0=oraw[:, 0:128], scalar=RDEN[:, 0:1], in1=t32[:],
                                op0=ALU.mult, op1=ALU.subtract),
                                 reads=[b_oraw, b_rden, b_t32], writes=[boq])
                            P.op("dve", lambda e, oq=oq: e.tensor_tensor(out=t32[:], in0=oq[:], in1=oq[:], op=ALU.mult),
                                 reads=[boq], writes=[b_t32])
                            P.op("dve", lambda e, qi=qi, hp=hp: e.tensor_reduce(out=SS4[hp][:, qi:qi + 1], in_=t32[:], axis=AX.X, op=ALU.add),
                                 reads=[b_t32], writes=[b_ss4[hp]])

                def finalize2(hd=hd, hp=hp):
                    P.op("dve", lambda e: e.tensor_scalar(out=MS4[hp], in0=SS4[hp], scalar1=1.0 / 128, scalar2=EPS,
                                                          op0=ALU.mult, op1=ALU.add), reads=[b_ss4[hp]], writes=[b_ms4[hp]])
                    pool_pow(RS4[hp], MS4[hp], [b_ms4[hp]], [b_rs4[hp]])
                    for qi in range(4):
                        P.op("dve", lambda e, qi=qi: e.scalar_tensor_tensor(
                            out=atok[:, qi, hd * 128:(hd + 1) * 128], in0=o32s[hp][qi][:], scalar=RS4[hp][:, qi:qi + 1],
                            in1=gsub[:], op0=ALU.mult, op1=ALU.mult),
                             reads=[b_o32s[hp][qi], b_rs4[hp], b_gsub], writes=[b_atok])

                fns.append((qk_step, pv_step, finalize2))

            seq = [(hd_, kb_) for hd_ in range(8) for kb_ in range(nk)]
            pend = {}
            LAG = 2
            deferred = []
            for idx in range(len(seq) + LAG):
                if idx < len(seq):
                    hd_, kb_ = seq[idx]
                    if idx == 0:
                        load_kv(1, nk)
                    pend[idx] = fns[hd_][0](kb_)
                j = idx - LAG
                if j >= 0:
                    hd_, kb_ = seq[j]
                    fns[hd_][1](kb_, *pend.pop(j))
                    if kb_ == nk - 1:
                        deferred.append((idx + 2, fns[hd_][2]))
                        if hd_ + 2 < 8:
                            load_kv(hd_ + 2, nk)
                while deferred and deferred[0][0] <= idx:
                    deferred.pop(0)[1]()
            while deferred:
                deferred.pop(0)[1]()

            if dbg and og == 0:
                dbg_out["atok"] = nc.dram_tensor("dbg_atok", [128, 4 * 1024], BF, kind="ExternalOutput").ap()
                P.op("pool", lambda e: e.dma_start(out=dbg_out["atok"], in_=atok[:].rearrange("p a b -> p (a b)")),
                     reads=[b_atok], writes=[b_y], dma=True, sem="yout")
                dbg_out["QT"] = nc.dram_tensor("dbg_QT", [128, 8 * 512], BF, kind="ExternalOutput").ap()
                P.op("pool", lambda e: e.dma_start(out=dbg_out["QT"], in_=QT[:].rearrange("p a b -> p (a b)")),
                     reads=[b_QT], writes=[b_y], dma=True, sem="yout")

            for cg in range(4):
                s = load_panel(wb_in, "in5" if cg < 2 else "in6", 0, 10 + cg)
                for ctl in range(4):
                    ct = cg * 4 + ctl
                    mb = mmbank()
                    for kc in range(8):
                        P.op("pe", lambda e, s=s, ctl=ctl, kc=kc, mb=mb: e.matmul(
                            psA[:, mb, :], wp[s][:, kc, ctl * 128:(ctl + 1) * 128], nT[:, kc, :],
                            start=(kc == 0), stop=(kc == 7)),
                             reads=[b_wp[s], b_nT], writes=[b_psA[mb]])
                    P.op("act", lambda e, mb=mb, ct=ct: e.activation(out=gatesT[:, ct, :], in_=psA[:, mb, :],
                                                                     func=AF.Sigmoid),
                         reads=[b_psA[mb]], writes=[b_B[0]])
            transpose_tok_to_feat(atok, b_atok, aT, b_A[0])

            for cg in range(2):
                s = load_panel(wb_in, "in3", 0, 6 + cg)
                for ctl in range(4):
                    ct = cg * 4 + ctl
                    mb = mmbank()
                    for kc in range(8):
                        P.op("pe", lambda e, s=s, ctl=ctl, kc=kc, mb=mb: e.matmul(
                            psA[:, mb, :], wp[s][:, kc, ctl * 128:(ctl + 1) * 128], nT[:, kc, :],
                            start=(kc == 0), stop=(kc == 7)),
                             reads=[b_wp[s], b_nT], writes=[b_psA[mb]])
                    P.op("act", lambda e, mb=mb, ct=ct: e.activation(out=uT[:, ct, :], in_=psA[:, mb, :],
                                                                     func=AF.Gelu_apprx_tanh),
                         reads=[b_psA[mb]], writes=[b_B[1]])
            pvv = [load_panel(wb_in, "in4", 0, 8), load_panel(wb_in, "in4", 0, 9)]
            vi_g = load_vrep(5)
            vi_b = load_vrep(6)
            SG3 = psB[:, 2:4, :].rearrange("p a (g i) -> p (a g) i", i=128)
            vgb = [vg[:], hb[1 - hh][:, 0, :], QT.bitcast(F32)[:, 0:4, :].rearrange("p a b -> p (a b)")]
            b_vgb = [b_vg, b_h[1 - hh], b_QT]

            def vproj(blk):
                vgt = vgb[blk % 3]
                bvg = b_vgb[blk % 3]
                for cg in range(2):
                    mb = mmbank()
                    s = pvv[cg]
                    for kc in range(8):
                        P.op("pe", lambda e, s=s, kc=kc, mb=mb: e.matmul(
                            psA[:, mb, :], nT[:, kc, blk * 128:(blk + 1) * 128], wp[s][:, kc, :],
                            start=(kc == 0), stop=(kc == 7)),
                             reads=[b_wp[s], b_nT], writes=[b_psA[mb]])
                    P.op("act", lambda e, mb=mb, cg=cg: e.activation(out=vgt[:, cg * 512:(cg + 1) * 512], in_=psA[:, mb, :],
                                                                     func=AF.Gelu_apprx_tanh),
                         reads=[b_psA[mb]], writes=[bvg])

            def ln_spatial(blk):
                vgt = vgb[blk % 3]
                bvg = b_vgb[blk % 3]
                P.op("dve", lambda e: e.bn_stats(ST6[:, 0:6], vgt[:, 0:512]), reads=[bvg], writes=[b_st6])
                P.op("dve", lambda e: e.bn_stats(ST6[:, 6:12], vgt[:, 512:1024]), reads=[bvg, b_st6], writes=[b_st6])
                P.op("dve", lambda e: e.bn_aggr(MV, ST6), reads=[b_st6], writes=[b_mv])
                P.op("dve", lambda e: e.tensor_scalar(out=LNV, in0=MV[:, 1:2], scalar1=1e-5, scalar2=None, op0=ALU.add),
                     reads=[b_mv], writes=[b_lnv])
                pool_pow(LNR, LNV, [b_lnv], [b_lnr])
                P.op("dve", lambda e: e.scalar_tensor_tensor(out=vgt, in0=vgt, scalar=MV[:, 0:1], in1=vrep[vi_g][:],
                                                             op0=ALU.subtract, op1=ALU.mult),
                     reads=[bvg, b_mv, b_vrep[vi_g]], writes=[bvg])
                P.op("dve", lambda e: e.scalar_tensor_tensor(out=vn[:], in0=vgt, scalar=LNR, in1=vrep[vi_b][:],
                                                             op0=ALU.mult, op1=ALU.add),
                     reads=[bvg, b_lnr, b_vrep[vi_b]], writes=[b_vn])
                for g in range(8):
                    bk = 2 + g // 4
                    P.op("pe", lambda e, g=g: e.matmul(SG3[:, g, :], vn[:, g * 128:(g + 1) * 128], WsT[:, g, :],
                                                       start=(g % 4 == 0), stop=False, skip_group_check=True),
                         reads=[b_vn, b_WsT], writes=[b_psB[bk]])
                    P.op("pe", lambda e, g=g: e.matmul(SG3[:, g, :], ones1[:], bs2[:, g * 128:(g + 1) * 128],
                                                       start=False, stop=True, skip_group_check=True),
                         reads=[b_ones1, b_bs2], writes=[b_psB[bk]])
                P.op("dve", lambda e: e.tensor_tensor(out=sgoT[:, :, blk * 128:(blk + 1) * 128], in0=SG3,
                                                      in1=uT[:, :, blk * 128:(blk + 1) * 128], op=ALU.mult),
                     reads=[b_psB[2], b_psB[3], b_B[1]], writes=[b_B[1]])

            vproj(0)
            vproj(1)
            for blk in range(4):
                if blk + 2 < 4:
                    vproj(blk + 2)
                ln_spatial(blk)
            for cg in range(2):
                sa = load_panel(wb_ba, "ba0", 0, cg)
                ss_ = load_panel(wb_bs, "bs0", 0, cg)
                for ctl in range(4):
                    ct = cg * 4 + ctl
                    m1 = mmbank()
                    for kc in range(8):
                        P.op("pe", lambda e, sa=sa, ctl=ctl, kc=kc, m1=m1: e.matmul(
                            psA[:, m1, :], wp[sa][:, kc, ctl * 128:(ctl + 1) * 128], aT[:, kc, :],
                            start=(kc == 0), stop=(kc == 7)),
                             reads=[b_wp[sa], b_A[0]], writes=[b_psA[m1]])
                    m2 = mmbank()
                    for kc in range(8):
                        P.op("pe", lambda e, ss_=ss_, ctl=ctl, kc=kc, m2=m2: e.matmul(
                            psA[:, m2, :], wp[ss_][:, kc, ctl * 128:(ctl + 1) * 128], sgoT[:, kc, :],
                            start=(kc == 0), stop=(kc == 7)),
                             reads=[b_wp[ss_], b_B[1]], writes=[b_psA[m2]])
                    P.op("dve", lambda e, m1=m1, ct=ct: e.tensor_tensor(out=T[1][:], in0=psA[:, m1, :], in1=gatesT[:, ct, :],
                                                                        op=ALU.mult),
                         reads=[b_psA[m1], b_B[0]], writes=[b_T[1]])
                    P.op("dve", lambda e, m2=m2, ct=ct: e.tensor_tensor(out=T[2][:], in0=psA[:, m2, :], in1=gatesT[:, 8 + ct, :],
                                                                        op=ALU.mult),
                         reads=[b_psA[m2], b_B[0]], writes=[b_T[2]])
                    P.op("dve", lambda e, ct=ct: e.tensor_tensor(out=mergedT[:, ct, :], in0=T[1][:], in1=T[2][:], op=ALU.add),
                         reads=[b_T[1], b_T[2]], writes=[b_mT])

            def out_proj(wb, castname, srcT, bsrc, stats_par=None, next_vrow=None):
                ss2 = [load_panel(wb, castname, 0, cg) for cg in range(2)]
                vi_n = load_vrep(next_vrow) if next_vrow is not None else None
                junk = PT[2][:].rearrange("p a b -> p (a b)")
                for blk in range(4):
                    for cg in range(2):
                        s = ss2[cg]
                        mb = mmbank()
                        for kc in range(8):
                            P.op("pe", lambda e, s=s, kc=kc, mb=mb, blk=blk: e.matmul(
                                psA[:, mb, :], srcT[:, kc, blk * 128:(blk + 1) * 128], wp[s][:, kc, :],
                                start=(kc == 0), stop=(kc == 7)),
                                 reads=[b_wp[s], bsrc], writes=[b_psA[mb]])
                        P.op("dve", lambda e, mb=mb, blk=blk, cg=cg: e.tensor_tensor(
                            out=h[:, blk, cg * 512:(cg + 1) * 512], in0=psA[:, mb, :],
                            in1=h[:, blk, cg * 512:(cg + 1) * 512], op=ALU.add),
                             reads=[b_psA[mb], bh], writes=[bh])
                    if stats_par is not None:
                        norm_sq(h, bh, blk, stats_par, junk, b_PT[2])
                        norm_finish_blk(stats_par, blk)
                        if vi_n is not None and blk in (1, 2):
                            norm_pre(h, bh, vi_n, stats_par, blk - 1)
                return vi_n

            vi_x = out_proj(wb_out, "out0", mergedT, b_mT, stats_par=0, next_vrow=1)

            if dbg and og == 0:
                dbg_out["h1"] = nc.dram_tensor("dbg_h1", [128, 4 * 1024], F32, kind="ExternalOutput").ap()
                P.op("pool", lambda e: e.dma_start(out=dbg_out["h1"], in_=h[:].rearrange("p a b -> p (a b)")),
                     reads=[bh], writes=[b_y], dma=True, sem="yout")

            norm_apply(h, bh, 1, 0, vi_fixed=vi_x, pre_done=2)
            for cg in range(2):
                s = load_panel(wb_xq, "xq0", 0, cg)
                for ctl in range(4):
                    ct = cg * 4 + ctl
                    mb = mmbank()
                    for kc in range(8):
                        P.op("pe", lambda e, s=s, ctl=ctl, kc=kc, mb=mb: e.matmul(
                            psA[:, mb, :], wp[s][:, kc, ctl * 128:(ctl + 1) * 128], nT[:, kc, :],
                            start=(kc == 0), stop=(kc == 7)),
                             reads=[b_wp[s], b_nT], writes=[b_psA[mb]])
                    P.op("act", lambda e, mb=mb, ct=ct: e.activation(out=QT[:, ct, :], in_=psA[:, mb, :], func=AF.Copy),
                         reads=[b_psA[mb]], writes=[b_QT])
            for hx in range(4):
                sbk = hx % 2
                for mbk in range(2):
                    for c in range(2):
                        P.op("pe", lambda e, sbk=sbk, mbk=mbk, c=c, hx=hx: e.matmul(
                            psA[:, 2 * sbk + mbk, :], kxT[:, hx * 2 + c, mbk * 128:(mbk + 1) * 128], QT[:, hx * 2 + c, :],
                            start=(c == 0), stop=(c == 1)),
                             reads=[b_kxT, b_QT], writes=[b_psA[2 * sbk + mbk]])
                pb = st["pt"] % NPT
                st["pt"] += 1
                P.op("act", lambda e, sbk=sbk, pb=pb: e.activation(
                    out=PT[pb][:], in_=psA[:, 2 * sbk:2 * sbk + 2, :], func=AF.Exp, scale=1.0 / 16.0),
                     reads=[b_psA[2 * sbk], b_psA[2 * sbk + 1]], writes=[b_PT[pb]])
                for blk in range(4):
                    for mbk in range(2):
                        P.op("pe", lambda e, blk=blk, mbk=mbk, pb=pb, hx=hx: e.matmul(
                            psB[:, blk, 0:257], PT[pb][:, mbk, blk * 128:(blk + 1) * 128], vxa[:, mbk, hx, :],
                            start=(mbk == 0), stop=(mbk == 1)),
                             reads=[b_PT[pb], b_vxa], writes=[b_psB[blk]])
                    P.op("dve", lambda e, blk=blk: e.reciprocal(RDX, psB[:, blk, 256:257]),
                         reads=[b_psB[blk]], writes=[b_rdx])
                    P.op("dve", lambda e, blk=blk, hx=hx: e.tensor_scalar(
                        out=atok[:, blk, hx * 256:(hx + 1) * 256], in0=psB[:, blk, 0:256], scalar1=RDX, scalar2=None,
                        op0=ALU.mult),
                         reads=[b_psB[blk], b_rdx], writes=[b_atok])
            transpose_tok_to_feat(atok, b_atok, aT, b_A[0])
            vi_f = out_proj(wb_xo, "xo0", aT, b_A[0], stats_par=0, next_vrow=2)

            if og + 1 < n_groups:
                hn = hb[1 - hh]
                P.op("sp", lambda e: e.dma_start(
                    out=hn[:], in_=xo[(og + 1) * 512:(og + 2) * 512, :].rearrange("(b p) d -> p b d", p=128)),
                     writes=[b_h[1 - hh]], dma=True, sem=f"x{1 - hh}")
                load_kv(0, 8 * (og + 2))
                hstate["kv0_loaded"] = True
            norm_apply(h, bh, 2, 0, vi_fixed=vi_f, pre_done=2)
            for half in range(2):
                for cgl in range(4):
                    cg = half * 4 + cgl
                    s = load_panel(wb_ff1, f"ff1{cg // 2}", 0, cg)
                    for ctl in range(4):
                        ctloc = cgl * 4 + ctl
                        mb = mmbank()
                        for kc in range(8):
                            P.op("pe", lambda e, s=s, ctl=ctl, kc=kc, mb=mb: e.matmul(
                                psA[:, mb, :], wp[s][:, kc, ctl * 128:(ctl + 1) * 128], nT[:, kc, :],
                                start=(kc == 0), stop=(kc == 7)),
                                 reads=[b_wp[s], b_nT], writes=[b_psA[mb]])
                        P.op("act", lambda e, mb=mb: e.activation(out=kraw[:], in_=psA[:, mb, :], func=AF.Square),
                             reads=[b_psA[mb]], writes=[b_kraw])
                        P.op("dve", lambda e, mb=mb, ctloc=ctloc: e.scalar_tensor_tensor(
                            out=hidT[:, ctloc, :], in0=psA[:, mb, :], scalar=0.0, in1=kraw[:],
                            op0=ALU.is_gt, op1=ALU.mult),
                             reads=[b_psA[mb], b_kraw], writes=[b_A[1]])
                if half == 1 and og + 1 < n_groups:
                    rope_tables(poso, (og + 1) * 512)
                    norm_stats(hb[1 - hh], b_h[1 - hh], 1)
                sp4 = [[load_panel(wb_ff2, f"ff2{half * 2 + kgl}", half * 2 + kgl, cg) for kgl in range(2)]
                       for cg in range(2)]
                for blk in range(4):
                    for cg in range(2):
                        sp_ = sp4[cg]
                        mb = mmbank()
                        for kgl in range(2):
                            for kc in range(8):
                                P.op("pe", lambda e, kgl=kgl, kc=kc, mb=mb, blk=blk, sp_=sp_: e.matmul(
                                    psA[:, mb, :], hidT[:, kgl * 8 + kc, blk * 128:(blk + 1) * 128], wp[sp_[kgl]][:, kc, :],
                                    start=(kgl == 0 and kc == 0), stop=(kgl == 1 and kc == 7)),
                                     reads=[b_wp[sp_[kgl]], b_A[1]], writes=[b_psA[mb]])
                        P.op("dve", lambda e, mb=mb, blk=blk, cg=cg: e.tensor_tensor(
                            out=h[:, blk, cg * 512:(cg + 1) * 512], in0=psA[:, mb, :],
                            in1=h[:, blk, cg * 512:(cg + 1) * 512], op=ALU.add),
                             reads=[b_psA[mb], bh], writes=[bh])
                    if half == 1:
                        norm_sq(h, bh, blk, 0, PT[2][:].rearrange("p a b -> p (a b)"), b_PT[2])
                        norm_finish_blk(0, blk)

            def epilogue():
                vi = load_vrep(3)
                for blk in range(4):
                    P.op("dve", lambda e, blk=blk: e.scalar_tensor_tensor(out=vg[:], in0=h[:, blk, :],
                                                                          scalar=RSp[0][:, blk:blk + 1], in1=vrep[vi][:],
                                                                          op0=ALU.mult, op1=ALU.mult),
                         reads=[bh, b_rsb[0][blk], b_vrep[vi]], writes=[b_vg])
                    r0 = (og * 4 + blk) * 128
                    P.op("pool", lambda e, r0=r0: e.dma_start(out=y[r0:r0 + 128, :], in_=vg[:]),
                         reads=[b_vg], writes=[b_y], dma=True, sem="yout")

            hstate["epilogue"] = epilogue

        for og_ in range(n_groups):
            group_body(og_)
        if hstate["epilogue"] is not None:
            hstate["epilogue"]()

        if dbg:
            dbg_out["Kt"] = Kt
        P.out_keys = [("dma", "yout")] if n_groups > 0 else [("dma", "kst"), ("dma", "vst")] + [
            ("dma", "c_" + nm) for nm in cast_bufs]
        if trunc is not None:
            P.ops = P.ops[:P.marks[trunc]]
            P.out_keys = None
        P.analyze()
        if P.out_keys is None:
            P.out_keys = [k for k in P.final if isinstance(k, tuple)]
        P.emit(nc)
    return nc, P


def own_blocks(hf):
    out = []
    for i in range(NOWN):
        if i % 2 == 0:
            out.append(2 * i + (0 if hf == 0 else 1))
        else:
            out.append(2 * i + (1 if hf == 0 else 0))
    return out


def make_consts():
    cbm = np.zeros((128, 256), np.float32)
    cbm[:, :128] = np.eye(128, dtype=np.float32)
    Pm = np.zeros((128, 128), np.float32)
    for base in (0, 64):
        for d in range(8):
            m = base + d
            Pm[base + d + 8, m] = -1.0
            m2 = base + 8 + d
            Pm[base + d, m2] = 1.0
    cbm[:, 128:] = Pm
    cfm = np.zeros((128, 4), np.float32)
    idx = np.arange(0, 16, 2, dtype=np.float32)
    inv_freq = np.power(np.float32(500000.0), -idx / np.float32(16)).astype(np.float32)
    for r in range(128):
        d = r % 64
        if d < 16:
            cfm[r, 0] = inv_freq[d % 8]
    cfm[:, 1] = np.float32(math.pi / 2)
    cfm[:, 2] = -0.5
    return cbm.astype(ml_dtypes.bfloat16), cfm


def make_mask(hf):
    Dm = np.ones((128, 128), np.float32)
    Dm[64:, :64] = 0.0
    ones = np.ones((128, 128), np.float32)
    zeros = np.zeros((128, 128), np.float32)
    typeA = (Dm, zeros)
    typeB = (ones, Dm)
    m = np.zeros((128, 2, 2, 128), np.float32)
    for par in range(2):
        j_is_even = (par == 0 and hf == 0) or (par == 1 and hf == 1)
        t = typeA if j_is_even else typeB
        m[:, par, 0, :] = t[0]
        m[:, par, 1, :] = t[1]
    return m.reshape(128, 512).astype(ml_dtypes.bfloat16)


_CACHE = {}


def kernel(x, mem, positions, g_mix, w_in, lam_q1, lam_k1, lam_q2, lam_k2, g_subln,
           sg_ln_g, sg_ln_b, sg_w, sg_b, w_branch_attn, w_branch_sg, w_out,
           g_xa, g_mem, w_xq, w_xkv, w_xo, g_ffn, w_ff1, w_ff2, g_final):
    f = lambda a: np.ascontiguousarray(np.asarray(a, dtype=np.float32))
    x = f(x); mem = f(mem)
    positions = np.ascontiguousarray(np.asarray(positions, dtype=np.int32))
    if "nc" not in _CACHE:
        _CACHE["nc"] = build_program()[0]
    nc = _CACHE["nc"]
    cbm, cfm = make_consts()
    vecs = np.stack([f(g_mix)[0], f(g_xa)[0], f(g_ffn)[0], f(g_final), f(g_mem)[0], f(sg_ln_g)[0], f(sg_ln_b)[0]], 0)
    lamv = np.stack([f(lam_q1)[0], f(lam_k1)[0], f(lam_q2)[0], f(lam_k2)[0]], 0)
    shared = {
        "w_in": f(w_in)[0], "w_ba": f(w_branch_attn)[0], "w_bs": f(w_branch_sg)[0], "w_out": f(w_out)[0],
        "w_xq": f(w_xq)[0], "w_xo": f(w_xo)[0], "w_xkv": f(w_xkv)[0], "w_ff1": f(w_ff1)[0], "w_ff2": f(w_ff2)[0],
        "vecs": np.ascontiguousarray(vecs), "g_subln": f(g_subln)[0], "lamv": np.ascontiguousarray(lamv),
        "sg_w": f(sg_w)[0], "sg_b": f(sg_b)[0].reshape(-1), "cb": cbm, "cf": cfm,
    }
    in_maps = []
    owns = []
    for c in range(8):
        b, hf = c // 2, c % 2
        ob = own_blocks(hf)
        owns.append(ob)
        rows = np.concatenate([np.arange(j * 128, (j + 1) * 128) for j in ob])
        m = dict(shared)
        m["xf"] = x[b]
        m["xo"] = np.ascontiguousarray(x[b][rows])
        m["posf"] = positions[b]
        m["poso"] = np.ascontiguousarray(positions[b][rows])
        m["memb"] = mem[b]
        m["maskd"] = make_mask(hf)
        in_maps.append(m)
    res = run_bass_kernel_spmd(nc, in_maps, core_ids=list(range(8)))
    out = np.empty((4, S, D), np.float32)
    for c in range(8):
        b = c // 2
        yc = np.asarray(res.results[c]["y"], dtype=np.float32)
        for i, j in enumerate(owns[c]):
            out[b, j * 128:(j + 1) * 128, :] = yc[i * 128:(i + 1) * 128, :]
    return out
```

```python
import math
from contextlib import ExitStack

import numpy as np
import ml_dtypes

import concourse.bass as bass
import concourse.mybir as mybir
from concourse.bass_utils import run_bass_kernel_spmd

F32 = mybir.dt.float32
BF = mybir.dt.bfloat16
I32 = mybir.dt.int32
AF = mybir.ActivationFunctionType
ALU = mybir.AluOpType
AX = mybir.AxisListType

D = 1024
S = 8192
NB = 64
NOWN = 32
NG = 8
NSG = 16
EPS = 1e-6
DIN = 7168
MAGIC = 12582912.0
TWO_PI_HI = 6.28125
TWO_PI_LO = 2.0 * math.pi - 6.28125
PI_SAFE = 3.1415925

ENGS = ("pe", "act", "dve", "pool", "sp")


class Buf:
    __slots__ = ("name", "writer", "readers", "excl")

    def __init__(self, name, excl=False):
        self.name = name
        self.writer = None
        self.readers = []
        self.excl = excl


class Op:
    __slots__ = ("eng", "dma", "reads", "writes", "emit", "deps", "signal",
                 "tokval", "sem", "idx", "waits", "n_dma")

    def __init__(self, eng, dma, reads, writes, emit, sem=None, n_dma=1):
        self.eng = eng
        self.dma = dma
        self.reads = reads
        self.writes = writes
        self.emit = emit
        self.deps = []
        self.signal = dma
        self.tokval = None
        self.sem = sem
        self.waits = None
        self.n_dma = n_dma


def _is_raw(p, o):
    for b in p.writes:
        for r in o.reads:
            if r is b:
                return True
    return False


class Prog:
    def __init__(self):
        self.ops = []
        self.out_keys = []
        self.marks = {}

    def mark(self, name):
        if name not in self.marks:
            self.marks[name] = len(self.ops)

    def op(self, eng, emit, reads=(), writes=(), dma=False, sem=None, n_dma=1):
        reads = list(reads)
        writes = list(writes)
        for b in list(reads):
            if b.excl:
                reads.remove(b)
                if b not in writes:
                    writes.append(b)
        o = Op(eng, dma, reads, writes, emit, sem, n_dma)
        o.idx = len(self.ops)
        self.ops.append(o)
        return o

    def analyze(self):
        for o in self.ops:
            deps = {}
            for b in o.reads:
                p = b.writer
                if p is not None:
                    deps[p.idx] = p
            for b in o.writes:
                p = b.writer
                if p is not None:
                    deps[p.idx] = p
                for r in b.readers:
                    deps[r.idx] = r
            deps.pop(o.idx, None)
            for b in o.reads:
                b.readers.append(o)
            for b in o.writes:
                b.writer = o
                b.readers = []
            o.deps = list(deps.values())
        last_wait = {e: {} for e in ENGS}
        need = {}
        for o in self.ops:
            per = {}
            for p in o.deps:
                if p.dma:
                    key = ("dma", p.sem)
                else:
                    key = p.eng
                    if p.eng == o.eng and not o.dma:
                        if p.eng == "pe":
                            continue
                if key not in per or per[key].idx < p.idx:
                    per[key] = p
            keep = {}
            lw = last_wait[o.eng]
            for key, p in per.items():
                if lw.get(key, -1) >= p.idx:
                    continue
                lw[key] = p.idx
                keep[key] = p
                p.signal = True
            need[o.idx] = keep
        cnt = {}
        for o in self.ops:
            if o.dma:
                key = ("dma", o.sem)
                cnt[key] = cnt.get(key, 0) + 16 * o.n_dma
                o.tokval = cnt[key]
            elif o.signal:
                cnt[o.eng] = cnt.get(o.eng, 0) + 1
                o.tokval = cnt[o.eng]
        run = {}
        for o in self.ops:
            w = []
            for key, p in need[o.idx].items():
                if isinstance(key, tuple):
                    w.append((key, run.get(key, p.tokval)))
                else:
                    w.append((key, p.tokval))
            o.waits = w
            if o.dma:
                run[("dma", o.sem)] = o.tokval
        self.final = dict(cnt)

    def emit(self, nc):
        keys = set()
        for o in self.ops:
            if o.dma:
                keys.add(("dma", o.sem))
            elif o.signal:
                keys.add(o.eng)
        with ExitStack() as es:
            sems = {}
            for k in sorted(keys, key=str):
                nm = "s_" + (k if isinstance(k, str) else "d_" + str(k[1]))
                sems[k] = es.enter_context(nc.semaphore(nm))
            block = es.enter_context(nc.Block())
            engobj = {"pe": block.tensor, "act": block.scalar, "dve": block.vector,
                      "pool": block.gpsimd, "sp": block.sync}
            final = self.final
            out_keys = self.out_keys

            def make(engname):
                myops = [o for o in self.ops if o.eng == engname]

                def body(e):
                    for o in myops:
                        for key, val in o.waits:
                            e.wait_ge(sems[key], val)
                        ins = o.emit(e)
                        if o.dma:
                            if not isinstance(ins, (list, tuple)):
                                ins = [ins]
                            assert len(ins) == o.n_dma
                            for i_ in ins:
                                i_.then_inc(sems[("dma", o.sem)], 16)
                        elif o.signal:
                            ins.then_inc(sems[engname], 1)
                    if engname == "sp":
                        for k in sorted(out_keys, key=str):
                            if k in final:
                                e.wait_ge(sems[k], final[k])
                return body

            for en in ENGS:
                engobj[en](make(en))


def build_program(n_groups=NG, n_sgroups=NSG, dbg=False, trunc=None):
    nc = bass.Bass("TRN2", target_bir_lowering=False)
    P = Prog()

    def din(name, shape, dt=F32):
        return nc.dram_tensor(name, list(shape), dt, kind="ExternalInput").ap()

    xf = din("xf", [S, D])
    xo = din("xo", [NOWN * 128, D])
    posf = din("posf", [S], I32)
    poso = din("poso", [NOWN * 128], I32)
    memd = din("memb", [256, D])
    w_in = din("w_in", [D, DIN])
    w_ba = din("w_ba", [D, D])
    w_bs = din("w_bs", [D, D])
    w_out = din("w_out", [D, D])
    w_xq = din("w_xq", [D, D])
    w_xo = din("w_xo", [D, D])
    w_xkv = din("w_xkv", [D, 2 * D])
    w_ff1 = din("w_ff1", [D, 4 * D])
    w_ff2 = din("w_ff2", [4 * D, D])
    vecs = din("vecs", [7, D])
    gsubd = din("g_subln", [128])
    lamd = din("lamv", [4, 64])
    sgwd = din("sg_w", [8, 128, 128])
    sgbd = din("sg_b", [8 * 128])
    cbd = din("cb", [128, 256], BF)
    cfd = din("cf", [128, 4])
    maskd = din("maskd", [128, 2 * 2 * 128], BF)
    y = nc.dram_tensor("y", [NOWN * 128, D], F32, kind="ExternalOutput").ap()

    def dscr(name, shape, dt=BF):
        return nc.dram_tensor(name, list(shape), dt).ap()

    wb_in = dscr("wb_in", [D, DIN])
    wb_ba = dscr("wb_ba", [D, D])
    wb_bs = dscr("wb_bs", [D, D])
    wb_out = dscr("wb_out", [D, D])
    wb_xq = dscr("wb_xq", [D, D])
    wb_xo = dscr("wb_xo", [D, D])
    wb_xkv = dscr("wb_xkv", [D, 2 * D])
    wb_ff1 = dscr("wb_ff1", [D, 4 * D])
    wb_ff2 = dscr("wb_ff2", [4 * D, D])
    Kt = dscr("Kt", [D, S])
    Vs = dscr("Vs", [8, 128, NB * 129])
    dbg_out = {}

    with ExitStack() as es:
        def sb(name, shape, dt):
            return es.enter_context(nc.sbuf_tensor(name, list(shape), dt))

        cb = sb("cb_s", [128, 256], BF); b_cb = Buf("cb")
        cf = sb("cf_s", [128, 4], F32); b_cf = Buf("cf")
        mask = sb("mask_s", [128, 2, 2, 128], BF); b_mask = Buf("mask")
        gsub = sb("gsub", [128, 128], F32); b_gsub = Buf("gsub")
        lams = sb("lams", [128, 4], F32); b_lams = Buf("lams")
        WsT = sb("WsT", [128, 8, 128], BF); b_WsT = Buf("WsT")
        bs2 = sb("bs2", [2, 1024], BF); b_bs2 = Buf("bs2")
        ones1 = sb("ones1", [2, 128], BF); b_ones1 = Buf("ones1")
        kxT = sb("kxT", [128, 8, 256], BF); b_kxT = Buf("kxT")
        vxa = sb("vxa", [128, 2, 4, 257], BF); b_vxa = Buf("vxa")
        NVR = 2
        vrep = [sb(f"vrep{i}", [128, 1024], F32) for i in range(NVR)]
        b_vrep = [Buf(f"vrep{i}") for i in range(NVR)]
        NWP = 4
        wp = [sb(f"wp{i}", [128, 8, 512], BF) for i in range(NWP)]
        b_wp = [Buf(f"wp{i}") for i in range(NWP)]
        hb = [sb(f"h{i}", [128, 4, 1024], F32) for i in range(2)]
        b_h = [Buf(f"h{i}") for i in range(2)]
        nbt = sb("nbt", [128, 1024], BF); b_nbt = Buf("nbt")
        nT = sb("nT", [128, 8, 512], BF); b_nT = Buf("nT")
        T = [sb(f"T{i}", [128, 512], F32) for i in range(3)]
        b_T = [Buf(f"T{i}") for i in range(3)]
        posi = T[0].bitcast(I32); b_posi = b_T[0]
        sinT = sb("sinT", [128, 512], F32); b_sinT = Buf("sinT")
        cosT = sb("cosT", [128, 512], F32); b_cosT = Buf("cosT")
        kraw = sb("kraw", [128, 512], BF); b_kraw = Buf("kraw")
        kraw2 = None
        QT = sb("QT", [128, 8, 512], BF); b_QT = Buf("QT")
        A = [sb(f"A{i}", [128, 8192], BF) for i in range(2)]
        b_A = [Buf(f"A{i}") for i in range(2)]
        b_mT = Buf("mergedT")
        b_Al = [[b_A[0], b_mT], [b_A[1]]]
        B = [sb(f"B{i}", [128, NB * 129], BF) for i in range(2)]
        b_B = [Buf(f"B{i}") for i in range(2)]
        NPT = 3
        PT = [sb(f"PT{i}", [128, 2, 512], BF) for i in range(NPT)]
        b_PT = [Buf(f"PT{i}") for i in range(NPT)]
        atok = sb("atok", [128, 4, 1024], BF); b_atok = Buf("atok")
        vg = sb("vg", [128, 1024], F32); b_vg = Buf("vg")
        vg2 = hb[1][:, 0, :]; b_vg2 = b_h[1]
        bsf = hb[1][0:1, 1, :]; b_bsf = b_h[1]
        bsf2 = hb[1][0:1, 2, :]; b_bsf2 = b_h[1]
        vn = sb("vn", [128, 1024], BF); b_vn = Buf("vn")
        bsh = nbt[0:1, :]; b_bsh = b_nbt
        bsl = vn[0:1, :]; b_bsl = b_vn
        kraw2 = [kraw, vn]; b_kraw2 = [b_kraw, b_vn]
        nbts = [nbt, vn]; b_nbts = [b_nbt, b_vn]
        lamt = hb[0][:, 0, 0:256].rearrange("p (a b) -> p a b", a=4); b_lamt = b_h[0]
        lamp = hb[0][:, 1, 0:128].rearrange("p (a b) -> p a b", a=2); b_lamp = b_h[0]
        sm = sb("sm", [128, 96], F32)
        b_ss = Buf("ss"); b_ms = Buf("ms"); b_rstd = Buf("rstd")
        b_st6 = Buf("st6"); b_mv = Buf("mv"); b_lnr = Buf("lnr")
        b_rden = Buf("rden"); b_lr = Buf("lr"); b_ss2 = Buf("ss2"); b_rs2 = Buf("rs2")
        oraws = [sb(f"oraw{i}", [128, 258], F32) for i in range(2)]
        b_oraws = [Buf(f"oraw{i}") for i in range(2)]
        o32s = [[sb(f"o32_{a_}_{q_}", [128, 128], F32) for q_ in range(4)] for a_ in range(2)]
        b_o32s = [[Buf(f"o32_{a_}_{q_}") for q_ in range(4)] for a_ in range(2)]
        t32 = sb("t32", [128, 128], F32); b_t32 = Buf("t32")
        SS = sm[:, 0:4]; MS = sm[:, 4:8]; RSTD = sm[:, 8:12]
        ST6 = sm[:, 12:24]; MV = sm[:, 24:26]; LNV = sm[:, 26:27]; LNR = sm[:, 27:28]
        RDEN = sm[:, 28:30]; LR = sm[:, 30:31]; SS2 = sm[:, 31:32]; MS2 = sm[:, 32:33]; RS2 = sm[:, 33:34]
        RDX = sm[:, 34:35]
        SS4 = [sm[:, 36:40], sm[:, 40:44]]; MS4 = [sm[:, 44:48], sm[:, 48:52]]; RS4 = [sm[:, 52:56], sm[:, 56:60]]
        b_ss4 = [Buf("ss4a"), Buf("ss4b")]; b_ms4 = [Buf("ms4a"), Buf("ms4b")]; b_rs4 = [Buf("rs4a"), Buf("rs4b")]
        b_rdx = Buf("rdx"); b_ms2 = Buf("ms2"); b_lnv = Buf("lnv")
        SSp = [sm[:, 0:4], sm[:, 64:68]]; MSp = [sm[:, 4:8], sm[:, 68:72]]; RSp = [sm[:, 8:12], sm[:, 72:76]]
        b_ssb = [[Buf(f"ss{p_}_{k_}") for k_ in range(4)] for p_ in range(2)]
        b_msb = [[Buf(f"ms{p_}_{k_}") for k_ in range(4)] for p_ in range(2)]
        b_rsb = [[Buf(f"rs{p_}_{k_}") for k_ in range(4)] for p_ in range(2)]

        aT = A[0][:, 0:4096].rearrange("p (k t) -> p k t", t=512)
        mergedT = A[0][:, 4096:8192].rearrange("p (k t) -> p k t", t=512)
        hidT = A[1][:, :].rearrange("p (k t) -> p k t", t=512)
        gatesT = B[0][:, 0:8192].rearrange("p (k t) -> p k t", t=512)
        uT = B[1][:, 0:4096].rearrange("p (k t) -> p k t", t=512)
        sgoT = B[1][:, 4096:8192].rearrange("p (k t) -> p k t", t=512)
        KTo = A[0][:, 0:4096].rearrange("p (k t) -> p k t", t=512)
        Vaug = B[0][:, 0:8 * 4 * 129].rearrange("p (h b e) -> p h b e", h=8, b=4)

        psA = es.enter_context(nc.psum_tensor("psA", [128, 4, 512], F32))
        psB = es.enter_context(nc.psum_tensor("psB", [128, 4, 512], F32))
        b_psA = [Buf(f"psA{i}", excl=True) for i in range(4)]
        b_psB = [Buf(f"psB{i}", excl=True) for i in range(4)]
        psBb = psB.bitcast(BF)

        ident = cb[:, 0:128]
        Pm = cb[:, 128:256]
        INVF = cf[:, 0:1]
        HALFPI = cf[:, 1:2]
        NEGHALF = cf[:, 2:3]
        EPS6 = cf[:, 2:3]
        EPS5 = cf[:, 3:4]

        b_Kt = [Buf(f"Kt{i}") for i in range(NSG)]
        b_Vs = [Buf(f"Vs{i}") for i in range(NSG)]
        b_y = Buf("y")

        cast_bufs = {}

        cast_order = []
        cast_dep = []

        def cast(name, src, dst, r0, r1, c0, c1):
            b = Buf("cast_" + name)
            cast_bufs[name] = b
            dep = list(cast_dep)
            cast_order.append(b)
            P.op("pool", lambda e: e.dma_start(out=dst[r0:r1, c0:c1], in_=src[r0:r1, c0:c1]),
                 reads=dep, writes=[b], dma=True, sem="c_" + name)

        cast_list = [("xkv0", w_xkv, wb_xkv, 0, D, 0, 1024), ("xkv1", w_xkv, wb_xkv, 0, D, 1024, 2048)]
        for cblk in (0, 5, 6, 3, 4):
            cast_list.append((f"in{cblk}", w_in, wb_in, 0, D, cblk * 1024, (cblk + 1) * 1024))
        cast_list += [("ba0", w_ba, wb_ba, 0, D, 0, D), ("bs0", w_bs, wb_bs, 0, D, 0, D),
                      ("out0", w_out, wb_out, 0, D, 0, D), ("xq0", w_xq, wb_xq, 0, D, 0, D),
                      ("xo0", w_xo, wb_xo, 0, D, 0, D)]
        for i in range(4):
            cast_list.append((f"ff1{i}", w_ff1, wb_ff1, 0, D, i * 1024, (i + 1) * 1024))
        for i in range(4):
            cast_list.append((f"ff2{i}", w_ff2, wb_ff2, i * 1024, (i + 1) * 1024, 0, D))

        def emit_casts(n):
            for _ in range(n):
                if cast_list:
                    cast(*cast_list.pop(0))

        st = {"wp": 0, "mm": 0, "vr": 0, "pt": 0, "tp": 0}

        def load_panel(wb, castname, kg, cg):
            s = st["wp"] % NWP
            st["wp"] += 1
            src = wb[kg * 1024:(kg + 1) * 1024, cg * 512:(cg + 1) * 512].rearrange("(kc p) c -> p kc c", p=128)
            P.op("sp", lambda e: e.dma_start(out=wp[s][:], in_=src),
                 reads=[cast_bufs[castname]], writes=[b_wp[s]], dma=True, sem=f"wp{s}")
            return s

        def mmbank():
            i = st["mm"] % 4
            st["mm"] += 1
            return i

        def load_vrep(row):
            i = st["vr"] % NVR
            st["vr"] += 1
            src = bass.AP(vecs.tensor, row * D, [[0, 128], [1, D]])
            P.op("sp", lambda e: e.dma_start(out=vrep[i][:], in_=src), writes=[b_vrep[i]], dma=True, sem=f"vr{i}")
            return i

        def pool_pow(out_ap, in_ap, rbufs, wbufs):
            P.op("act", lambda e: e.activation(out=out_ap, in_=in_ap, func=AF.Ln),
                 reads=list(rbufs), writes=list(wbufs))
            P.op("act", lambda e: e.activation(out=out_ap, in_=out_ap, func=AF.Exp, scale=-0.5),
                 reads=list(wbufs), writes=list(wbufs))

        def rsqrt_fused(out_ap, in_ap, scale, eps_col, rbufs, wbufs):
            P.op("act", lambda e: e.activation(out=out_ap, in_=in_ap, func=AF.Ln, scale=scale, bias=eps_col),
                 reads=list(rbufs) + [b_cf], writes=list(wbufs))
            P.op("act", lambda e: e.activation(out=out_ap, in_=out_ap, func=AF.Exp, scale=-0.5),
                 reads=list(wbufs), writes=list(wbufs))

        def norm_sq(h, bh, blk, par, junk=None, bjunk=None):
            jk = nbt[:] if junk is None else junk
            bjk = b_nbt if bjunk is None else bjunk
            P.op("act", lambda e: e.activation(out=jk, in_=h[:, blk, :], func=AF.Square,
                                               accum_out=SSp[par][:, blk:blk + 1]),
                 reads=[bh], writes=[bjk, b_ssb[par][blk]])

        def norm_pre(h, bh, vi, par, blk):
            nb_ = nbts[blk % 2]
            P.op("dve", lambda e: e.scalar_tensor_tensor(out=nb_[:], in0=h[:, blk, :],
                                                         scalar=RSp[par][:, blk:blk + 1], in1=vrep[vi][:],
                                                         op0=ALU.mult, op1=ALU.mult),
                 reads=[bh, b_rsb[par][blk], b_vrep[vi]], writes=[b_nbts[blk % 2]])

        def norm_finish(par, nblk=4):
            rsqrt_fused(RSp[par][:, 0:nblk], SSp[par][:, 0:nblk], 1.0 / D, EPS6, b_ssb[par][0:nblk], b_rsb[par][0:nblk])

        def norm_finish_blk(par, blk):
            rsqrt_fused(RSp[par][:, blk:blk + 1], SSp[par][:, blk:blk + 1], 1.0 / D, EPS6,
                        [b_ssb[par][blk]], [b_rsb[par][blk]])

        def norm_stats(h, bh, par, nblk=4):
            for blk in range(nblk):
                norm_sq(h, bh, blk, par)
            norm_finish(par, nblk)

        def norm_apply(h, bh, vrow, par, nblk=4, dst=None, bdst=None, vi_fixed=None, pre_done=0):
            dst = nT if dst is None else dst
            bdst = b_nT if bdst is None else bdst
            vi = load_vrep(vrow) if vi_fixed is None else vi_fixed
            for blk in range(nblk):
                nb_ = nbts[blk % 2]
                bnb_ = b_nbts[blk % 2]
                if blk >= pre_done:
                    norm_pre(h, bh, vi, par, blk)
                tb = st["tp"] % 2
                st["tp"] += 1
                for kc in range(8):
                    P.op("pe", lambda e, kc=kc, tb=tb, nb_=nb_: e.transpose(psBb[:, tb, kc * 128:(kc + 1) * 128],
                                                                  nb_[:, kc * 128:(kc + 1) * 128], ident),
                         reads=[bnb_, b_cb], writes=[b_psB[tb]])
                P.op("act", lambda e, blk=blk, tb=tb: e.activation(
                    out=dst[:, :, blk * 128:(blk + 1) * 128],
                    in_=psBb[:, tb, :].rearrange("p (k t) -> p k t", t=128), func=AF.Copy),
                     reads=[b_psB[tb]], writes=[bdst])

        def rmsnorm_to_nT(h, bh, vrow, nblk=4, dst=None, bdst=None, par=0):
            norm_stats(h, bh, par, nblk)
            norm_apply(h, bh, vrow, par, nblk, dst, bdst)

        def transpose_tok_to_feat(src, bsrc, dst, bdst, nblk=4):
            for blk in range(nblk):
                tb = st["tp"] % 2
                st["tp"] += 1
                for kc in range(8):
                    P.op("pe", lambda e, kc=kc, tb=tb, blk=blk: e.transpose(
                        psBb[:, tb, kc * 128:(kc + 1) * 128], src[:, blk, kc * 128:(kc + 1) * 128], ident),
                         reads=[bsrc, b_cb], writes=[b_psB[tb]])
                P.op("act", lambda e, blk=blk, tb=tb: e.activation(
                    out=dst[:, :, blk * 128:(blk + 1) * 128],
                    in_=psBb[:, tb, :].rearrange("p (k t) -> p k t", t=128), func=AF.Copy),
                     reads=[b_psB[tb]], writes=[bdst])

        def rope_tables(pos_dram, off):
            src = bass.AP(pos_dram.tensor, off, [[0, 128], [1, 512]])
            P.op("sp", lambda e: e.dma_start(out=posi[:], in_=src), writes=[b_posi], dma=True, sem="pos")
            P.op("dve", lambda e: e.tensor_copy(T[0][:], posi[:]), reads=[b_posi], writes=[b_T[0]])
            P.op("dve", lambda e: e.tensor_scalar(out=T[1][:], in0=T[0][:], scalar1=INVF, scalar2=None, op0=ALU.mult),
                 reads=[b_T[0], b_cf], writes=[b_T[1]])
            P.op("dve", lambda e: e.tensor_scalar(out=T[2][:], in0=T[1][:], scalar1=1.0 / (2.0 * math.pi), scalar2=MAGIC,
                                                  op0=ALU.mult, op1=ALU.add), reads=[b_T[1]], writes=[b_T[2]])
            P.op("dve", lambda e: e.tensor_scalar(out=T[0][:], in0=T[2][:], scalar1=-MAGIC, scalar2=None, op0=ALU.add),
                 reads=[b_T[2]], writes=[b_T[0]])
            P.op("dve", lambda e: e.scalar_tensor_tensor(out=T[2][:], in0=T[0][:], scalar=-TWO_PI_HI, in1=T[1][:],
                                                         op0=ALU.mult, op1=ALU.add),
                 reads=[b_T[0], b_T[1]], writes=[b_T[2]])
            P.op("dve", lambda e: e.scalar_tensor_tensor(out=T[1][:], in0=T[0][:], scalar=-TWO_PI_LO, in1=T[2][:],
                                                         op0=ALU.mult, op1=ALU.add),
                 reads=[b_T[0], b_T[2]], writes=[b_T[1]])
            P.op("dve", lambda e: e.tensor_scalar(out=T[2][:], in0=T[1][:], scalar1=PI_SAFE, scalar2=-PI_SAFE,
                                                  op0=ALU.min, op1=ALU.max), reads=[b_T[1]], writes=[b_T[2]])
            P.op("act", lambda e: e.activation(out=sinT[:], in_=T[2][:], func=AF.Sin), reads=[b_T[2]], writes=[b_sinT])
            P.op("dve", lambda e: e.scalar_tensor_tensor(out=T[0][:], in0=T[2][:], scalar=-1.0, in1=T[2][:],
                                                         op0=ALU.mult, op1=ALU.max),
                 reads=[b_T[2]], writes=[b_T[0]])
            P.op("act", lambda e: e.activation(out=cosT[:], in_=T[0][:], func=AF.Sin, scale=-1.0, bias=HALFPI),
                 reads=[b_T[0], b_cf], writes=[b_cosT])

        def proj_rope(panel_of_ct, dst, bdst, mid_hook=None):
            def stage_a(ct):
                s, ctl = panel_of_ct(ct)
                mb = mmbank()
                for kc in range(8):
                    P.op("pe", lambda e, s=s, ctl=ctl, kc=kc, mb=mb: e.matmul(
                        psA[:, mb, :], wp[s][:, kc, ctl * 128:(ctl + 1) * 128], nT[:, kc, :],
                        start=(kc == 0), stop=(kc == 7)),
                         reads=[b_wp[s], b_nT], writes=[b_psA[mb]])
                kb_ = ct % 2
                P.op("act", lambda e, mb=mb, kb_=kb_: e.activation(out=kraw2[kb_][:, 0:512], in_=psA[:, mb, :], func=AF.Copy),
                     reads=[b_psA[mb]], writes=[b_kraw2[kb_]])
                return mb

            def stage_b(ct, mb):
                kb_ = ct % 2
                mb2 = mmbank()
                P.op("pe", lambda e, mb2=mb2, kb_=kb_: e.matmul(psA[:, mb2, :], Pm, kraw2[kb_][:, 0:512], start=True, stop=True),
                     reads=[b_kraw2[kb_], b_cb], writes=[b_psA[mb2]])
                P.op("dve", lambda e, mb=mb: e.tensor_tensor(out=T[1][:], in0=psA[:, mb, :], in1=cosT[:], op=ALU.mult),
                     reads=[b_psA[mb], b_cosT], writes=[b_T[1]])
                P.op("dve", lambda e, mb2=mb2: e.tensor_tensor(out=T[2][:], in0=psA[:, mb2, :], in1=sinT[:], op=ALU.mult),
                     reads=[b_psA[mb2], b_sinT], writes=[b_T[2]])
                P.op("dve", lambda e, ct=ct: e.tensor_tensor(out=dst[:, ct, :], in0=T[1][:], in1=T[2][:], op=ALU.add),
                     reads=[b_T[1], b_T[2]], writes=[bdst])

            prev = None
            for ct in range(8):
                mb = stage_a(ct)
                if prev is not None:
                    stage_b(*prev)
                prev = (ct, mb)
                if ct == 4 and mid_hook is not None:
                    mid_hook()
            stage_b(*prev)

        P.op("sp", lambda e: e.dma_start(out=cb[:], in_=cbd), writes=[b_cb], dma=True, sem="cst")
        P.op("sp", lambda e: e.dma_start(out=cf[:], in_=cfd), writes=[b_cf], dma=True, sem="cst")
        P.op("sp", lambda e: e.dma_start(out=mask[:].rearrange("p a b c -> p (a b c)"), in_=maskd),
             writes=[b_mask], dma=True, sem="cst")
        P.op("sp", lambda e: e.dma_start(out=gsub[:], in_=bass.AP(gsubd.tensor, 0, [[0, 128], [1, 128]])),
             writes=[b_gsub], dma=True, sem="cst")
        P.op("sp", lambda e: e.dma_start(out=hb[0][:, 0, 0:256],
                                         in_=bass.AP(lamd.tensor, 0, [[0, 128], [1, 256]])),
             writes=[b_lamt], dma=True, sem="cst")
        P.op("sp", lambda e: e.dma_start(out=bsf, in_=bass.AP(sgbd.tensor, 0, [[0, 1], [1, 1024]])),
             writes=[b_bsf], dma=True, sem="cst")
        P.op("sp", lambda e: e.dma_start(out=vg2.rearrange("p (g j) -> p g j", g=8),
                                         in_=sgwd.rearrange("g i j -> i g j")),
             writes=[b_vg2], dma=True, sem="cst")
        P.op("dve", lambda e: e.tensor_scalar(out=gsub[:], in0=gsub[:], scalar1=0.8, scalar2=None, op0=ALU.mult),
             reads=[b_gsub], writes=[b_gsub])
        P.op("dve", lambda e: e.tensor_tensor(out=lamp[:, 0, :], in0=lamt[:, 0, :], in1=lamt[:, 1, :], op=ALU.mult),
             reads=[b_lamt], writes=[b_lamp])
        P.op("dve", lambda e: e.tensor_tensor(out=lamp[:, 1, :], in0=lamt[:, 2, :], in1=lamt[:, 3, :], op=ALU.mult),
             reads=[b_lamt, b_lamp], writes=[b_lamp])
        P.op("dve", lambda e: e.tensor_reduce(out=lams[:, 0:2], in_=lamp, axis=AX.X, op=ALU.add),
             reads=[b_lamp], writes=[b_lams])
        P.op("act", lambda e: e.activation(out=lams[:, 2:4], in_=lams[:, 0:2], func=AF.Exp),
             reads=[b_lams], writes=[b_lams])
        P.op("dve", lambda e: e.tensor_tensor(out=lams[:, 0:1], in0=lams[:, 2:3], in1=lams[:, 3:4], op=ALU.subtract),
             reads=[b_lams], writes=[b_lams])
        P.op("dve", lambda e: e.tensor_scalar(out=lams[:, 3:4], in0=lams[:, 0:1], scalar1=0.2, scalar2=None, op0=ALU.add),
             reads=[b_lams], writes=[b_lams])
        LAM = lams[:, 3:4]
        P.op("dve", lambda e: e.memset(ones1[:], 1.0), writes=[b_ones1])
        P.op("dve", lambda e: e.memset(vxa[:, :, :, 256:257], 1.0), writes=[b_vxa])
        P.op("dve", lambda e: e.tensor_copy(bsh, bsf), reads=[b_bsf], writes=[b_bsh])
        P.op("dve", lambda e: e.tensor_copy(bsf2, bsh), reads=[b_bsh], writes=[b_bsf2])
        P.op("dve", lambda e: e.tensor_tensor(out=bsf2, in0=bsf, in1=bsf2, op=ALU.subtract),
             reads=[b_bsf, b_bsf2], writes=[b_bsf2])
        P.op("dve", lambda e: e.tensor_copy(bsl, bsf2), reads=[b_bsf2], writes=[b_bsl])
        P.op("sp", lambda e: [e.dma_start(out=bs2[0:1, :], in_=bsh), e.dma_start(out=bs2[1:2, :], in_=bsl)],
             reads=[b_bsh, b_bsl], writes=[b_bs2], dma=True, sem="cst", n_dma=2)
        vg2_3 = vg2.rearrange("p (g j) -> p g j", g=8)
        P.op("dve", lambda e: e.memset(vg2_3[0:64, :, 64:128], 0.0), reads=[b_vg2], writes=[b_vg2])
        P.op("dve", lambda e: e.tensor_copy(vn[:], vg2), reads=[b_vg2], writes=[b_vn])
        for g in range(8):
            P.op("pe", lambda e, g=g: e.transpose(psBb[:, 0, g * 128:(g + 1) * 128], vn[:, g * 128:(g + 1) * 128], ident),
                 reads=[b_vn, b_cb], writes=[b_psB[0]])
        P.op("dve", lambda e: e.tensor_copy(WsT[:].rearrange("p g i -> p (g i)"), psBb[:, 0, :]),
             reads=[b_psB[0]], writes=[b_WsT])

        P.op("sp", lambda e: e.dma_start(
            out=hb[0][:], in_=xf[0:512, :].rearrange("(b p) d -> p b d", p=128)),
             writes=[b_h[0]], dma=True, sem="x0")
        stg = [A[1].bitcast(F32)[:, 0:4096], B[1].bitcast(F32)[:, 0:4096]]
        b_stg = [b_A[1], b_B[1]]
        p1slots = {}
        for i_, cg_ in enumerate((4, 5, 2, 3)):
            sl = st["wp"] % NWP
            st["wp"] += 1
            p1slots[cg_] = sl
            sg_ = stg[i_ % 2]
            P.op("sp", lambda e, cg_=cg_, sg_=sg_: e.dma_start(
                out=sg_.rearrange("p (k c) -> p k c", c=512),
                in_=w_in[:, cg_ * 512:(cg_ + 1) * 512].rearrange("(kc p) c -> p kc c", p=128)),
                 writes=[b_stg[i_ % 2]], dma=True, sem=f"stg{i_ % 2}")
            if i_ % 2 == 0:
                P.op("dve", lambda e, sl=sl, sg_=sg_: e.tensor_copy(wp[sl][:].rearrange("p k c -> p (k c)"), sg_),
                     reads=[b_stg[i_ % 2]], writes=[b_wp[sl]])
            else:
                P.op("act", lambda e, sl=sl, sg_=sg_: e.activation(out=wp[sl][:].rearrange("p k c -> p (k c)"), in_=sg_,
                                                                  func=AF.Copy),
                     reads=[b_stg[i_ % 2]], writes=[b_wp[sl]])
        pk = [p1slots[2], p1slots[3]]
        pv = [p1slots[4], p1slots[5]]
        vi_p1 = load_vrep(0)
        b_tick = Buf("tick")
        P.op("dve", lambda e: e.memset(Vaug[:, :, :, 128:129], 1.0), writes=[b_B[0]])
        for sg in range(n_sgroups):
            hh = sg % 2
            h = hb[hh]
            if sg + 1 < n_sgroups:
                hn = hb[1 - hh]
                P.op("sp", lambda e, sg=sg, hn=hn: e.dma_start(
                    out=hn[:], in_=xf[(sg + 1) * 512:(sg + 2) * 512, :].rearrange("(b p) d -> p b d", p=128)),
                     writes=[b_h[1 - hh]], dma=True, sem=f"x{1 - hh}")
            P.mark("p1_xload")
            if sg == 0:
                norm_stats(h, b_h[hh], 0)
            norm_apply(h, b_h[hh], 0, sg % 2, vi_fixed=vi_p1, pre_done=(1 if sg > 0 else 0))
            P.mark("p1_norm")
            rope_tables(posf, sg * 512)
            P.mark("p1_rope")
            for blk in range(4):
                for cg in range(2):
                    mb = mmbank()
                    s = pv[cg]
                    for kc in range(8):
                        P.op("pe", lambda e, s=s, kc=kc, mb=mb, blk=blk: e.matmul(
                            psA[:, mb, :], nT[:, kc, blk * 128:(blk + 1) * 128], wp[s][:, kc, :],
                            start=(kc == 0), stop=(kc == 7)),
                             reads=[b_wp[s], b_nT], writes=[b_psA[mb]])
                    tick = Buf(f"tick{sg}")
                    P.op("act", lambda e, mb=mb, blk=blk, cg=cg: e.activation(
                        out=Vaug[:, cg * 4:(cg + 1) * 4, blk, 0:128],
                        in_=psA[:, mb, :].rearrange("p (h e) -> p h e", e=128), func=AF.Copy),
                         reads=[b_psA[mb]], writes=[b_B[0], tick])
            P.mark("p1_vproj")
            del cast_dep[:]
            cast_dep.append(tick)
            emit_casts(2 if (sg < 4 or n_sgroups < 16) else 1)
            P.op("act", lambda e, sg=sg: e.dma_start(
                out=Vs[:, :, sg * 4 * 129:(sg + 1) * 4 * 129].rearrange("h p e -> p h e"),
                in_=Vaug.rearrange("p h b e -> p h (b e)")),
                 reads=[b_B[0]], writes=[b_Vs[sg]], dma=True, sem="vst")
            P.mark("p1_vst")
            hook = None
            if sg + 1 < n_sgroups:
                hn_, bhn_, pn_ = hb[1 - hh], b_h[1 - hh], (sg + 1) % 2
                norm_sq(hn_, bhn_, 0, pn_)
                norm_sq(hn_, bhn_, 1, pn_)

                def hook(hn_=hn_, bhn_=bhn_, pn_=pn_):
                    norm_sq(hn_, bhn_, 2, pn_)
                    norm_sq(hn_, bhn_, 3, pn_)
                    norm_finish(pn_)
                    norm_pre(hn_, bhn_, vi_p1, pn_, 0)
            proj_rope(lambda ct: (pk[ct // 4], ct % 4), KTo, b_A[0], mid_hook=hook)
            P.mark("p1_proj")
            P.op("act", lambda e, sg=sg: e.dma_start(
                out=Kt[:, sg * 512:(sg + 1) * 512].rearrange("(c p) t -> p c t", p=128), in_=KTo),
                 reads=[b_A[0]], writes=[b_Kt[sg]], dma=True, sem="kst")
            P.mark("p1_kst")

        del cast_dep[:]
        emit_casts(len(cast_list))
        if n_groups > 0:
            P.op("sp", lambda e: e.dma_start(out=hb[0][:, 0:2, :], in_=memd.rearrange("(b p) d -> p b d", p=128)),
                 writes=[b_h[0]], dma=True, sem="x0")
            rmsnorm_to_nT(hb[0], b_h[0], 4, nblk=2)
            for cg in range(2):
                s = load_panel(wb_xkv, "xkv0", 0, cg)
                for ctl in range(4):
                    ct = cg * 4 + ctl
                    mb = mmbank()
                    for kc in range(8):
                        P.op("pe", lambda e, s=s, ctl=ctl, kc=kc, mb=mb: e.matmul(
                            psA[:, mb, 0:256], wp[s][:, kc, ctl * 128:(ctl + 1) * 128], nT[:, kc, 0:256],
                            start=(kc == 0), stop=(kc == 7)),
                             reads=[b_wp[s], b_nT], writes=[b_psA[mb]])
                    P.op("act", lambda e, mb=mb, ct=ct: e.activation(out=kxT[:, ct, :], in_=psA[:, mb, 0:256], func=AF.Copy),
                         reads=[b_psA[mb]], writes=[b_kxT])
            for cg in range(2):
                s = load_panel(wb_xkv, "xkv1", 0, 2 + cg)
                for mbk in range(2):
                    mb = mmbank()
                    for kc in range(8):
                        P.op("pe", lambda e, s=s, kc=kc, mb=mb, mbk=mbk: e.matmul(
                            psA[:, mb, :], nT[:, kc, mbk * 128:(mbk + 1) * 128], wp[s][:, kc, :],
                            start=(kc == 0), stop=(kc == 7)),
                             reads=[b_wp[s], b_nT], writes=[b_psA[mb]])
                    P.op("act", lambda e, mb=mb, mbk=mbk, cg=cg: e.activation(
                        out=vxa[:, mbk, 2 * cg:2 * cg + 2, 0:256],
                        in_=psA[:, mb, :].rearrange("p (h e) -> p h e", e=256), func=AF.Copy),
                         reads=[b_psA[mb]], writes=[b_vxa])

        hstate = {"hcount": 0, "deferred": None, "kv0_loaded": False, "epilogue": None}

        def group_body(og):
            hh = og % 2
            h = hb[hh]
            bh = b_h[hh]
            if og == 0:
                P.op("sp", lambda e: e.dma_start(
                    out=h[:], in_=xo[0:512, :].rearrange("(b p) d -> p b d", p=128)),
                     writes=[bh], dma=True, sem=f"x{hh}")
            if og == 0:
                norm_stats(h, bh, 1)
                rope_tables(poso, 0)
            norm_apply(h, bh, 0, 1)
            pq = [load_panel(wb_in, "in0", 0, 0), load_panel(wb_in, "in0", 0, 1)]
            proj_rope(lambda ct: (pq[ct // 4], ct % 4), QT, b_QT)
            if hstate["epilogue"] is not None:
                hstate["epilogue"]()
                hstate["epilogue"] = None

            nk = 8 * (og + 1)

            def load_kv(hd, nk_):
                ab = hd % 2
                need = (nk_ * 128 + 511) // 512
                P.op("sp", lambda e: e.dma_start(
                    out=A[ab][:, 0:nk_ * 128], in_=Kt[hd * 128:(hd + 1) * 128, 0:nk_ * 128]),
                     reads=b_Kt[:need], writes=list(b_Al[ab]), dma=True, sem=f"A{ab}")
                P.op("sp", lambda e: e.dma_start(
                    out=B[ab][:, 0:nk_ * 129], in_=Vs[hd, :, 0:nk_ * 129]),
                     reads=b_Vs[:need], writes=[b_B[ab]], dma=True, sem=f"B{ab}")

            fns = []
            if not hstate["kv0_loaded"]:
                load_kv(0, nk)
            hstate["kv0_loaded"] = False
            for hd in range(8):
                ab = hd % 2
                hp_ = ab

                def qk_step(kb, hd=hd, ab=ab):
                    imin = max(4 * og, (kb) // 2)
                    q0 = imin - 4 * og
                    c0 = q0 * 128
                    sbk = kb % 2
                    for m in range(2):
                        P.op("pe", lambda e, m=m, sbk=sbk, c0=c0: e.matmul(
                            psA[:, 2 * sbk + m, c0:512], A[ab][m * 64:(m + 1) * 64, kb * 128:(kb + 1) * 128],
                            QT[m * 64:(m + 1) * 64, hd, c0:512], start=True, stop=True),
                             reads=b_Al[ab] + [b_QT], writes=[b_psA[2 * sbk + m]])
                    pb = st["pt"] % NPT
                    st["pt"] += 1
                    P.op("act", lambda e, sbk=sbk, pb=pb, c0=c0: e.activation(
                        out=PT[pb][:, :, c0:512], in_=psA[:, 2 * sbk:2 * sbk + 2, c0:512], func=AF.Exp, scale=0.125),
                         reads=[b_psA[2 * sbk], b_psA[2 * sbk + 1]], writes=[b_PT[pb]])
                    if kb >= 8 * og:
                        i = kb // 2
                        qi = i - 4 * og
                        mk0 = mask[:, i % 2, kb % 2, :]
                        mkb = bass.AP(mk0.tensor, mk0.offset, [list(mk0.ap[0]), [0, 2], [1, 128]])
                        P.op("pool", lambda e, pb=pb, qi=qi, mkb=mkb: e.tensor_tensor(
                            out=PT[pb][:, :, qi * 128:(qi + 1) * 128], in0=PT[pb][:, :, qi * 128:(qi + 1) * 128],
                            in1=mkb, op=ALU.mult),
                             reads=[b_PT[pb], b_mask], writes=[b_PT[pb]])
                    return pb, q0

                hp = hp_

                def pv_step(kb, pb, q0, hd=hd, ab=ab, hp=hp):
                    for qi in range(q0, 4):
                        i = 4 * og + qi
                        last = 2 * i + 1
                        for m in range(2):
                            P.op("pe", lambda e, qi=qi, m=m, last=last: e.matmul(
                                psB[:, qi, m * 129:(m + 1) * 129], PT[pb][:, m, qi * 128:(qi + 1) * 128],
                                B[ab][:, kb * 129:(kb + 1) * 129], start=(kb == 0 and m == 0),
                                stop=(kb == last and m == 1), skip_group_check=True),
                                 reads=[b_PT[pb], b_B[ab]], writes=[b_psB[qi]])
                        if kb == last:
                            oq = o32s[hp][qi]
                            boq = b_o32s[hp][qi]
                            oraw = oraws[qi % 2]
                            b_oraw = b_oraws[qi % 2]
                            P.op("dve", lambda e, qi=qi, oraw=oraw: e.tensor_copy(oraw[:], psB[:, qi, 0:258]),
                                 reads=[b_psB[qi]], writes=[b_oraw])
                            Ov = oraw[:].rearrange("p (m e) -> p m e", e=129)
                            P.op("dve", lambda e, Ov=Ov: e.reciprocal(RDEN.rearrange("p (m o) -> p m o", o=1), Ov[:, :, 128:129]),
                                 reads=[b_oraw], writes=[b_rden])
                            P.op("dve", lambda e: e.tensor_tensor(out=LR, in0=RDEN[:, 1:2], in1=LAM, op=ALU.mult),
                                 reads=[b_rden, b_lams], writes=[b_lr])
                            P.op("dve", lambda e, oraw=oraw: e.tensor_scalar(out=t32[:], in0=oraw[:, 129:257], scalar1=LR,
                                                                  scalar2=None, op0=ALU.mult),
                                 reads=[b_oraw, b_lr], writes=[b_t32])
                            P.op("dve", lambda e, oq=oq, oraw=oraw: e.scalar_tensor_tensor(
                                out=oq[:], in0=oraw[:, 0:128], scalar=RDEN[:, 0:1], in1=t32[:],
                                op0=ALU.mult, op1=ALU.subtract),
                                 reads=[b_oraw, b_rden, b_t32], writes=[boq])
                            P.op("dve", lambda e, oq=oq: e.tensor_tensor(out=t32[:], in0=oq[:], in1=oq[:], op=ALU.mult),
                                 reads=[boq], writes=[b_t32])
                            P.op("dve", lambda e, qi=qi, hp=hp: e.tensor_reduce(out=SS4[hp][:, qi:qi + 1], in_=t32[:], axis=AX.X, op=ALU.add),
                                 reads=[b_t32], writes=[b_ss4[hp]])

                def finalize2(hd=hd, hp=hp):
                    rsqrt_fused(RS4[hp], SS4[hp], 1.0 / 128, EPS6, [b_ss4[hp]], [b_rs4[hp]])
                    for qi in range(4):
                        P.op("dve", lambda e, qi=qi: e.scalar_tensor_tensor(
                            out=atok[:, qi, hd * 128:(hd + 1) * 128], in0=o32s[hp][qi][:], scalar=RS4[hp][:, qi:qi + 1],
                            in1=gsub[:], op0=ALU.mult, op1=ALU.mult),
                             reads=[b_o32s[hp][qi], b_rs4[hp], b_gsub], writes=[b_atok])

                fns.append((qk_step, pv_step, finalize2))

            seq = [(hd_, kb_) for hd_ in range(8) for kb_ in range(nk)]
            pend = {}
            LAG = 2
            deferred = []
            for idx in range(len(seq) + LAG):
                if idx < len(seq):
                    hd_, kb_ = seq[idx]
                    if idx == 0:
                        load_kv(1, nk)
                    pend[idx] = fns[hd_][0](kb_)
                j = idx - LAG
                if j >= 0:
                    hd_, kb_ = seq[j]
                    fns[hd_][1](kb_, *pend.pop(j))
                    if kb_ == nk - 1:
                        deferred.append((idx + 2, fns[hd_][2]))
                        if hd_ + 2 < 8:
                            load_kv(hd_ + 2, nk)
                while deferred and deferred[0][0] <= idx:
                    deferred.pop(0)[1]()
            while deferred:
                deferred.pop(0)[1]()

            if dbg and og == 0:
                dbg_out["atok"] = nc.dram_tensor("dbg_atok", [128, 4 * 1024], BF, kind="ExternalOutput").ap()
                P.op("pool", lambda e: e.dma_start(out=dbg_out["atok"], in_=atok[:].rearrange("p a b -> p (a b)")),
                     reads=[b_atok], writes=[b_y], dma=True, sem="yout")
                dbg_out["QT"] = nc.dram_tensor("dbg_QT", [128, 8 * 512], BF, kind="ExternalOutput").ap()
                P.op("pool", lambda e: e.dma_start(out=dbg_out["QT"], in_=QT[:].rearrange("p a b -> p (a b)")),
                     reads=[b_QT], writes=[b_y], dma=True, sem="yout")

            for cg in range(4):
                s = load_panel(wb_in, "in5" if cg < 2 else "in6", 0, 10 + cg)
                for ctl in range(4):
                    ct = cg * 4 + ctl
                    mb = mmbank()
                    for kc in range(8):
                        P.op("pe", lambda e, s=s, ctl=ctl, kc=kc, mb=mb: e.matmul(
                            psA[:, mb, :], wp[s][:, kc, ctl * 128:(ctl + 1) * 128], nT[:, kc, :],
                            start=(kc == 0), stop=(kc == 7)),
                             reads=[b_wp[s], b_nT], writes=[b_psA[mb]])
                    P.op("act", lambda e, mb=mb, ct=ct: e.activation(out=gatesT[:, ct, :], in_=psA[:, mb, :],
                                                                     func=AF.Sigmoid),
                         reads=[b_psA[mb]], writes=[b_B[0]])
            transpose_tok_to_feat(atok, b_atok, aT, b_A[0])

            for cg in range(2):
                s = load_panel(wb_in, "in3", 0, 6 + cg)
                for ctl in range(4):
                    ct = cg * 4 + ctl
                    mb = mmbank()
                    for kc in range(8):
                        P.op("pe", lambda e, s=s, ctl=ctl, kc=kc, mb=mb: e.matmul(
                            psA[:, mb, :], wp[s][:, kc, ctl * 128:(ctl + 1) * 128], nT[:, kc, :],
                            start=(kc == 0), stop=(kc == 7)),
                             reads=[b_wp[s], b_nT], writes=[b_psA[mb]])
                    P.op("act", lambda e, mb=mb, ct=ct: e.activation(out=uT[:, ct, :], in_=psA[:, mb, :],
                                                                     func=AF.Gelu_apprx_tanh),
                         reads=[b_psA[mb]], writes=[b_B[1]])
            pvv = [load_panel(wb_in, "in4", 0, 8), load_panel(wb_in, "in4", 0, 9)]
            vi_g = load_vrep(5)
            vi_b = load_vrep(6)
            SG3 = psB[:, 2:4, :].rearrange("p a (g i) -> p (a g) i", i=128)
            vgb = [vg[:], hb[1 - hh][:, 0, :], QT.bitcast(F32)[:, 0:4, :].rearrange("p a b -> p (a b)")]
            b_vgb = [b_vg, b_h[1 - hh], b_QT]

            def vproj(blk):
                vgt = vgb[blk % 3]
                bvg = b_vgb[blk % 3]
                for cg in range(2):
                    mb = mmbank()
                    s = pvv[cg]
                    for kc in range(8):
                        P.op("pe", lambda e, s=s, kc=kc, mb=mb: e.matmul(
                            psA[:, mb, :], nT[:, kc, blk * 128:(blk + 1) * 128], wp[s][:, kc, :],
                            start=(kc == 0), stop=(kc == 7)),
                             reads=[b_wp[s], b_nT], writes=[b_psA[mb]])
                    P.op("act", lambda e, mb=mb, cg=cg: e.activation(out=vgt[:, cg * 512:(cg + 1) * 512], in_=psA[:, mb, :],
                                                                     func=AF.Gelu_apprx_tanh),
                         reads=[b_psA[mb]], writes=[bvg])

            def ln_spatial(blk):
                vgt = vgb[blk % 3]
                bvg = b_vgb[blk % 3]
                P.op("dve", lambda e: e.bn_stats(ST6[:, 0:6], vgt[:, 0:512]), reads=[bvg], writes=[b_st6])
                P.op("dve", lambda e: e.bn_stats(ST6[:, 6:12], vgt[:, 512:1024]), reads=[bvg, b_st6], writes=[b_st6])
                P.op("dve", lambda e: e.bn_aggr(MV, ST6), reads=[b_st6], writes=[b_mv])
                rsqrt_fused(LNR, MV[:, 1:2], 1.0, EPS5, [b_mv], [b_lnr])
                P.op("dve", lambda e: e.scalar_tensor_tensor(out=vgt, in0=vgt, scalar=MV[:, 0:1], in1=vrep[vi_g][:],
                                                             op0=ALU.subtract, op1=ALU.mult),
                     reads=[bvg, b_mv, b_vrep[vi_g]], writes=[bvg])
                P.op("dve", lambda e: e.scalar_tensor_tensor(out=vn[:], in0=vgt, scalar=LNR, in1=vrep[vi_b][:],
                                                             op0=ALU.mult, op1=ALU.add),
                     reads=[bvg, b_lnr, b_vrep[vi_b]], writes=[b_vn])
                for g in range(8):
                    bk = 2 + g // 4
                    P.op("pe", lambda e, g=g: e.matmul(SG3[:, g, :], vn[:, g * 128:(g + 1) * 128], WsT[:, g, :],
                                                       start=(g % 4 == 0), stop=False, skip_group_check=True),
                         reads=[b_vn, b_WsT], writes=[b_psB[bk]])
                    P.op("pe", lambda e, g=g: e.matmul(SG3[:, g, :], ones1[:], bs2[:, g * 128:(g + 1) * 128],
                                                       start=False, stop=True, skip_group_check=True),
                         reads=[b_ones1, b_bs2], writes=[b_psB[bk]])
                P.op("dve", lambda e: e.tensor_tensor(out=sgoT[:, :, blk * 128:(blk + 1) * 128], in0=SG3,
                                                      in1=uT[:, :, blk * 128:(blk + 1) * 128], op=ALU.mult),
                     reads=[b_psB[2], b_psB[3], b_B[1]], writes=[b_B[1]])

            vproj(0)
            vproj(1)
            for blk in range(4):
                if blk + 2 < 4:
                    vproj(blk + 2)
                ln_spatial(blk)
            for cg in range(2):
                sa = load_panel(wb_ba, "ba0", 0, cg)
                ss_ = load_panel(wb_bs, "bs0", 0, cg)
                for ctl in range(4):
                    ct = cg * 4 + ctl
                    m1 = mmbank()
                    for kc in range(8):
                        P.op("pe", lambda e, sa=sa, ctl=ctl, kc=kc, m1=m1: e.matmul(
                            psA[:, m1, :], wp[sa][:, kc, ctl * 128:(ctl + 1) * 128], aT[:, kc, :],
                            start=(kc == 0), stop=(kc == 7)),
                             reads=[b_wp[sa], b_A[0]], writes=[b_psA[m1]])
                    m2 = mmbank()
                    for kc in range(8):
                        P.op("pe", lambda e, ss_=ss_, ctl=ctl, kc=kc, m2=m2: e.matmul(
                            psA[:, m2, :], wp[ss_][:, kc, ctl * 128:(ctl + 1) * 128], sgoT[:, kc, :],
                            start=(kc == 0), stop=(kc == 7)),
                             reads=[b_wp[ss_], b_B[1]], writes=[b_psA[m2]])
                    P.op("dve", lambda e, m1=m1, ct=ct: e.tensor_tensor(out=T[1][:], in0=psA[:, m1, :], in1=gatesT[:, ct, :],
                                                                        op=ALU.mult),
                         reads=[b_psA[m1], b_B[0]], writes=[b_T[1]])
                    P.op("dve", lambda e, m2=m2, ct=ct: e.tensor_tensor(out=T[2][:], in0=psA[:, m2, :], in1=gatesT[:, 8 + ct, :],
                                                                        op=ALU.mult),
                         reads=[b_psA[m2], b_B[0]], writes=[b_T[2]])
                    P.op("dve", lambda e, ct=ct: e.tensor_tensor(out=mergedT[:, ct, :], in0=T[1][:], in1=T[2][:], op=ALU.add),
                         reads=[b_T[1], b_T[2]], writes=[b_mT])

            def out_proj(wb, castname, srcT, bsrc, stats_par=None, next_vrow=None):
                ss2 = [load_panel(wb, castname, 0, cg) for cg in range(2)]
                vi_n = load_vrep(next_vrow) if next_vrow is not None else None
                junk = PT[2][:].rearrange("p a b -> p (a b)")
                for blk in range(4):
                    for cg in range(2):
                        s = ss2[cg]
                        mb = mmbank()
                        for kc in range(8):
                            P.op("pe", lambda e, s=s, kc=kc, mb=mb, blk=blk: e.matmul(
                                psA[:, mb, :], srcT[:, kc, blk * 128:(blk + 1) * 128], wp[s][:, kc, :],
                                start=(kc == 0), stop=(kc == 7)),
                                 reads=[b_wp[s], bsrc], writes=[b_psA[mb]])
                        P.op("dve", lambda e, mb=mb, blk=blk, cg=cg: e.tensor_tensor(
                            out=h[:, blk, cg * 512:(cg + 1) * 512], in0=psA[:, mb, :],
                            in1=h[:, blk, cg * 512:(cg + 1) * 512], op=ALU.add),
                             reads=[b_psA[mb], bh], writes=[bh])
                    if stats_par is not None:
                        norm_sq(h, bh, blk, stats_par, junk, b_PT[2])
                        norm_finish_blk(stats_par, blk)
                        if vi_n is not None and blk in (1, 2):
                            norm_pre(h, bh, vi_n, stats_par, blk - 1)
                return vi_n

            vi_x = out_proj(wb_out, "out0", mergedT, b_mT, stats_par=0, next_vrow=1)

            if dbg and og == 0:
                dbg_out["h1"] = nc.dram_tensor("dbg_h1", [128, 4 * 1024], F32, kind="ExternalOutput").ap()
                P.op("pool", lambda e: e.dma_start(out=dbg_out["h1"], in_=h[:].rearrange("p a b -> p (a b)")),
                     reads=[bh], writes=[b_y], dma=True, sem="yout")

            norm_apply(h, bh, 1, 0, vi_fixed=vi_x, pre_done=2)
            for cg in range(2):
                s = load_panel(wb_xq, "xq0", 0, cg)
                for ctl in range(4):
                    ct = cg * 4 + ctl
                    mb = mmbank()
                    for kc in range(8):
                        P.op("pe", lambda e, s=s, ctl=ctl, kc=kc, mb=mb: e.matmul(
                            psA[:, mb, :], wp[s][:, kc, ctl * 128:(ctl + 1) * 128], nT[:, kc, :],
                            start=(kc == 0), stop=(kc == 7)),
                             reads=[b_wp[s], b_nT], writes=[b_psA[mb]])
                    P.op("act", lambda e, mb=mb, ct=ct: e.activation(out=QT[:, ct, :], in_=psA[:, mb, :], func=AF.Copy),
                         reads=[b_psA[mb]], writes=[b_QT])
            for hx in range(4):
                sbk = hx % 2
                for mbk in range(2):
                    for c in range(2):
                        P.op("pe", lambda e, sbk=sbk, mbk=mbk, c=c, hx=hx: e.matmul(
                            psA[:, 2 * sbk + mbk, :], kxT[:, hx * 2 + c, mbk * 128:(mbk + 1) * 128], QT[:, hx * 2 + c, :],
                            start=(c == 0), stop=(c == 1)),
                             reads=[b_kxT, b_QT], writes=[b_psA[2 * sbk + mbk]])
                pb = st["pt"] % NPT
                st["pt"] += 1
                P.op("act", lambda e, sbk=sbk, pb=pb: e.activation(
                    out=PT[pb][:], in_=psA[:, 2 * sbk:2 * sbk + 2, :], func=AF.Exp, scale=1.0 / 16.0),
                     reads=[b_psA[2 * sbk], b_psA[2 * sbk + 1]], writes=[b_PT[pb]])
                for blk in range(4):
                    for mbk in range(2):
                        P.op("pe", lambda e, blk=blk, mbk=mbk, pb=pb, hx=hx: e.matmul(
                            psB[:, blk, 0:257], PT[pb][:, mbk, blk * 128:(blk + 1) * 128], vxa[:, mbk, hx, :],
                            start=(mbk == 0), stop=(mbk == 1)),
                             reads=[b_PT[pb], b_vxa], writes=[b_psB[blk]])
                    P.op("dve", lambda e, blk=blk: e.reciprocal(RDX, psB[:, blk, 256:257]),
                         reads=[b_psB[blk]], writes=[b_rdx])
                    P.op("dve", lambda e, blk=blk, hx=hx: e.tensor_scalar(
                        out=atok[:, blk, hx * 256:(hx + 1) * 256], in0=psB[:, blk, 0:256], scalar1=RDX, scalar2=None,
                        op0=ALU.mult),
                         reads=[b_psB[blk], b_rdx], writes=[b_atok])
            transpose_tok_to_feat(atok, b_atok, aT, b_A[0])
            vi_f = out_proj(wb_xo, "xo0", aT, b_A[0], stats_par=0, next_vrow=2)

            if og + 1 < n_groups:
                hn = hb[1 - hh]
                P.op("sp", lambda e: e.dma_start(
                    out=hn[:], in_=xo[(og + 1) * 512:(og + 2) * 512, :].rearrange("(b p) d -> p b d", p=128)),
                     writes=[b_h[1 - hh]], dma=True, sem=f"x{1 - hh}")
                load_kv(0, 8 * (og + 2))
                hstate["kv0_loaded"] = True
            norm_apply(h, bh, 2, 0, vi_fixed=vi_f, pre_done=2)
            for half in range(2):
                for cgl in range(4):
                    cg = half * 4 + cgl
                    s = load_panel(wb_ff1, f"ff1{cg // 2}", 0, cg)
                    for ctl in range(4):
                        ctloc = cgl * 4 + ctl
                        mb = mmbank()
                        for kc in range(8):
                            P.op("pe", lambda e, s=s, ctl=ctl, kc=kc, mb=mb: e.matmul(
                                psA[:, mb, :], wp[s][:, kc, ctl * 128:(ctl + 1) * 128], nT[:, kc, :],
                                start=(kc == 0), stop=(kc == 7)),
                                 reads=[b_wp[s], b_nT], writes=[b_psA[mb]])
                        P.op("act", lambda e, mb=mb: e.activation(out=kraw[:], in_=psA[:, mb, :], func=AF.Square),
                             reads=[b_psA[mb]], writes=[b_kraw])
                        P.op("dve", lambda e, mb=mb, ctloc=ctloc: e.scalar_tensor_tensor(
                            out=hidT[:, ctloc, :], in0=psA[:, mb, :], scalar=0.0, in1=kraw[:],
                            op0=ALU.is_gt, op1=ALU.mult),
                             reads=[b_psA[mb], b_kraw], writes=[b_A[1]])
                if half == 1 and og + 1 < n_groups:
                    rope_tables(poso, (og + 1) * 512)
                    norm_stats(hb[1 - hh], b_h[1 - hh], 1)
                sp4 = [[load_panel(wb_ff2, f"ff2{half * 2 + kgl}", half * 2 + kgl, cg) for kgl in range(2)]
                       for cg in range(2)]
                for blk in range(4):
                    for cg in range(2):
                        sp_ = sp4[cg]
                        mb = mmbank()
                        for kgl in range(2):
                            for kc in range(8):
                                P.op("pe", lambda e, kgl=kgl, kc=kc, mb=mb, blk=blk, sp_=sp_: e.matmul(
                                    psA[:, mb, :], hidT[:, kgl * 8 + kc, blk * 128:(blk + 1) * 128], wp[sp_[kgl]][:, kc, :],
                                    start=(kgl == 0 and kc == 0), stop=(kgl == 1 and kc == 7)),
                                     reads=[b_wp[sp_[kgl]], b_A[1]], writes=[b_psA[mb]])
                        P.op("dve", lambda e, mb=mb, blk=blk, cg=cg: e.tensor_tensor(
                            out=h[:, blk, cg * 512:(cg + 1) * 512], in0=psA[:, mb, :],
                            in1=h[:, blk, cg * 512:(cg + 1) * 512], op=ALU.add),
                             reads=[b_psA[mb], bh], writes=[bh])
                    if half == 1:
                        norm_sq(h, bh, blk, 0, PT[2][:].rearrange("p a b -> p (a b)"), b_PT[2])
                        norm_finish_blk(0, blk)

            def epilogue():
                vi = load_vrep(3)
                for blk in range(4):
                    P.op("dve", lambda e, blk=blk: e.scalar_tensor_tensor(out=vg[:], in0=h[:, blk, :],
                                                                          scalar=RSp[0][:, blk:blk + 1], in1=vrep[vi][:],
                                                                          op0=ALU.mult, op1=ALU.mult),
                         reads=[bh, b_rsb[0][blk], b_vrep[vi]], writes=[b_vg])
                    r0 = (og * 4 + blk) * 128
                    P.op("pool", lambda e, r0=r0: e.dma_start(out=y[r0:r0 + 128, :], in_=vg[:]),
                         reads=[b_vg], writes=[b_y], dma=True, sem="yout")

            hstate["epilogue"] = epilogue

        for og_ in range(n_groups):
            group_body(og_)
        if hstate["epilogue"] is not None:
            hstate["epilogue"]()

        if dbg:
            dbg_out["Kt"] = Kt
        P.out_keys = [("dma", "yout")] if n_groups > 0 else [("dma", "kst"), ("dma", "vst")] + [
            ("dma", "c_" + nm) for nm in cast_bufs]
        if trunc is not None:
            P.ops = P.ops[:P.marks[trunc]]
            P.out_keys = None
        P.analyze()
        if P.out_keys is None:
            P.out_keys = [k for k in P.final if isinstance(k, tuple)]
        P.emit(nc)
    return nc, P


def own_blocks(hf):
    out = []
    for i in range(NOWN):
        if i % 2 == 0:
            out.append(2 * i + (0 if hf == 0 else 1))
        else:
            out.append(2 * i + (1 if hf == 0 else 0))
    return out


def make_consts():
    cbm = np.zeros((128, 256), np.float32)
    cbm[:, :128] = np.eye(128, dtype=np.float32)
    Pm = np.zeros((128, 128), np.float32)
    for base in (0, 64):
        for d in range(8):
            m = base + d
            Pm[base + d + 8, m] = -1.0
            m2 = base + 8 + d
            Pm[base + d, m2] = 1.0
    cbm[:, 128:] = Pm
    cfm = np.zeros((128, 4), np.float32)
    idx = np.arange(0, 16, 2, dtype=np.float32)
    inv_freq = np.power(np.float32(500000.0), -idx / np.float32(16)).astype(np.float32)
    for r in range(128):
        d = r % 64
        if d < 16:
            cfm[r, 0] = inv_freq[d % 8]
    cfm[:, 1] = np.float32(math.pi / 2)
    cfm[:, 2] = EPS
    cfm[:, 3] = 1e-5
    return cbm.astype(ml_dtypes.bfloat16), cfm


def make_mask(hf):
    Dm = np.ones((128, 128), np.float32)
    Dm[64:, :64] = 0.0
    ones = np.ones((128, 128), np.float32)
    zeros = np.zeros((128, 128), np.float32)
    typeA = (Dm, zeros)
    typeB = (ones, Dm)
    m = np.zeros((128, 2, 2, 128), np.float32)
    for par in range(2):
        j_is_even = (par == 0 and hf == 0) or (par == 1 and hf == 1)
        t = typeA if j_is_even else typeB
        m[:, par, 0, :] = t[0]
        m[:, par, 1, :] = t[1]
    return m.reshape(128, 512).astype(ml_dtypes.bfloat16)


_CACHE = {}


def kernel(x, mem, positions, g_mix, w_in, lam_q1, lam_k1, lam_q2, lam_k2, g_subln,
           sg_ln_g, sg_ln_b, sg_w, sg_b, w_branch_attn, w_branch_sg, w_out,
           g_xa, g_mem, w_xq, w_xkv, w_xo, g_ffn, w_ff1, w_ff2, g_final):
    f = lambda a: np.ascontiguousarray(np.asarray(a, dtype=np.float32))
    x = f(x); mem = f(mem)
    positions = np.ascontiguousarray(np.asarray(positions, dtype=np.int32))
    if "nc" not in _CACHE:
        _CACHE["nc"] = build_program()[0]
    nc = _CACHE["nc"]
    cbm, cfm = make_consts()
    vecs = np.stack([f(g_mix)[0], f(g_xa)[0], f(g_ffn)[0], f(g_final), f(g_mem)[0], f(sg_ln_g)[0], f(sg_ln_b)[0]], 0)
    lamv = np.stack([f(lam_q1)[0], f(lam_k1)[0], f(lam_q2)[0], f(lam_k2)[0]], 0)
    shared = {
        "w_in": f(w_in)[0], "w_ba": f(w_branch_attn)[0], "w_bs": f(w_branch_sg)[0], "w_out": f(w_out)[0],
        "w_xq": f(w_xq)[0], "w_xo": f(w_xo)[0], "w_xkv": f(w_xkv)[0], "w_ff1": f(w_ff1)[0], "w_ff2": f(w_ff2)[0],
        "vecs": np.ascontiguousarray(vecs), "g_subln": f(g_subln)[0], "lamv": np.ascontiguousarray(lamv),
        "sg_w": f(sg_w)[0], "sg_b": f(sg_b)[0].reshape(-1), "cb": cbm, "cf": cfm,
    }
    in_maps = []
    owns = []
    for c in range(8):
        b, hf = c // 2, c % 2
        ob = own_blocks(hf)
        owns.append(ob)
        rows = np.concatenate([np.arange(j * 128, (j + 1) * 128) for j in ob])
        m = dict(shared)
        m["xf"] = x[b]
        m["xo"] = np.ascontiguousarray(x[b][rows])
        m["posf"] = positions[b]
        m["poso"] = np.ascontiguousarray(positions[b][rows])
        m["memb"] = mem[b]
        m["maskd"] = make_mask(hf)
        in_maps.append(m)
    res = run_bass_kernel_spmd(nc, in_maps, core_ids=list(range(8)))
    out = np.empty((4, S, D), np.float32)
    for c in range(8):
        b = c // 2
        yc = np.asarray(res.results[c]["y"], dtype=np.float32)
        for i, j in enumerate(owns[c]):
            out[b, j * 128:(j + 1) * 128, :] = yc[i * 128:(i + 1) * 128, :]
    return out
```
